# Optimizing a Trainium2 kernel written in Bass

```python
import math, functools
import jax, jax.numpy as jnp
from jax import lax
import numpy as np

D_MODEL = 2048
BATCH = 1
SEQ = 8192
DEPTH = 1
DEC_BATCH = 128
DEC_SEQ = 8
PAST_LEN = 8192
PAGE_SIZE = 128

HEAD_DIM = 64
N_Q_HEADS = 16
N_KV_HEADS = 4
GROUP = N_Q_HEADS // N_KV_HEADS
ATTN_WIDTH = N_Q_HEADS * HEAD_DIM
KV_WIDTH = N_KV_HEADS * HEAD_DIM
WINDOW = 128
BLOCK = 128
CONV_CH = D_MODEL // 2
CONV_WIDTH = 31
D_FF = 256 * (-(-8 * D_MODEL // (3 * 256)))
ROPE_THETA = 10000.0
LN_EPS = 1e-5
ALPHA = (2 * DEPTH) ** 0.25
BETA = (8 * DEPTH) ** -0.25
ATTN_SCALE = HEAD_DIM ** -0.5
NEG = -1e30
N_IN = ATTN_WIDTH + 2 * KV_WIDTH + 2 * CONV_CH + 2 * D_MODEL
SPLITS = tuple(np.cumsum([ATTN_WIDTH, KV_WIDTH, KV_WIDTH, CONV_CH, CONV_CH, D_MODEL]).tolist())

kernel_name = "hybrid_swa_sink_conformer_conv_deepnorm_adaln_step"


def layer_norm(x, gain=None, bias=None):
    xf = x.astype(jnp.float32)
    mu = jnp.mean(xf, axis=-1, keepdims=True)
    var = jnp.mean(jnp.square(xf - mu), axis=-1, keepdims=True)
    y = (xf - mu) * lax.rsqrt(var + LN_EPS)
    if gain is not None:
        y = y * gain.astype(jnp.float32) + bias.astype(jnp.float32)
    return y.astype(x.dtype)


def rope(x, pos):
    half = HEAD_DIM // 2
    inv = ROPE_THETA ** (-jnp.arange(half, dtype=jnp.float32) / half)
    ang = pos.astype(jnp.float32)[:, None] * inv[None, :]
    cos = jnp.cos(ang)[:, None, :]
    sin = jnp.sin(ang)[:, None, :]
    xf = x.astype(jnp.float32)
    x1, x2 = xf[..., :half], xf[..., half:]
    return jnp.concatenate([x1 * cos - x2 * sin, x2 * cos + x1 * sin], axis=-1).astype(x.dtype)


def sink_softmax(s, sinks):
    sk = sinks.astype(jnp.float32).reshape(N_KV_HEADS, GROUP)[:, :, None, None]
    sk = jnp.broadcast_to(sk, s.shape[:-1] + (1,))
    m = jnp.maximum(jnp.max(s, axis=-1, keepdims=True), sk)
    p = jnp.exp(s - m)
    return p / (jnp.sum(p, axis=-1, keepdims=True) + jnp.exp(sk - m))


def attn_prompt(q, k, v, sinks):
    B, S = q.shape[0], q.shape[1]
    nb = S // BLOCK
    qb = q.reshape(B, nb, BLOCK, N_KV_HEADS, GROUP, HEAD_DIM)

    def with_prev(t):
        tb = t.reshape(B, nb, BLOCK, N_KV_HEADS, HEAD_DIM)
        prev = jnp.concatenate([jnp.zeros_like(tb[:, :1]), tb[:, :-1]], axis=1)
        return jnp.concatenate([prev, tb], axis=2)

    kk, vv = with_prev(k), with_prev(v)
    s = jnp.einsum('bnqhgd,bnkhd->bnhgqk', qb, kk, preferred_element_type=jnp.float32) * ATTN_SCALE
    qi = jnp.arange(BLOCK)[:, None] + BLOCK
    kj = jnp.arange(2 * BLOCK)[None, :]
    diff = qi - kj
    blk = jnp.arange(nb)[:, None, None]
    valid = (diff >= 0) & (diff < WINDOW) & (blk * BLOCK + kj - BLOCK >= 0)
    p = sink_softmax(jnp.where(valid[None, :, None, None], s, NEG), sinks)
    o = jnp.einsum('bnhgqk,bnkhd->bnqhgd', p.astype(vv.dtype), vv)
    n = min(WINDOW, S)
    return o.reshape(B, S, ATTN_WIDTH), k[:, S - n:], v[:, S - n:]


def attn_sample(q, k, v, sinks, k_buf, v_buf):
    Bd, T = q.shape[0], q.shape[1]
    nbuf = k_buf.shape[1]
    kk = jnp.concatenate([k_buf, k], axis=1)
    vv = jnp.concatenate([v_buf, v], axis=1)
    qg = q.reshape(Bd, T, N_KV_HEADS, GROUP, HEAD_DIM)
    s = jnp.einsum('bqhgd,bkhd->bhgqk', qg, kk, preferred_element_type=jnp.float32) * ATTN_SCALE
    qpos = PAST_LEN + jnp.arange(T)
    kpos = PAST_LEN - nbuf + jnp.arange(nbuf + T)
    diff = qpos[:, None] - kpos[None, :]
    valid = (diff >= 0) & (diff < WINDOW) & (kpos[None, :] >= 0)
    p = sink_softmax(jnp.where(valid[None, None, None], s, NEG), sinks)
    o = jnp.einsum('bhgqk,bkhd->bqhgd', p.astype(vv.dtype), vv)
    return o.reshape(Bd, T, ATTN_WIDTH), kk[:, -nbuf:], vv[:, -nbuf:]


def causal_depthwise(x_ext, w, b):
    out = lax.conv_general_dilated(x_ext, w[:, None, :].astype(x_ext.dtype), window_strides=(1,), padding='VALID',
                                   dimension_numbers=('NWC', 'WIO', 'NWC'), feature_group_count=x_ext.shape[-1])
    return out + b


def decoder_layer(x, c, pos, attend, conv_prev, w_ada, b_ada, w_in, sinks, w_dw, b_dw, cn_gain, cn_bias,
                  w_br_attn, w_br_conv, w_out, ln1_gain, ln1_bias, w_ffn_gate, w_ffn_up, w_ffn_down,
                  ln2_gain, ln2_bias):
    B, S = x.shape[0], x.shape[1]
    mod = jnp.dot(jax.nn.silu(c), w_ada) + b_ada
    sh1, sc1, g1, sh2, sc2, g2 = jnp.split(mod[:, None, :], 6, axis=-1)
    u = layer_norm(x) * (1 + sc1) + sh1
    proj = jnp.dot(u, w_in)
    q, k, v, glu_a, glu_b, gate_a, gate_b = jnp.split(proj, SPLITS, axis=-1)
    q = rope(q.reshape(B, S, N_Q_HEADS, HEAD_DIM), pos)
    k = rope(k.reshape(B, S, N_KV_HEADS, HEAD_DIM), pos)
    v = v.reshape(B, S, N_KV_HEADS, HEAD_DIM)
    attn, k_state, v_state = attend(q, k, v, sinks)
    glu = glu_a * jax.nn.sigmoid(glu_b)
    conv_in = jnp.concatenate([conv_prev, glu], axis=1)
    conv = jax.nn.silu(layer_norm(causal_depthwise(conv_in, w_dw, b_dw), cn_gain, cn_bias))
    merged = jax.nn.sigmoid(gate_a) * jnp.dot(attn, w_br_attn) + jax.nn.sigmoid(gate_b) * jnp.dot(conv, w_br_conv)
    mix = jnp.dot(merged, w_out)
    x = layer_norm(ALPHA * x + g1 * mix, ln1_gain, ln1_bias)
    u2 = layer_norm(x) * (1 + sc2) + sh2
    h = jax.nn.silu(jnp.dot(u2, w_ffn_gate)) * jnp.dot(u2, w_ffn_up)
    x = layer_norm(ALPHA * x + g2 * jnp.dot(h, w_ffn_down), ln2_gain, ln2_bias)
    conv_state = conv_in[:, conv_in.shape[1] - (CONV_WIDTH - 1):]
    return x, k_state, v_state, conv_state


def setup_inputs(seed: int = 0) -> dict:
    key = jax.random.key(seed)
    ks = jax.random.split(key, 26)
    nbuf = min(WINDOW, PAST_LEN)
    f32 = jnp.float32

    def nrm(k, shape, scale):
        return jax.random.normal(k, shape, f32) * scale

    return {
        "x_prompt": nrm(ks[0], (BATCH, SEQ, D_MODEL), 1.0),
        "x_sample": nrm(ks[1], (DEC_BATCH, DEC_SEQ, D_MODEL), 1.0),
        "c_prompt": nrm(ks[2], (BATCH, D_MODEL), 1.0),
        "c_sample": nrm(ks[3], (DEC_BATCH, D_MODEL), 1.0),
        "cache_k_win": nrm(ks[4], (DEPTH, DEC_BATCH, nbuf, N_KV_HEADS, HEAD_DIM), 1.0),
        "cache_v_win": nrm(ks[5], (DEPTH, DEC_BATCH, nbuf, N_KV_HEADS, HEAD_DIM), 1.0),
        "state_conv": nrm(ks[6], (DEPTH, DEC_BATCH, CONV_WIDTH - 1, CONV_CH), 0.5),
        "w_ada": nrm(ks[7], (DEPTH, D_MODEL, 6 * D_MODEL), 0.5 * D_MODEL ** -0.5),
        "b_ada": nrm(ks[8], (DEPTH, 6 * D_MODEL), 0.1),
        "w_in": nrm(ks[9], (DEPTH, D_MODEL, N_IN), D_MODEL ** -0.5),
        "attn_sinks": nrm(ks[10], (DEPTH, N_Q_HEADS), 0.5),
        "w_dw": nrm(ks[11], (DEPTH, CONV_WIDTH, CONV_CH), CONV_WIDTH ** -0.5),
        "b_dw": nrm(ks[12], (DEPTH, CONV_CH), 0.01),
        "cn_gain": 1.0 + nrm(ks[13], (DEPTH, CONV_CH), 0.01),
        "cn_bias": nrm(ks[14], (DEPTH, CONV_CH), 0.01),
        "w_br_attn": nrm(ks[15], (DEPTH, ATTN_WIDTH, D_MODEL), BETA * ATTN_WIDTH ** -0.5),
        "w_br_conv": nrm(ks[16], (DEPTH, CONV_CH, D_MODEL), BETA * CONV_CH ** -0.5),
        "w_out": nrm(ks[17], (DEPTH, D_MODEL, D_MODEL), BETA * D_MODEL ** -0.5),
        "ln1_gain": 1.0 + nrm(ks[18], (DEPTH, D_MODEL), 0.01),
        "ln1_bias": nrm(ks[19], (DEPTH, D_MODEL), 0.01),
        "w_ffn_gate": nrm(ks[20], (DEPTH, D_MODEL, D_FF), D_MODEL ** -0.5),
        "w_ffn_up": nrm(ks[21], (DEPTH, D_MODEL, D_FF), D_MODEL ** -0.5),
        "w_ffn_down": nrm(ks[22], (DEPTH, D_FF, D_MODEL), BETA * D_FF ** -0.5),
        "ln2_gain": 1.0 + nrm(ks[23], (DEPTH, D_MODEL), 0.01),
        "ln2_bias": nrm(ks[24], (DEPTH, D_MODEL), 0.01),
    }


def reference(x_prompt, x_sample, c_prompt, c_sample, cache_k_win, cache_v_win, state_conv,
              w_ada, b_ada, w_in, attn_sinks, w_dw, b_dw, cn_gain, cn_bias, w_br_attn, w_br_conv, w_out,
              ln1_gain, ln1_bias, w_ffn_gate, w_ffn_up, w_ffn_down, ln2_gain, ln2_bias):
    S = x_prompt.shape[1]
    T = x_sample.shape[1]
    pos_p = jnp.arange(S, dtype=jnp.int32)
    pos_s = PAST_LEN + jnp.arange(T, dtype=jnp.int32)
    yp, ys = x_prompt, x_sample
    kp_l, vp_l, cp_l, ks_l, vs_l, cs_l = [], [], [], [], [], []
    for l in range(DEPTH):
        w = (w_ada[l], b_ada[l], w_in[l], attn_sinks[l], w_dw[l], b_dw[l], cn_gain[l], cn_bias[l],
             w_br_attn[l], w_br_conv[l], w_out[l], ln1_gain[l], ln1_bias[l], w_ffn_gate[l], w_ffn_up[l],
             w_ffn_down[l], ln2_gain[l], ln2_bias[l])
        conv0 = jnp.zeros((yp.shape[0], CONV_WIDTH - 1, CONV_CH), yp.dtype)
        yp, kp, vp, cp = decoder_layer(yp, c_prompt, pos_p, attn_prompt, conv0, *w)
        att_s = functools.partial(attn_sample, k_buf=cache_k_win[l], v_buf=cache_v_win[l])
        ys, kn, vn, cn = decoder_layer(ys, c_sample, pos_s, att_s, state_conv[l], *w)
        kp_l.append(kp); vp_l.append(vp); cp_l.append(cp)
        ks_l.append(kn); vs_l.append(vn); cs_l.append(cn)
    k_win_prompt = jnp.stack(kp_l)
    v_win_prompt = jnp.stack(vp_l)
    conv_prompt = jnp.stack(cp_l)
    k_win_sample = jnp.stack(ks_l)
    v_win_sample = jnp.stack(vs_l)
    conv_sample = jnp.stack(cs_l)
    return (yp, ys, k_win_prompt, v_win_prompt, conv_prompt, k_win_sample, v_win_sample, conv_sample)
```

```python
import numpy as np
from contextlib import ExitStack
import concourse.bass as bass
import concourse.mybir as mybir
from concourse.bass_utils import run_bass_kernel_spmd

F32 = mybir.dt.float32
BF16 = mybir.dt.bfloat16
AF = mybir.ActivationFunctionType
ALU = mybir.AluOpType

D = 2048
NCORE = 8
ALPHA = 2.0 ** 0.25
EPS = 1e-5
NEG = -30000.0
DFF = 5632
RING = 4096


class Sync:
    ENG = ("pe", "act", "dve", "pool", "sp")

    def __init__(self, nc, stack):
        self.nc = nc
        self.stack = stack
        self.sem = {}
        self.cnt = {}
        self.prog = {e: [] for e in self.ENG}
        self.waited = {e: {} for e in self.ENG}
        self.res = {}
        self.stopped = False
        self.dpool = {}
        self.dctr = {}
        self.fences = {}
        for e in self.ENG:
            self._mksem(e)

    def _mksem(self, name):
        self.sem[name] = self.stack.enter_context(self.nc.semaphore("s_" + name))
        self.cnt[name] = 0

    def _deps(self, eng, reads, writes):
        need = {}

        def add(sv):
            if sv is None:
                return
            s, v = sv
            if need.get(s, 0) < v:
                need[s] = v

        for r in reads:
            st = self.res.get(r)
            if st:
                add(st["w"])
        for w in writes:
            st = self.res.get(w)
            if st:
                add(st["w"])
                for sv in st["r"]:
                    add(sv)
            fc = self.fences.pop(w, None)
            if fc:
                for sv in fc.items():
                    add(sv)
        out = []
        for s, v in need.items():
            if s == "pe" and eng == "pe":
                continue
            if self.waited[eng].get(s, 0) >= v:
                continue
            self.waited[eng][s] = v
            out.append((s, v))
        return out

    def _record(self, reads, writes, sv):
        for r in reads:
            st = self.res.setdefault(r, {"w": None, "r": []})
            st["r"].append(sv)
            if len(st["r"]) > 64:
                mx = {}
                for s, v in st["r"]:
                    mx[s] = max(mx.get(s, 0), v)
                st["r"] = list(mx.items())
        for w in writes:
            self.res[w] = {"w": sv, "r": []}

    def fence(self, names):
        snap = {k: v for k, v in self.cnt.items() if v > 0}
        for n in names:
            self.fences[n] = dict(snap)
            self.res.pop(n, None)

    def op(self, eng, fn, reads=(), writes=(), signal=True):
        if self.stopped:
            return
        waits = self._deps(eng, reads, writes)
        if signal:
            self.cnt[eng] += 1
            sv = (eng, self.cnt[eng])
        else:
            sv = (eng, self.cnt[eng] + 1)
        sem = self.sem[eng]
        sems = self.sem

        def run(e, waits=waits, fn=fn, signal=signal, sem=sem):
            for s, v in waits:
                e.wait_ge(sems[s], v)
            ins = fn(e)
            if signal:
                ins.then_inc(sem, 1)

        self.prog[eng].append(run)
        self._record(reads, writes, sv)

    def dma(self, eng, stream, out, in_, reads=(), writes=()):
        if self.stopped:
            return
        if eng not in self.dpool:
            n = 16 if eng == "sp" else 8
            self.dpool[eng] = [f"d_{eng}{i}" for i in range(n)]
            self.dctr[eng] = 0
            for nm in self.dpool[eng]:
                self._mksem(nm)
        pool = self.dpool[eng]
        stream = pool[self.dctr[eng] % len(pool)]
        self.dctr[eng] += 1
        waits = self._deps(eng, reads, writes)
        prev = self.cnt[stream]
        if prev > 0 and self.waited[eng].get(stream, 0) < prev:
            self.waited[eng][stream] = prev
            waits.append((stream, prev))
        self.cnt[stream] += 16
        sv = (stream, self.cnt[stream])
        sem = self.sem[stream]
        sems = self.sem

        def run(e, waits=waits, out=out, in_=in_, sem=sem):
            for s, v in waits:
                e.wait_ge(sems[s], v)
            e.dma_start(out=out, in_=in_).then_inc(sem, 16)

        self.prog[eng].append(run)
        self._record(reads, writes, sv)

    def finish(self, final_eng="sp"):
        waits = []
        for s, c in self.cnt.items():
            if c > 0 and s != final_eng:
                waits.append((s, c))
        sems = self.sem

        def run(e, waits=waits):
            for s, v in waits:
                e.wait_ge(sems[s], v)

        self.prog[final_eng].append(run)

    def emit(self):
        nc = self.nc
        prog = self.prog
        with nc.Block() as block:
            @block.tensor
            def _(e):
                for f in prog["pe"]:
                    f(e)

            @block.scalar
            def _(e):
                for f in prog["act"]:
                    f(e)

            @block.vector
            def _(e):
                for f in prog["dve"]:
                    f(e)

            @block.gpsimd
            def _(e):
                for f in prog["pool"]:
                    f(e)

            @block.sync
            def _(e):
                for f in prog["sp"]:
                    f(e)


class _Stop(Exception):
    pass


def build_nc(stop=None, dumps=()):
    nc = bass.Bass("TRN2", target_bir_lowering=False)

    def din(name, shape):
        return nc.dram_tensor(name, list(shape), F32, kind="ExternalInput").ap()

    def dout(name, shape):
        return nc.dram_tensor(name, list(shape), F32, kind="ExternalOutput").ap()

    xin = din("xin", [10, 128, D])
    cvec = din("cvec", [17, D])
    cachek = din("cachek", [16, 128, 256])
    cachev = din("cachev", [16, 128, 256])
    stconv = din("stconv", [16, 30, 1024])
    w1 = din("w1", [14, 128, 4096])
    permd = din("permd", [5, 128, 128])
    w2 = din("w2", [16, 128, 6144])
    w3 = din("w3", [8, 128, 4096])
    w4 = din("w4", [44, 128, 4096])
    w5 = din("w5", [16, 128, 5632])
    wa = din("wa", [48, 128, 4096])
    bada = din("bada", [96, 128])
    sinks8 = din("sinks8", [8, 128])
    wdw = din("wdw", [31, 1024])
    cvs = din("cvs", [3, 1024])
    lngb = din("lngb", [4, D])
    ropec = din("ropec", [3, 128, 512])
    ropes = din("ropes", [3, 128, 512])
    flags = din("flags", [128, 3])
    identd = din("identd", [128, 128])
    masks = din("masks", [3, 128, 128])
    msc = din("msc", [16, 128, 128])
    msn = din("msn", [128, 128])

    y_main = dout("y_main", [9, 128, D])
    kwin_p = dout("kwin_p", [128, 256])
    vwin_p = dout("vwin_p", [128, 256])
    conv_p = dout("conv_p", [30, 1024])
    kwin_s = dout("kwin_s", [16, 128, 256])
    vwin_s = dout("vwin_s", [16, 128, 256])
    conv_s = dout("conv_s", [16, 30, 1024])

    with ExitStack() as st:
        S = Sync(nc, st)
        T = lambda name, shape, dt=F32: st.enter_context(nc.sbuf_tensor(name, list(shape), dt))
        ps = [st.enter_context(nc.psum_tensor(f"ps{i}", [128, 512], F32)) for i in range(8)]
        bank_ctr = [0]

        def nb():
            b = bank_ctr[0] % 6
            bank_ctr[0] += 1
            return b

        def PS(b):
            return ("ps", b)

        def stage(name, **tiles):
            for k, (ap, shape, dt, rname) in tiles.items():
                if k in dumps:
                    d = nc.dram_tensor("dbg_" + k, list(shape), dt, kind="ExternalOutput").ap()
                    S.dma("sp", "st", d, ap, reads=[rname])
            if stop == name:
                S.stopped = True

        identf = T("identf", [128, 128])
        identb = T("identb", [128, 128], BF16)
        onesf = T("onesf", [128, 128])
        onesb = T("onesb", [128, 64], BF16)
        mk = T("mk", [128, 3, 128], BF16)
        mksc = T("mksc", [128, 16, 128], BF16)
        mksn = T("mksn", [128, 128], BF16)
        flg = T("flg", [128, 3])
        eps_t = T("eps_t", [128, 1])
        S.dma("sp", "ld", identf[:], identd, writes=["identf"])
        S.dma("pool", "wld", identb[:], identd, writes=["identb"])
        S.dma("sp", "ld", flg[:], flags, writes=["flg"])
        perm = T("perm", [128, 5, 128])
        S.dma("sp", "ld", perm[:], permd.rearrange("i p c -> p i c"), writes=["perm"])
        S.op("dve", lambda e: e.memset(onesf[:], 1.0), writes=["onesf"])
        S.op("dve", lambda e: e.memset(onesb[:], 1.0), writes=["onesb"])
        S.op("dve", lambda e: e.memset(eps_t[:], EPS), writes=["eps_t"])

        NRING = 4
        ring = [T(f"ring{i}", [128, RING], BF16) for i in range(NRING)]
        ring_ctr = [0]

        def wload(src, nel):
            i = ring_ctr[0] % NRING
            ring_ctr[0] += 1
            S.dma("pool", "wld", ring[i][:, 0:nel], src, writes=[("ring", i)])
            return i

        xs = T("xs", [128, D])
        zt = T("zt", [128, D], BF16)
        uT = T("uT", [128, 16, 512], BF16)
        mergedT = T("mergedT", [128, 16, 384], BF16)
        tA = [T(f"tA{i}", [128, 512]) for i in range(2)]
        tB = [T(f"tB{i}", [128, 512]) for i in range(2)]
        pTb = [T(f"pTb{i}", [128, 512], BF16) for i in range(2)]
        rc = T("rc", [128, 256])
        stats = T("stats", [128, 3, 4, 6])
        mv = T("mv", [128, 3, 2])
        rstd = T("rstd", [128, 3, 1])
        nmr = T("nmr", [128, 3, 1])
        csT = T("csT", [128, 16, 17], BF16)
        modT = T("modT", [128, 6, 16, 17])
        esT = T("esT", [128, 8])
        bT1 = T("bT1", [128, 96])
        tmp_ctr = [0]

        def tmpi():
            i = tmp_ctr[0] % 2
            tmp_ctr[0] += 1
            return i

        def small_T(src_ap, nrows, name, ncol_chunks, tmp, tres):
            dst = T(name, [128, ncol_chunks, nrows])
            S.dma("sp", "ld", tmp, src_ap, writes=[tres])
            for c in range(ncol_chunks):
                b = nb()
                S.op("pe", lambda e, b=b, c=c: e.transpose(out=ps[b][:, 0:nrows], in_=tmp[:, c * 128:(c + 1) * 128],
                                                           identity=identf[0:nrows, 0:nrows]),
                     reads=[tres, "identf"], writes=[PS(b)])
                S.op("dve", lambda e, b=b, c=c: e.tensor_copy(out=dst[:, c, :], in_=ps[b][:, 0:nrows]),
                     reads=[], writes=[PS(b), name])
            return dst

        bT = small_T(bada, 96, "bT", 1, xs[0:96, 0:128], "xs")
        wdwT = small_T(wdw, 31, "wdwT", 8, xs[0:31, 0:1024], "xs")
        cvT = small_T(cvs, 3, "cvT", 8, xs[0:3, 0:1024], "xs")
        skT = small_T(sinks8, 8, "skT", 1, xs[0:8, 0:128], "xs")
        S.op("act", lambda e: e.activation(out=esT[:], in_=skT[:, 0, :], func=AF.Exp), reads=["skT"], writes=["esT"])
        S.op("dve", lambda e: e.tensor_scalar(out=bT1[:], in0=bT[:, 0, :], scalar1=1.0, scalar2=None, op0=ALU.add),
             reads=["bT"], writes=["bT1"])

        cld = xs[0:17, :]
        S.dma("sp", "ld", cld, cvec, writes=["xs"])
        S.op("act", lambda e: e.activation(out=cld, in_=cld, func=AF.Silu), writes=["xs"])
        for kc in range(16):
            b = nb()
            S.op("pe", lambda e, b=b, kc=kc: e.transpose(out=ps[b][:, 0:17], in_=cld[:, kc * 128:(kc + 1) * 128],
                                                         identity=identf[0:17, 0:17]),
                 reads=["xs", "identf"], writes=[PS(b)])
            S.op("dve", lambda e, b=b, kc=kc: e.tensor_copy(out=csT[:, kc, :], in_=ps[b][:, 0:17]),
                 writes=[PS(b), "csT"])

        ada_pending = []

        def ada_blk(blk):
            ri = wload(wa[blk], 4096)
            rv = ring[ri][:, 0:4096].rearrange("p (k c) -> p k c", c=256)
            grp = blk // 8
            for jj in range(2):
                ch = (blk % 8) * 2 + jj
                b = nb()
                for kc in range(16):
                    S.op("pe", lambda e, b=b, kc=kc, jj=jj: e.matmul(ps[b][:, 0:17], lhsT=rv[:, kc, jj * 128:(jj + 1) * 128],
                                                                     rhs=csT[:, kc, :], start=(kc == 0), stop=(kc == 15)),
                         reads=[("ring", ri), "csT"], writes=[PS(b)], signal=(kc == 15))
                bsrc = bT1 if grp in (1, 4) else bT[:, 0, :]
                col = grp * 16 + ch
                S.op("act", lambda e, b=b, ch=ch, bsrc=bsrc, col=col: e.activation(
                    out=modT[:, grp, ch, :], in_=ps[b][:, 0:17], func=AF.Identity, bias=bsrc[:, col:col + 1], scale=1.0),
                    reads=["bT", "bT1"], writes=[PS(b), "modT"])

        try:
            stage("const", bT=(bT[:, 0, :], [128, 96], F32, "bT"), wdwT=(wdwT[:].rearrange("p a b -> p (a b)"), [128, 248], F32, "wdwT"),
                  esT=(esT[:], [128, 8], F32, "esT"), csT=(csT[:].rearrange("p a b -> p (a b)"), [128, 272], BF16, "csT"))
            for blk in range(16):
                ada_blk(blk)
            ada_pending.extend(range(16, 48))
            S.dma("pool", "wld", mk[:], masks.rearrange("i p c -> p i c"), writes=["mk"])
            for q in range(4):
                S.dma("pool", "wld", mksc[:, q * 4:(q + 1) * 4, :], msc[q * 4:(q + 1) * 4].rearrange("i p c -> p i c"), writes=["mksc"])
            S.dma("pool", "wld", mksn[:], msn, writes=["mksn"])
            stage("ada", modT=(modT[:].rearrange("p a b c -> p (a b c)"), [128, 6 * 16 * 17], F32, "modT"))
        except _Stop:
            S.finish("sp")
            S.emit()
            return nc

        def ln_stats(src, rname, ti=0):
            for c in range(4):
                S.op("dve", lambda e, c=c: e.bn_stats(out=stats[:, ti, c, :], in_=src[:, c * 512:(c + 1) * 512]),
                     reads=[rname], writes=[("stats", ti)])
            S.op("dve", lambda e: e.bn_aggr(out=mv[:, ti, :], in_=stats[:, ti, :, :]), reads=[("stats", ti)], writes=[("mv", ti)])
            S.op("act", lambda e: e.activation(out=rstd[:, ti, :], in_=mv[:, ti, 1:2], func=AF.Sqrt, bias=eps_t[:, 0:1], scale=1.0),
                 reads=[("mv", ti), "eps_t"], writes=[("rstd", ti)])
            S.op("dve", lambda e: e.reciprocal(out=rstd[:, ti, :], in_=rstd[:, ti, :]), reads=[("rstd", ti)], writes=[("rstd", ti)])
            S.op("dve", lambda e: e.tensor_scalar(out=nmr[:, ti, :], in0=mv[:, ti, 0:1], scalar1=rstd[:, ti, 0:1], scalar2=-1.0,
                                                  op0=ALU.mult, op1=ALU.mult), reads=[("mv", ti), ("rstd", ti)], writes=[("nmr", ti)])

        def ln_z(src, rname, ti, ztile, zname):
            S.op("act", lambda e: e.activation(out=ztile[:], in_=src, func=AF.Identity, bias=nmr[:, ti, 0:1], scale=rstd[:, ti, 0:1]),
                 reads=[rname, ("nmr", ti), ("rstd", ti)], writes=[zname])

        def ln_mod_T(src, rname, col0, gsh, gsc, sample):
            ln_stats(src, rname, 0)
            ln_z(src, rname, 0, zt, "zt")
            ln_T(zt, "zt", col0, gsh, gsc, sample)

        def ln_T(zt, zname, col0, gsh, gsc, sample):
            for half in range(2):
                b = nb()
                pv = ps[b][:].bitcast(BF16).rearrange("p (k c) -> p k c", c=128)[:, 0:8, :]
                for k8 in range(8):
                    kc = half * 8 + k8
                    S.op("pe", lambda e, pv=pv, k8=k8, kc=kc: e.transpose(out=pv[:, k8, :], in_=zt[:, kc * 128:(kc + 1) * 128],
                                                                         identity=identb[:]),
                         reads=[zname, "identb"], writes=[PS(b)], signal=(k8 == 7))
                for k8 in range(8):
                    kc = half * 8 + k8
                    if not sample:
                        S.op("act", lambda e, pv=pv, k8=k8, kc=kc: e.activation(
                            out=uT[:, kc, col0:col0 + 128], in_=pv[:, k8, :], func=AF.Identity,
                            bias=modT[:, gsh, kc, 0:1], scale=modT[:, gsc, kc, 0:1]),
                            reads=["modT"], writes=[PS(b), "uT"])
                    else:
                        i = tmpi()
                        t3 = tA[i][:, 0:128].rearrange("p (s t) -> p s t", t=8)
                        S.op("dve", lambda e, pv=pv, k8=k8, kc=kc, t3=t3: e.tensor_tensor(
                            out=t3, in0=pv[:, k8, :].rearrange("p (s t) -> p s t", t=8),
                            in1=modT[:, gsc, kc, 1:17].unsqueeze(2).to_broadcast([128, 16, 8]), op=ALU.mult),
                            reads=["modT"], writes=[PS(b), ("tA", i)])
                        S.op("dve", lambda e, kc=kc, t3=t3: e.tensor_tensor(
                            out=uT[:, kc, col0:col0 + 128].rearrange("p (s t) -> p s t", t=8), in0=t3,
                            in1=modT[:, gsh, kc, 1:17].unsqueeze(2).to_broadcast([128, 16, 8]), op=ALU.add),
                            reads=["modT", ("tA", i)], writes=["uT"])

        def gate_evac(b, dst, dname, grp, j, ncol, has_sample):
            npr = ncol - 128 if has_sample else ncol
            S.op("act", lambda e: e.activation(out=dst[:, 0:npr], in_=ps[b][:, 0:npr], func=AF.Identity, bias=0.0,
                                               scale=modT[:, grp, j, 0:1]),
                 reads=["modT"], writes=[PS(b), dname])
            if has_sample:
                S.op("dve", lambda e: e.tensor_tensor(
                    out=dst[:, npr:ncol].rearrange("p (s t) -> p s t", t=8),
                    in0=ps[b][:, npr:ncol].rearrange("p (s t) -> p s t", t=8),
                    in1=modT[:, grp, j, 1:17].unsqueeze(2).to_broadcast([128, 16, 8]), op=ALU.mult),
                    reads=["modT"], writes=[PS(b), dname])

        def resid_add(XR, src, sname, j, ntile):
            b = nb()
            for t in range(ntile):
                S.op("pe", lambda e, t=t: e.transpose(out=ps[b][:, t * 128:(t + 1) * 128], in_=src[:, t * 128:(t + 1) * 128], identity=identf[:]),
                     reads=[sname, "identf"], writes=[PS(b)], signal=(t == ntile - 1))
            S.op("dve", lambda e: e.scalar_tensor_tensor(
                out=XR[:, 0:ntile, j * 128:(j + 1) * 128], in0=XR[:, 0:ntile, j * 128:(j + 1) * 128], scalar=ALPHA,
                in1=ps[b][:, 0:ntile * 128].rearrange("p (t c) -> p t c", t=ntile), op0=ALU.mult, op1=ALU.add),
                reads=[], writes=[PS(b)] + [("XR", t) for t in range(ntile)])

        def s2_tiles(hidx, mains, has_s):
            out = []
            srcs = [hidx] + list(mains)
            for ct, xi in enumerate(srcs):
                def fa(ct=ct, xi=xi):
                    S.dma("sp", "ld", xs[:], xin[xi], writes=["xs"])
                    ln_stats(xs[:], "xs", 0)
                    ln_z(xs[:], "xs", 0, zt, "zt")

                def fb(ct=ct):
                    ln_T(zt, "zt", 128 * ct, 0, 1, has_s and ct == 3)
                out.append(fa)
                out.append(fb)
            return out

        def drain_ada(n):
            for _ in range(n):
                if ada_pending:
                    ada_blk(ada_pending.pop(0))

        def run_pass(pi, hidx, mains, has_s, s2_next):
            NT = 3
            NC_ = 128 * NT
            npr_t = NT - 1 if has_s else NT
            npc = npr_t * 128
            stage(f"s2_{pi}", **{f"uT{pi}": (uT[:].rearrange("p a b -> p (a b)"), [128, 8192], BF16, "uT")})
            with ExitStack() as s1:
                T1 = lambda name, shape, dt=F32: s1.enter_context(nc.sbuf_tensor(f"{name}{pi}", list(shape), dt))
                gluT = T1("gluT", [128, 8, 512], BF16)
                gluTf = T1("gluTf", [128, 8, 256])
                attnT = T1("attnT", [128, 8, 384], BF16)
                convT = T1("convT", [128, 8, 384], BF16)
                kTf = T1("kTf", [128, 4, 256])
                vf = T1("vf", [128, 4, 256])
                sgT = T1("sgT", [128, 2, 16, 384], BF16)
                S.fence(["gluT", "gluTf", "attnT", "convT", "kTf", "vf", "sgT"])
                with ExitStack() as sA:
                    TA_ = lambda name, shape, dt=F32: sA.enter_context(nc.sbuf_tensor(f"{name}{pi}", list(shape), dt))
                    qT = TA_("qT", [128, 8, 384], BF16)
                    kT = TA_("kT", [128, 4, 512], BF16)
                    vb = TA_("vb", [128, 4, 256], BF16)
                    sR = ExitStack()
                    rcos = sR.enter_context(nc.sbuf_tensor(f"rcos{pi}", [128, 512], F32))
                    rsin = sR.enter_context(nc.sbuf_tensor(f"rsin{pi}", [128, 512], F32))
                    kraw = sR.enter_context(nc.sbuf_tensor(f"kraw{pi}", [128, 2, 512], F32))
                    S.fence(["qT", "kT", "vb", "rcos", "rsin", ("kraw", 0), ("kraw", 1)])
                    S.dma("sp", "ld", rcos[:], ropec[pi], writes=["rcos"])
                    S.dma("sp", "ld", rsin[:], ropes[pi], writes=["rsin"])

                    def proj(ri, col_lo, ncol, c0, b):
                        rv = ring[ri][:, 0:4096].rearrange("p (k c) -> p k c", c=256)
                        for kc in range(16):
                            S.op("pe", lambda e, kc=kc: e.matmul(ps[b][:, 0:ncol], lhsT=rv[:, kc, c0:c0 + 128],
                                                                 rhs=uT[:, kc, col_lo:col_lo + ncol], start=(kc == 0), stop=(kc == 15)),
                                 reads=[("ring", ri), "uT"], writes=[PS(b)], signal=(kc == 15))

                    def rope_evac(ba, bb, col_lo, ncol, dst, dname, dstf=None):
                        i = tmpi()
                        S.op("dve", lambda e: e.tensor_tensor(out=tA[i][:, 0:ncol], in0=ps[ba][:, 0:ncol], in1=rcos[:, col_lo:col_lo + ncol], op=ALU.mult),
                             reads=["rcos"], writes=[PS(ba), ("tA", i)])
                        S.op("dve", lambda e: e.tensor_tensor(out=tB[i][:, 0:ncol], in0=ps[bb][:, 0:ncol], in1=rsin[:, col_lo:col_lo + ncol], op=ALU.mult),
                             reads=["rsin"], writes=[PS(bb), ("tB", i)])
                        S.op("dve", lambda e: e.tensor_tensor(out=dst, in0=tA[i][:, 0:ncol], in1=tB[i][:, 0:ncol], op=ALU.add),
                             reads=[("tA", i), ("tB", i)], writes=[dname])
                        if dstf is not None:
                            S.op("dve", lambda e: e.tensor_tensor(out=dstf, in0=tA[i][:, 256:512], in1=tB[i][:, 256:512], op=ALU.add),
                                 reads=[("tA", i), ("tB", i)], writes=["kTf"])

                    def k_unit():
                        ri = wload(w1[0], 4096)
                        bA, bB = nb(), nb()
                        proj(ri, 0, 512, 0, bA)
                        proj(ri, 0, 512, 128, bB)
                        S.op("act", lambda e: e.activation(out=kraw[:, 0, :], in_=ps[bA][:], func=AF.Identity, bias=0.0, scale=1.0),
                             writes=[PS(bA), ("kraw", 0)])
                        S.op("act", lambda e: e.activation(out=kraw[:, 1, :], in_=ps[bB][:], func=AF.Identity, bias=0.0, scale=1.0),
                             writes=[PS(bB), ("kraw", 1)])
                        for g in range(4):
                            ba, bb = nb(), nb()
                            S.op("pe", lambda e, g=g, ba=ba: e.matmul(ps[ba][:, 0:512], lhsT=perm[:, 1 + g % 2, :], rhs=kraw[:, g // 2, :], start=True, stop=True),
                                 reads=["perm", ("kraw", g // 2)], writes=[PS(ba)])
                            S.op("pe", lambda e, g=g, bb=bb: e.matmul(ps[bb][:, 0:512], lhsT=perm[:, 3 + g % 2, :], rhs=kraw[:, g // 2, :], start=True, stop=True),
                                 reads=["perm", ("kraw", g // 2)], writes=[PS(bb)])
                            rope_evac(ba, bb, 0, 512, kT[:, g, :], "kT", kTf[:, g, :])

                    k_unit()

                    def v_unit():
                        ri = wload(w1[1], 4096)
                        rvv = ring[ri][:, 0:4096].rearrange("p (k c) -> p k c", c=256)
                        for ct in range(4):
                            b = nb()
                            for kc in range(16):
                                S.op("pe", lambda e, kc=kc, ct=ct, b=b: e.matmul(ps[b][:, 0:256], lhsT=uT[:, kc, ct * 128:(ct + 1) * 128],
                                                                             rhs=rvv[:, kc, :], start=(kc == 0), stop=(kc == 15)),
                                     reads=[("ring", ri), "uT"], writes=[PS(b)], signal=(kc == 15))
                            S.op("act", lambda e, ct=ct, b=b: e.activation(out=vb[:, ct, :], in_=ps[b][:, 0:256], func=AF.Identity, bias=0.0, scale=1.0),
                                 writes=[PS(b), "vb"])
                            S.op("dve", lambda e, ct=ct, b=b: e.tensor_copy(out=vf[:, ct, :], in_=ps[b][:, 0:256]), writes=[PS(b), "vf"])

                    v_unit()
                    def q_unit(iq):
                        ri = wload(w1[2 + iq], 4096)
                        for jj in range(2):
                            c = 2 * iq + jj
                            ba = nb()
                            proj(ri, 128, 384, jj * 128, ba)
                            S.op("act", lambda e, jj=jj, ba=ba: e.activation(out=kraw[:, jj, 0:384], in_=ps[ba][:, 0:384], func=AF.Identity, bias=0.0, scale=1.0),
                                 writes=[PS(ba), ("kraw", jj)])
                            bb = nb()
                            S.op("pe", lambda e, jj=jj, bb=bb: e.matmul(ps[bb][:, 0:384], lhsT=perm[:, 0, :], rhs=kraw[:, jj, 0:384], start=True, stop=True),
                                 reads=["perm", ("kraw", jj)], writes=[PS(bb)])
                            rope_evac(ba, bb, 128, 384, qT[:, c, :], "qT")

                    for iq in range(4):
                        q_unit(iq)

                    def glu_unit(i8):
                        ri = wload(w1[6 + i8], 4096)
                        ba, bb = nb(), nb()
                        proj(ri, 0, 512, 0, ba)
                        proj(ri, 0, 512, 128, bb)
                        i = tmpi()
                        S.op("act", lambda e: e.activation(out=tA[i][:], in_=ps[bb][:], func=AF.Sigmoid),
                             writes=[PS(bb), ("tA", i)])
                        S.op("dve", lambda e: e.tensor_tensor(out=gluT[:, i8, :], in0=ps[ba][:], in1=tA[i][:], op=ALU.mult),
                             reads=[("tA", i)], writes=[PS(ba), "gluT"])
                        S.op("dve", lambda e: e.tensor_tensor(out=gluTf[:, i8, :], in0=ps[ba][:, 256:512], in1=tA[i][:, 256:512], op=ALU.mult),
                             reads=[("tA", i)], writes=[PS(ba), "gluTf"])
                        S.op("dve", lambda e: e.tensor_scalar(out=gluT[:, i8, 0:128], in0=gluT[:, i8, 0:128], scalar1=flg[:, pi:pi + 1],
                                                              scalar2=None, op0=ALU.mult), reads=["flg"], writes=["gluT"])

                    for i8 in range(8):
                        glu_unit(i8)

                    stage(f"s3_{pi}", **{f"kT{pi}": (kT[:].rearrange("p a b -> p (a b)"), [128, 2048], BF16, "kT"), f"qT{pi}": (qT[:].rearrange("p a b -> p (a b)"), [128, 3072], BF16, "qT"), f"vb{pi}": (vb[:].rearrange("p a b -> p (a b)"), [128, 1024], BF16, "vb"), f"gluT{pi}": (gluT[:].rearrange("p a b -> p (a b)"), [128, 4096], BF16, "gluT"), f"ring0_{pi}": (ring[0][:], [128, 6144], BF16, ("ring", 0)), f"ring1_{pi}": (ring[1][:], [128, 6144], BF16, ("ring", 1)), f"rcos{pi}": (rcos[:], [128, 512], F32, "rcos"), f"tA{pi}": (tA[0][:], [128, 512], F32, ("tA", 0))})
                    gate_pending = list(range(16))

                    def gate_unit(j):
                        ri = wload(w2[j][:, 0:4096], 4096)
                        rv = ring[ri][:, 0:4096].rearrange("p (k c) -> p k c", c=128)
                        bA, bB = nb(), nb()
                        for kc in range(16):
                            S.op("pe", lambda e, kc=kc: e.matmul(ps[bA][:, 0:NC_], lhsT=rv[:, kc, :], rhs=uT[:, kc, 128:512], start=(kc == 0), stop=(kc == 15)),
                                 reads=[("ring", ri), "uT"], writes=[PS(bA)], signal=(kc == 15))
                        for kc in range(16):
                            S.op("pe", lambda e, kc=kc: e.matmul(ps[bB][:, 0:NC_], lhsT=rv[:, 16 + kc, :], rhs=uT[:, kc, 128:512], start=(kc == 0), stop=(kc == 15)),
                                 reads=[("ring", ri), "uT"], writes=[PS(bB)], signal=(kc == 15))
                        S.op("act", lambda e: e.activation(out=sgT[:, 0, j, :], in_=ps[bA][:, 0:NC_], func=AF.Sigmoid), writes=[PS(bA), "sgT"])
                        S.op("act", lambda e: e.activation(out=sgT[:, 1, j, :], in_=ps[bB][:, 0:NC_], func=AF.Sigmoid), writes=[PS(bB), "sgT"])

                    def drain_gates(n):
                        for _ in range(n):
                            if gate_pending:
                                gate_unit(gate_pending.pop(0))

                    sR.close()
                    if has_s:
                        kcb = TA_("kcb", [128, 16, 256], BF16)
                        kcT = TA_("kcT", [128, 4, 16, 128], BF16)
                        vcb = TA_("vcb", [128, 16, 256], BF16)
                        S.fence(["kcb", "kcT", "vcb"])
                    def attn_tile(qc0, keytiles):
                        nk = len(keytiles)
                        for cp in range(4):
                            bo, bd = 6, 7
                            def qk_exp(ki):
                                kfn, vfn, mask_ap, kres = keytiles[ki]
                                bE, bO = nb(), nb()
                                for bank, hhs in ((bE, (0, 2)), (bO, (1, 3))):
                                    for n_, hh in enumerate(hhs):
                                        S.op("pe", lambda e, n_=n_, bank=bank, mask_ap=mask_ap: e.matmul(
                                            ps[bank][:, n_ * 128:(n_ + 1) * 128], lhsT=identb[:], rhs=mask_ap, start=(n_ == 0), stop=False,
                                            skip_group_check=True),
                                            reads=["identb", "mk", "mksc", "mksn"], writes=[PS(bank)], signal=False)
                                    for n_, hh in enumerate(hhs):
                                        h = 4 * cp + hh
                                        c, hf = h // 2, h % 2
                                        kap = kfn(cp)
                                        S.op("pe", lambda e, n_=n_, bank=bank, kap=kap, c=c, hf=hf: e.matmul(
                                            ps[bank][:, n_ * 128:(n_ + 1) * 128], lhsT=kap[hf * 64:(hf + 1) * 64, :],
                                            rhs=qT[hf * 64:(hf + 1) * 64, c, qc0:qc0 + 128], start=False, stop=True, skip_group_check=True),
                                            reads=[kres, "qT"], writes=[PS(bank)], signal=(n_ == 1))
                                i = tmpi()
                                S.op("act", lambda e, i=i, bE=bE: e.activation(out=pTb[i][:, 0:256], in_=ps[bE][:, 0:256], func=AF.Exp, scale=0.125),
                                     writes=[PS(bE), ("pTb", i)])
                                S.op("act", lambda e, i=i, bO=bO: e.activation(out=pTb[i][:, 256:512], in_=ps[bO][:, 0:256], func=AF.Exp, scale=0.125),
                                     writes=[PS(bO), ("pTb", i)])
                                return i

                            def pv(ki, i):
                                kfn, vfn, mask_ap, kres = keytiles[ki]
                                for hh in range(4):
                                    h = 4 * cp + hh
                                    c, hf = h // 2, h % 2
                                    cl = (c % 2) * 128
                                    pc = {0: 0, 2: 128, 1: 256, 3: 384}[hh]
                                    vap = vfn(cp)
                                    S.op("pe", lambda e, pc=pc, i=i, vap=vap, hf=hf, cl=cl, ki=ki, hh=hh: e.matmul(
                                        ps[bo][hf * 64:(hf + 1) * 64, cl:cl + 128], lhsT=vap, rhs=pTb[i][:, pc:pc + 128],
                                        start=(ki == 0 and hh < 2), stop=(ki == nk - 1), skip_group_check=True),
                                        reads=[("pTb", i), "vb", "vcb"], writes=[PS(bo)], signal=False)
                                    S.op("pe", lambda e, pc=pc, i=i, hf=hf, cl=cl, ki=ki, hh=hh: e.matmul(
                                        ps[bd][hf * 64:(hf + 1) * 64, cl:cl + 128], lhsT=onesb[:], rhs=pTb[i][:, pc:pc + 128],
                                        start=(ki == 0 and hh < 2), stop=(ki == nk - 1), skip_group_check=True),
                                        reads=[("pTb", i), "onesb"], writes=[PS(bd)], signal=(hh == 3))

                            icur = qk_exp(0)
                            for ki in range(nk):
                                inext = qk_exp(ki + 1) if ki + 1 < nk else None
                                pv(ki, icur)
                                icur = inext
                            for cc in range(2):
                                c = 2 * cp + cc
                                S.op("dve", lambda e, cc=cc, c=c: e.tensor_scalar(out=rc[:, cc * 128:(cc + 1) * 128], in0=ps[bd][:, cc * 128:(cc + 1) * 128],
                                                                               scalar1=esT[:, c:c + 1], scalar2=None, op0=ALU.add),
                                     reads=["esT"], writes=[PS(bd), "rc"])
                            S.op("dve", lambda e: e.reciprocal(out=rc[:, 0:256], in_=rc[:, 0:256]), reads=["rc"], writes=["rc"])
                            S.op("dve", lambda e, cp=cp: e.tensor_tensor(
                                out=attnT[:, 2 * cp:2 * cp + 2, qc0:qc0 + 128], in0=ps[bo][:, 0:256].rearrange("p (c q) -> p c q", c=2),
                                in1=rc[:, 0:256].rearrange("p (c q) -> p c q", c=2), op=ALU.mult),
                                reads=["rc"], writes=[PS(bo), "attnT"])
                            drain_gates(1)
                            drain_ada(1)

                    for t in range(npr_t):
                        ct = t + 1
                        mprev = mk[:, 2, :] if (pi == 0 and t == 0) else mk[:, 0, :]
                        kts = [(lambda g, ct=ct: kT[:, g, (ct - 1) * 128:ct * 128], lambda g, ct=ct: vb[:, ct - 1, g * 64:(g + 1) * 64], mprev, "kT"),
                               (lambda g, ct=ct: kT[:, g, ct * 128:(ct + 1) * 128], lambda g, ct=ct: vb[:, ct, g * 64:(g + 1) * 64], mk[:, 1, :], "kT")]
                        attn_tile(t * 128, kts)
                    if has_s:
                        for q in range(4):
                            S.dma("pool", "wld", kcb[:, q * 4:(q + 1) * 4, :], cachek[q * 4:(q + 1) * 4].rearrange("s p c -> p s c"), writes=["kcb"])
                        for q in range(4):
                            S.dma("pool", "wld", vcb[:, q * 4:(q + 1) * 4, :], cachev[q * 4:(q + 1) * 4].rearrange("s p c -> p s c"), writes=["vcb"])
                        for s in range(16):
                            b = nb()
                            pv = ps[b][:].bitcast(BF16)
                            for g in range(4):
                                for hf in range(2):
                                    S.op("pe", lambda e, g=g, hf=hf, pv=pv, s=s: e.transpose(
                                        out=pv[hf * 64:(hf + 1) * 64, g * 128:(g + 1) * 128],
                                        in_=kcb[:, s, g * 64:(g + 1) * 64], identity=identb[:]),
                                        reads=["kcb", "identb"], writes=[PS(b)], signal=(g == 3 and hf == 1))
                            S.op("act", lambda e, s=s, pv=pv: e.activation(out=kcT[:, :, s, :], in_=pv[:, 0:512].rearrange("p (g k) -> p g k", g=4),
                                                                          func=AF.Identity, bias=0.0, scale=1.0), writes=[PS(b), "kcT"])
                        kts = []
                        for s in range(16):
                            kts.append((lambda g, s=s: kcT[:, g, s, :], lambda g, s=s: vcb[:, s, g * 64:(g + 1) * 64], mksc[:, s, :], "kcT"))
                        kts.append((lambda g: kT[:, g, 384:512], lambda g: vb[:, 3, g * 64:(g + 1) * 64], mksn[:], "kT"))
                        attn_tile(256, kts)
                stage(f"s4_{pi}", **{f"attnT{pi}": (attnT[:].rearrange("p a b -> p (a b)"), [128, 3072], BF16, "attnT")})
                with ExitStack() as sB:
                    TB_ = lambda name, shape, dt=F32: sB.enter_context(nc.sbuf_tensor(f"{name}{pi}", list(shape), dt))
                    ycv = TB_("ycv", [128, 8, 384])
                    diag2 = [TB_("diagA", [128, 31, 128], BF16), TB_("diagB", [128, 31, 128], BF16)]
                    cm = TB_("cm", [128, 384])
                    cr = TB_("cr", [128, 384])
                    S.fence([("ycv", 0), ("ycv", 1), ("ycv", 2), ("ycv", 3), ("ycv", 4), ("ycv", 5), ("ycv", 6), ("ycv", 7), ("diag", 0), ("diag", 1), "cm", "cr", "gs", ("stl", 0), ("stl", 1)])
                    if has_s:
                        gs = TB_("gs", [128, 8, 16, 38], BF16)
                        stl = TB_("stl", [120, 2, 1024], BF16)
                        for q4 in range(4):
                            S.dma("pool", "wld", stl[:, q4 % 2, :], stconv[q4 * 4:(q4 + 1) * 4].rearrange("s t c -> (s t) c"), writes=[("stl", q4 % 2)])
                            b = nb()
                            pv = ps[b][:].bitcast(BF16)
                            for i8 in range(8):
                                S.op("pe", lambda e, i8=i8, pv=pv, q4=q4: e.transpose(out=pv[:, i8 * 120:(i8 + 1) * 120], in_=stl[:, q4 % 2, i8 * 128:(i8 + 1) * 128],
                                                                                identity=identb[0:120, 0:120]),
                                     reads=[("stl", q4 % 2), "identb"], writes=[PS(b)], signal=(i8 == 7))
                            for i8 in range(8):
                                S.op("act", lambda e, i8=i8, pv=pv, q4=q4: e.activation(
                                    out=gs[:, i8, q4 * 4:(q4 + 1) * 4, 0:30], in_=pv[:, i8 * 120:(i8 + 1) * 120].rearrange("p (s t) -> p s t", s=4),
                                    func=AF.Identity, bias=0.0, scale=1.0), writes=[PS(b), "gs"])
                        for i8 in range(8):
                            S.op("dve", lambda e, i8=i8: e.tensor_copy(out=gs[:, i8, :, 30:38], in_=gluT[:, i8, 384:512].rearrange("p (s t) -> p s t", t=8)),
                                 reads=["gluT"], writes=["gs"])
                    bS, bQ = 6, 7
                    pend5 = []
                    for i8 in range(8):
                        diag = diag2[i8 % 2]
                        dres = ("diag", i8 % 2)
                        S.op("dve", lambda e, i8=i8, diag=diag: e.tensor_tensor(out=diag[:], in0=identf[:].unsqueeze(1).to_broadcast([128, 31, 128]),
                                                                     in1=wdwT[:, i8, :].unsqueeze(2).to_broadcast([128, 31, 128]), op=ALU.mult),
                             reads=["identf", "wdwT"], writes=[dres])
                        drain_ada(1)
                        b = nb()
                        for j in range(31):
                            S.op("pe", lambda e, j=j, i8=i8, b=b, diag=diag: e.matmul(ps[b][:, 0:npc], lhsT=diag[:, j, :], rhs=gluT[:, i8, 98 + j:98 + j + npc],
                                                                      start=(j == 0), stop=(j == 30)),
                                 reads=[dres, "gluT"], writes=[PS(b)], signal=(j == 30 and not has_s))
                        if has_s:
                            for j in range(31):
                                S.op("pe", lambda e, j=j, i8=i8, b=b, diag=diag: e.matmul(ps[b][:, npc:npc + 128], lhsT=diag[:, j, :], rhs=gs[:, i8, :, j:j + 8],
                                                                          start=False, stop=(j == 30), skip_group_check=True),
                                     reads=[dres, "gs"], writes=[PS(b)], signal=(j == 30))
                        S.op("act", lambda e, i8=i8, b=b: e.activation(out=ycv[:, i8, :], in_=ps[b][:, 0:NC_], func=AF.Identity, bias=cvT[:, i8, 0:1], scale=1.0),
                             reads=["cvT"], writes=[PS(b), ("ycv", i8)])
                        i = tmpi()
                        S.op("act", lambda e, i8=i8, i=i: e.activation(out=tA[i][:, 0:NC_], in_=ycv[:, i8, :], func=AF.Square),
                             reads=[("ycv", i8)], writes=[("tA", i)])

                        def stats_mm(i8=i8, i=i):
                            S.op("pe", lambda e: e.matmul(ps[bS][:, 0:NC_], lhsT=onesf[:], rhs=ycv[:, i8, :], start=(i8 == 0), stop=(i8 == 7)),
                                 reads=["onesf", ("ycv", i8)], writes=[PS(bS)], signal=(i8 == 7))
                            S.op("pe", lambda e: e.matmul(ps[bQ][:, 0:NC_], lhsT=onesf[:], rhs=tA[i][:, 0:NC_], start=(i8 == 0), stop=(i8 == 7)),
                                 reads=["onesf", ("tA", i)], writes=[PS(bQ)], signal=True)
                        if pend5:
                            pend5.pop(0)()
                        pend5.append(stats_mm)
                    while pend5:
                        pend5.pop(0)()
                    S.op("act", lambda e: e.activation(out=cm[:], in_=ps[bS][:, 0:NC_], func=AF.Identity, bias=0.0, scale=1.0 / 1024),
                         writes=[PS(bS), "cm"])
                    S.op("dve", lambda e: e.tensor_tensor(out=cr[:], in0=cm[:], in1=cm[:], op=ALU.mult), reads=["cm"], writes=["cr"])
                    S.op("dve", lambda e: e.scalar_tensor_tensor(out=cr[:], in0=ps[bQ][:, 0:NC_], scalar=1.0 / 1024, in1=cr[:],
                                                                 op0=ALU.mult, op1=ALU.subtract), reads=[], writes=[PS(bQ), "cr"])
                    S.op("act", lambda e: e.activation(out=cr[:], in_=cr[:], func=AF.Sqrt, bias=eps_t[:, 0:1], scale=1.0),
                         reads=["eps_t"], writes=["cr"])
                    S.op("dve", lambda e: e.reciprocal(out=cr[:], in_=cr[:]), reads=[], writes=["cr"])
                    for i8 in range(8):
                        S.op("dve", lambda e, i8=i8: e.tensor_tensor(out=ycv[:, i8, :], in0=ycv[:, i8, :], in1=cm[:], op=ALU.subtract),
                             reads=["cm"], writes=[("ycv", i8)])
                        S.op("dve", lambda e, i8=i8: e.tensor_tensor(out=ycv[:, i8, :], in0=ycv[:, i8, :], in1=cr[:], op=ALU.mult),
                             reads=["cr"], writes=[("ycv", i8)])
                        S.op("act", lambda e, i8=i8: e.activation(out=convT[:, i8, :], in_=ycv[:, i8, :], func=AF.Silu,
                                                                  bias=cvT[:, i8, 2:3], scale=cvT[:, i8, 1:2]),
                             reads=[("ycv", i8), "cvT"], writes=["convT"])
                stage(f"s5_{pi}", **{f"convT{pi}": (convT[:].rearrange("p a b -> p (a b)"), [128, 3072], BF16, "convT")})
                if has_s:
                    sO = ExitStack()
                    osb = sO.enter_context(nc.sbuf_tensor("osb", [128, 512], F32))
                    osc = sO.enter_context(nc.sbuf_tensor("osc", [128, 1024], F32))
                    osp = sO.enter_context(nc.sbuf_tensor("osp", [32, 1024], F32))
                    S.fence(["osb", "osc", "osp"])
                    S.dma("sp", "st", kwin_s.rearrange("s p c -> s (p c)")[:, 0:120 * 256], cachek.rearrange("s p c -> s (p c)")[:, 8 * 256:128 * 256])
                    S.dma("sp", "st", vwin_s.rearrange("s p c -> s (p c)")[:, 0:120 * 256], cachev.rearrange("s p c -> s (p c)")[:, 8 * 256:128 * 256])
                    S.dma("sp", "st", conv_s.rearrange("s t c -> s (t c)")[:, 0:22 * 1024], stconv.rearrange("s t c -> s (t c)")[:, 8 * 1024:30 * 1024])
                    for which in range(2):
                        b = nb()
                        for g in range(4):
                            S.op("pe", lambda e, g=g, b=b, which=which: e.transpose(out=ps[b][:, g * 64:(g + 1) * 64],
                                                                                    in_=kTf[0:64, g, which * 128:(which + 1) * 128],
                                                                                    identity=identf[0:64, 0:64]),
                                 reads=["kTf", "identf"], writes=[PS(b)], signal=(g == 3))
                        S.op("dve", lambda e, b=b, which=which: e.tensor_copy(out=osb[:, which * 256:(which + 1) * 256], in_=ps[b][:, 0:256]),
                             writes=[PS(b), "osb"])
                    S.dma("sp", "st", kwin_p, osb[:, 0:256], reads=["osb"])
                    S.dma("sp", "st", vwin_p, vf[:, 2, :], reads=["vf"])
                    for s in range(16):
                        S.dma("sp", "st", kwin_s[s, 120:128, :], osb[s * 8:(s + 1) * 8, 256:512], reads=["osb"])
                        S.dma("sp", "st", vwin_s[s, 120:128, :], vf[s * 8:(s + 1) * 8, 3, :], reads=["vf"])
                    for half in range(2):
                        b = nb()
                        for i4 in range(4):
                            i8 = half * 4 + i4
                            S.op("pe", lambda e, i8=i8, i4=i4, b=b: e.transpose(out=ps[b][:, i4 * 128:(i4 + 1) * 128], in_=gluTf[:, i8, 128:256],
                                                                               identity=identf[:]),
                                 reads=["gluTf", "identf"], writes=[PS(b)], signal=(i4 == 3))
                        S.op("dve", lambda e, half=half, b=b: e.tensor_copy(out=osc[:, half * 512:(half + 1) * 512], in_=ps[b][:]),
                             writes=[PS(b), "osc"])
                        b2 = nb()
                        for i4 in range(4):
                            i8 = half * 4 + i4
                            S.op("pe", lambda e, i8=i8, i4=i4, b2=b2: e.transpose(out=ps[b2][0:32, i4 * 128:(i4 + 1) * 128], in_=gluTf[:, i8, 96:128],
                                                                                 identity=identf[:]),
                                 reads=["gluTf", "identf"], writes=[PS(b2)], signal=(i4 == 3))
                        S.op("dve", lambda e, half=half, b2=b2: e.tensor_copy(out=osp[:, half * 512:(half + 1) * 512], in_=ps[b2][0:32, :]),
                             writes=[PS(b2), "osp"])
                    S.dma("sp", "st", conv_p, osp[2:32, :], reads=["osp"])
                    for s in range(16):
                        S.dma("sp", "st", conv_s[s, 22:30, :], osc[s * 8:(s + 1) * 8, :], reads=["osc"])
                    sO.close()

                def s6_unit(j):
                    ri2 = wload(w2[j][:, 4096:6144], 2048)
                    rv2 = ring[ri2][:, 0:2048].rearrange("p (k c) -> p k c", c=128)
                    bC, bD = nb(), nb()
                    for kc in range(8):
                        S.op("pe", lambda e, kc=kc: e.matmul(ps[bC][:, 0:NC_], lhsT=rv2[:, kc, :], rhs=attnT[:, kc, :], start=(kc == 0), stop=(kc == 7)),
                             reads=[("ring", ri2), "attnT"], writes=[PS(bC)], signal=(kc == 7))
                    for kc in range(8):
                        S.op("pe", lambda e, kc=kc: e.matmul(ps[bD][:, 0:NC_], lhsT=rv2[:, 8 + kc, :], rhs=convT[:, kc, :], start=(kc == 0), stop=(kc == 7)),
                             reads=[("ring", ri2), "convT"], writes=[PS(bD)], signal=(kc == 7))
                    i = tmpi()
                    S.op("dve", lambda e: e.tensor_tensor(out=tA[i][:, 0:NC_], in0=ps[bC][:, 0:NC_], in1=sgT[:, 0, j, :], op=ALU.mult),
                         reads=["sgT"], writes=[PS(bC), ("tA", i)])
                    S.op("dve", lambda e: e.tensor_tensor(out=tB[i][:, 0:NC_], in0=ps[bD][:, 0:NC_], in1=sgT[:, 1, j, :], op=ALU.mult),
                         reads=["sgT"], writes=[PS(bD), ("tB", i)])
                    S.op("dve", lambda e: e.tensor_tensor(out=mergedT[:, j, :], in0=tA[i][:, 0:NC_], in1=tB[i][:, 0:NC_], op=ALU.add),
                         reads=[("tA", i), ("tB", i)], writes=["mergedT"])

                drain_gates(16)
                for j in range(16):
                    s6_unit(j)
                    drain_ada(1)

            stage(f"s6_{pi}", **{f"mergedT{pi}": (mergedT[:].rearrange("p a b -> p (a b)"), [128, 6144], BF16, "mergedT")})
            with ExitStack() as s2b:
                lnG = s2b.enter_context(nc.sbuf_tensor(f"lnG{pi}", [128, 2, D], F32))
                hT = s2b.enter_context(nc.sbuf_tensor(f"hT{pi}", [128, 44, 384], BF16))
                XR = s2b.enter_context(nc.sbuf_tensor(f"XR{pi}", [128, 3, D], F32))
                S.fence(["lnG", "hT", ("XR", 0), ("XR", 1), ("XR", 2)])
                for t in range(NT):
                    S.dma("sp", "ld", XR[:, t, :], xin[mains[t]], writes=[("XR", t)])

                def load_lnG(r0):
                    for k in range(2):
                        S.dma("sp", "ld", lnG[:, k, :], lngb[r0 + k:r0 + k + 1, :].broadcast_to([128, D]), writes=["lnG"])

                zt1 = s2b.enter_context(nc.sbuf_tensor(f"zt1_{pi}", [128, D], BF16))
                zt2 = s2b.enter_context(nc.sbuf_tensor(f"zt2_{pi}", [128, D], BF16))
                S.fence(["zt1", "zt2"])
                zts = [(zt, "zt"), (zt1, "zt1"), (zt2, "zt2")]

                def ln_affine_all():
                    for t in range(NT):
                        ln_stats(XR[:, t, :], ("XR", t), t)
                    for t in range(NT):
                        S.op("act", lambda e, t=t: e.activation(out=XR[:, t, :], in_=XR[:, t, :], func=AF.Identity,
                                                                bias=nmr[:, t, 0:1], scale=rstd[:, t, 0:1]),
                             reads=[("nmr", t), ("rstd", t)], writes=[("XR", t)])
                    for t in range(NT):
                        S.op("dve", lambda e, t=t: e.tensor_tensor(out=XR[:, t, :], in0=XR[:, t, :], in1=lnG[:, 0, :], op=ALU.mult),
                             reads=["lnG"], writes=[("XR", t)])
                        S.op("dve", lambda e, t=t: e.tensor_tensor(out=XR[:, t, :], in0=XR[:, t, :], in1=lnG[:, 1, :], op=ALU.add),
                             reads=["lnG"], writes=[("XR", t)])

                def s7_blk(blk):
                    ri = wload(w3[blk], 4096)
                    rv = ring[ri][:, 0:4096].rearrange("p (k c) -> p k c", c=256)
                    for jj in range(2):
                        j = blk * 2 + jj
                        b = nb()
                        for kc in range(16):
                            S.op("pe", lambda e, kc=kc, jj=jj, b=b: e.matmul(ps[b][:, 0:NC_], lhsT=rv[:, kc, jj * 128:(jj + 1) * 128], rhs=mergedT[:, kc, :],
                                                                        start=(kc == 0), stop=(kc == 15)),
                                 reads=[("ring", ri), "mergedT"], writes=[PS(b)], signal=(kc == 15))
                        i = tmpi()
                        gate_evac(b, tB[i], ("tB", i), 2, j, NC_, has_s)
                        if pend7:
                            pend7.pop(0)()
                        pend7.append(lambda i=i, j=j: resid_add(XR, tB[i], ("tB", i), j, NT))

                pend7 = []
                for blk in range(8):
                    s7_blk(blk)
                while pend7:
                    pend7.pop(0)()
                stage(f"s7a_{pi}", **{f"r1_{pi}": (XR[:, 0, :], [128, 2048], F32, ("XR", 0))})
                load_lnG(0)
                for t in range(NT):
                    ln_stats(XR[:, t, :], ("XR", t), t)
                for t in range(NT):
                    S.op("act", lambda e, t=t: e.activation(out=XR[:, t, :], in_=XR[:, t, :], func=AF.Identity,
                                                            bias=nmr[:, t, 0:1], scale=rstd[:, t, 0:1]),
                         reads=[("nmr", t), ("rstd", t)], writes=[("XR", t)])
                for t in range(NT):
                    S.op("dve", lambda e, t=t: e.tensor_tensor(out=XR[:, t, :], in0=XR[:, t, :], in1=lnG[:, 0, :], op=ALU.mult),
                         reads=["lnG"], writes=[("XR", t)])
                    S.op("dve", lambda e, t=t: e.tensor_tensor(out=XR[:, t, :], in0=XR[:, t, :], in1=lnG[:, 1, :], op=ALU.add),
                         reads=["lnG"], writes=[("XR", t)])
                    ln_stats(XR[:, t, :], ("XR", t), t)
                    ln_z(XR[:, t, :], ("XR", t), t, zts[t][0], zts[t][1])
                    ln_T(zts[t][0], zts[t][1], 128 * (t + 1), 3, 4, has_s and t == NT - 1)

                stage(f"s7_{pi}", **{f"x1_{pi}": (XR[:, 0, :], [128, 2048], F32, ("XR", 0)), f"u2T{pi}": (uT[:].rearrange("p a b -> p (a b)"), [128, 8192], BF16, "uT"), f"lnG{pi}": (lnG[:].rearrange("p a b -> p (a b)"), [128, 4096], F32, "lnG")})
                def s8_unit(j):
                    ri = wload(w4[j], 4096)
                    rv = ring[ri][:, 0:4096].rearrange("p (k c) -> p k c", c=128)
                    bG, bU = nb(), nb()
                    for kc in range(16):
                        S.op("pe", lambda e, kc=kc: e.matmul(ps[bG][:, 0:NC_], lhsT=rv[:, kc, :], rhs=uT[:, kc, 128:512], start=(kc == 0), stop=(kc == 15)),
                             reads=[("ring", ri), "uT"], writes=[PS(bG)], signal=(kc == 15))
                    for kc in range(16):
                        S.op("pe", lambda e, kc=kc: e.matmul(ps[bU][:, 0:NC_], lhsT=rv[:, 16 + kc, :], rhs=uT[:, kc, 128:512], start=(kc == 0), stop=(kc == 15)),
                             reads=[("ring", ri), "uT"], writes=[PS(bU)], signal=(kc == 15))
                    i = tmpi()
                    S.op("act", lambda e: e.activation(out=tA[i][:, 0:NC_], in_=ps[bG][:, 0:NC_], func=AF.Silu), writes=[PS(bG), ("tA", i)])
                    S.op("dve", lambda e: e.tensor_tensor(out=hT[:, j, :], in0=ps[bU][:, 0:NC_], in1=tA[i][:, 0:NC_], op=ALU.mult),
                         reads=[("tA", i)], writes=[PS(bU), "hT"])

                for j in range(44):
                    s8_unit(j)

                stage(f"s8_{pi}", **{f"hT{pi}": (hT[:].rearrange("p a b -> p (a b)"), [128, 44 * 384], BF16, "hT")})
                def s9_unit(j):
                    ria = wload(w5[j][:, 0:2816], 2816)
                    rib = wload(w5[j][:, 2816:5632], 2816)
                    rva = ring[ria][:, 0:2816].rearrange("p (k c) -> p k c", c=128)
                    rvb = ring[rib][:, 0:2816].rearrange("p (k c) -> p k c", c=128)
                    b = nb()
                    for kc in range(44):
                        rvx, rix, kk = (rva, ria, kc) if kc < 22 else (rvb, rib, kc - 22)
                        S.op("pe", lambda e, kc=kc, rvx=rvx, kk=kk: e.matmul(ps[b][:, 0:NC_], lhsT=rvx[:, kk, :], rhs=hT[:, kc, :], start=(kc == 0), stop=(kc == 43)),
                             reads=[("ring", rix), "hT"], writes=[PS(b)], signal=(kc == 43))
                    i = tmpi()
                    gate_evac(b, tB[i], ("tB", i), 5, j, NC_, has_s)
                    if pend9:
                        pend9.pop(0)()
                    pend9.append(lambda i=i, j=j: resid_add(XR, tB[i], ("tB", i), j, NT))

                pend9 = []
                for j in range(16):
                    s9_unit(j)
                    if s2_next and j % 2 == 1:
                        s2_next.pop(0)()
                while pend9:
                    pend9.pop(0)()
                while s2_next:
                    s2_next.pop(0)()
                stage(f"s9_{pi}")
                load_lnG(2)
                ln_affine_all()
                for t in range(NT):
                    S.dma("sp", "st", y_main[mains[t] - 1], XR[:, t, :], reads=[("XR", t)])

        passes = [(0, [1, 2, 3], False), (3, [4, 5, 6], False), (6, [7, 8, 9], True)]
        for f in s2_tiles(*passes[0]):
            f()
        for pi, (hidx, mains, has_s) in enumerate(passes):
            nxt = s2_tiles(*passes[pi + 1]) if pi + 1 < len(passes) else []
            run_pass(pi, hidx, mains, has_s, nxt)
            drain_ada(48)

        S.finish("sp")
        S.emit()
    return nc


def _tile_w(W, ncols):
    K, N = W.shape
    return np.ascontiguousarray(W.reshape(K // 128, 128, N // ncols, ncols).transpose(2, 1, 0, 3).reshape(N // ncols, 128, -1))


def _tile_units(units):
    parts = []
    for W in units:
        K = W.shape[0]
        parts.append(W.reshape(K // 128, 128, 128).transpose(1, 0, 2).reshape(128, -1))
    return np.concatenate(parts, axis=1)


_CACHE = {}


def kernel(x_prompt, x_sample, c_prompt, c_sample, cache_k_win, cache_v_win, state_conv,
           w_ada, b_ada, w_in, attn_sinks, w_dw, b_dw, cn_gain, cn_bias, w_br_attn, w_br_conv, w_out,
           ln1_gain, ln1_bias, w_ffn_gate, w_ffn_up, w_ffn_down, ln2_gain, ln2_bias):
    f = lambda a: np.asarray(a, dtype=np.float32)
    x_prompt, x_sample, c_prompt, c_sample = f(x_prompt), f(x_sample), f(c_prompt), f(c_sample)
    cache_k_win, cache_v_win, state_conv = f(cache_k_win), f(cache_v_win), f(state_conv)
    w_in0 = f(w_in)[0]
    partner = np.concatenate([np.arange(32, 64), np.arange(0, 32)])
    qcols = np.arange(1024).reshape(16, 64)
    qrh = qcols[:, partner].reshape(-1)
    kcols = 1024 + np.arange(256).reshape(4, 64)
    krh = kcols[:, partner]
    blocks1 = [_tile_w(w_in0[:, 1024:1280], 256)[0], _tile_w(w_in0[:, 1280:1536], 256)[0]]
    for i in range(4):
        blocks1.append(_tile_w(w_in0[:, i * 256:(i + 1) * 256], 256)[0])
    for i in range(8):
        cols = np.concatenate([1536 + np.arange(i * 128, (i + 1) * 128), 2560 + np.arange(i * 128, (i + 1) * 128)])
        blocks1.append(_tile_w(w_in0[:, cols], 256)[0])
    w1 = np.stack(blocks1)
    wba, wbc = f(w_br_attn)[0], f(w_br_conv)[0]
    w2 = np.stack([_tile_units([w_in0[:, 3584 + j * 128:3584 + (j + 1) * 128], w_in0[:, 5632 + j * 128:5632 + (j + 1) * 128],
                                wba[:, j * 128:(j + 1) * 128], wbc[:, j * 128:(j + 1) * 128]]) for j in range(16)])
    w3 = _tile_w(f(w_out)[0], 256)
    wg, wu = f(w_ffn_gate)[0], f(w_ffn_up)[0]
    w4 = np.stack([_tile_units([wg[:, j * 128:(j + 1) * 128], wu[:, j * 128:(j + 1) * 128]]) for j in range(44)])
    w5 = _tile_w(f(w_ffn_down)[0], 128)
    wa = _tile_w(f(w_ada)[0], 256)
    bada = np.ascontiguousarray(f(b_ada)[0].reshape(96, 128))
    sk = f(attn_sinks)[0]
    sinks8 = np.ascontiguousarray(np.repeat(sk.reshape(8, 2), 64, axis=1))
    wdw = f(w_dw)[0]
    cvs = np.stack([f(b_dw)[0], f(cn_gain)[0], f(cn_bias)[0]])
    lngb = np.stack([f(ln1_gain)[0], f(ln1_bias)[0], f(ln2_gain)[0], f(ln2_bias)[0]])
    identd = np.eye(128, dtype=np.float32)
    dst = np.arange(128)
    permd = np.zeros((5, 128, 128), np.float32)
    permd[0, (dst // 64) * 64 + partner[dst % 64], dst] = 1.0
    for h in range(2):
        permd[1 + h, h * 64 + dst % 64, dst] = 1.0
        permd[3 + h, h * 64 + partner[dst % 64], dst] = 1.0
    kk = np.arange(128)[:, None]
    qq = np.arange(128)[None, :]
    m_prev = np.where(kk > qq, 0.0, NEG).astype(np.float32)
    m_own = np.where(kk <= qq, 0.0, NEG).astype(np.float32)
    qs, qt = np.arange(128)[None, :] // 8, np.arange(128)[None, :] % 8
    msc = np.stack([np.where((qs == s) & (kk > qt), 0.0, NEG) for s in range(16)]).astype(np.float32)
    ks_, kt_ = np.arange(128)[:, None] // 8, np.arange(128)[:, None] % 8
    msn = np.where((ks_ == qs) & (kt_ <= qt), 0.0, NEG).astype(np.float32)
    inv = (10000.0 ** (-np.arange(32, dtype=np.float64) / 32.0)).astype(np.float32)
    prow = np.arange(128)
    sgn = np.where((prow % 64) < 32, -1.0, 1.0).astype(np.float32)[:, None]

    def rope_tabs(pos):
        ang = (pos.astype(np.float32)[None, :] * inv[prow % 32][:, None]).astype(np.float32).astype(np.float64)
        return np.cos(ang).astype(np.float32), (np.sin(ang) * sgn).astype(np.float32)

    in_maps = []
    for c in range(NCORE):
        halo = np.zeros((128, D), np.float32) if c == 0 else x_prompt[0, c * 1024 - 128:c * 1024]
        xin = np.concatenate([halo[None], x_prompt[0, c * 1024:(c + 1) * 1024].reshape(8, 128, D),
                              x_sample[c * 16:(c + 1) * 16].reshape(1, 128, D)], axis=0)
        cvec = np.concatenate([c_prompt, c_sample[c * 16:(c + 1) * 16]], axis=0)
        rc_, rs_ = [], []
        for p in range(3):
            base = c * 1024 - 128 + p * 384
            pos = base + np.arange(512)
            if p == 2:
                pos = pos.copy()
                pos[384:] = 8192 + (np.arange(128) % 8)
            a, b = rope_tabs(pos)
            rc_.append(a)
            rs_.append(b)
        flags = np.ones((128, 3), np.float32)
        if c == 0:
            flags[:, 0] = 0.0
        m0 = np.full((128, 128), NEG, np.float32) if c == 0 else m_prev
        in_maps.append(dict(
            xin=np.ascontiguousarray(xin), cvec=np.ascontiguousarray(cvec),
            cachek=np.ascontiguousarray(cache_k_win[0, c * 16:(c + 1) * 16].reshape(16, 128, 256)),
            cachev=np.ascontiguousarray(cache_v_win[0, c * 16:(c + 1) * 16].reshape(16, 128, 256)),
            stconv=np.ascontiguousarray(state_conv[0, c * 16:(c + 1) * 16]),
            w1=w1, w2=w2, w3=w3, w4=w4, w5=w5, wa=wa, bada=bada, sinks8=sinks8, wdw=wdw, cvs=cvs, lngb=lngb,
            ropec=np.stack(rc_), ropes=np.stack(rs_), flags=flags, identd=identd,
            masks=np.stack([m_prev, m_own, m0]), msc=msc, msn=msn, permd=permd))
    if "nc" not in _CACHE:
        _CACHE["nc"] = build_nc()
    res = run_bass_kernel_spmd(_CACHE["nc"], in_maps, core_ids=list(range(NCORE)))
    R = res.results
    y_prompt = np.concatenate([R[c]["y_main"][0:8].reshape(1024, D) for c in range(NCORE)], axis=0)[None]
    y_sample = np.concatenate([R[c]["y_main"][8].reshape(16, 8, D) for c in range(NCORE)], axis=0)
    k_win_prompt = R[7]["kwin_p"].reshape(1, 1, 128, 4, 64)
    v_win_prompt = R[7]["vwin_p"].reshape(1, 1, 128, 4, 64)
    conv_prompt = R[7]["conv_p"].reshape(1, 1, 30, 1024)
    k_win_sample = np.concatenate([R[c]["kwin_s"] for c in range(NCORE)], axis=0).reshape(1, 128, 128, 4, 64)
    v_win_sample = np.concatenate([R[c]["vwin_s"] for c in range(NCORE)], axis=0).reshape(1, 128, 128, 4, 64)
    conv_sample = np.concatenate([R[c]["conv_s"] for c in range(NCORE)], axis=0).reshape(1, 128, 30, 1024)
    return (y_prompt.astype(np.float32), y_sample.astype(np.float32), k_win_prompt.astype(np.float32),
            v_win_prompt.astype(np.float32), conv_prompt.astype(np.float32), k_win_sample.astype(np.float32),
            v_win_sample.astype(np.float32), conv_sample.astype(np.float32))
```

```python
import numpy as np
from contextlib import ExitStack
import concourse.bass as bass
import concourse.mybir as mybir
from concourse.bass_utils import run_bass_kernel_spmd

F32 = mybir.dt.float32
BF16 = mybir.dt.bfloat16
AF = mybir.ActivationFunctionType
ALU = mybir.AluOpType

D = 2048
NCORE = 8
ALPHA = 2.0 ** 0.25
EPS = 1e-5
NEG = -30000.0
DFF = 5632
RING = 4096


class Sync:
    ENG = ("pe", "act", "dve", "pool", "sp")

    def __init__(self, nc, stack):
        self.nc = nc
        self.stack = stack
        self.sem = {}
        self.cnt = {}
        self.prog = {e: [] for e in self.ENG}
        self.waited = {e: {} for e in self.ENG}
        self.res = {}
        self.stopped = False
        self.dpool = {}
        self.dctr = {}
        self.fences = {}
        for e in self.ENG:
            self._mksem(e)

    def _mksem(self, name):
        self.sem[name] = self.stack.enter_context(self.nc.semaphore("s_" + name))
        self.cnt[name] = 0

    def _deps(self, eng, reads, writes):
        need = {}

        def add(sv):
            if sv is None:
                return
            s, v = sv
            if need.get(s, 0) < v:
                need[s] = v

        for r in reads:
            st = self.res.get(r)
            if st:
                add(st["w"])
        for w in writes:
            st = self.res.get(w)
            if st:
                add(st["w"])
                for sv in st["r"]:
                    add(sv)
            fc = self.fences.pop(w, None)
            if fc:
                for sv in fc.items():
                    add(sv)
        out = []
        for s, v in need.items():
            if s == "pe" and eng == "pe":
                continue
            if self.waited[eng].get(s, 0) >= v:
                continue
            self.waited[eng][s] = v
            out.append((s, v))
        return out

    def _record(self, reads, writes, sv):
        for r in reads:
            st = self.res.setdefault(r, {"w": None, "r": []})
            st["r"].append(sv)
            if len(st["r"]) > 64:
                mx = {}
                for s, v in st["r"]:
                    mx[s] = max(mx.get(s, 0), v)
                st["r"] = list(mx.items())
        for w in writes:
            self.res[w] = {"w": sv, "r": []}

    def snapshot(self):
        return {k: v for k, v in self.cnt.items() if v > 0}

    def fence(self, names, snap=None):
        if snap is None:
            snap = {k: v for k, v in self.cnt.items() if v > 0}
        for n in names:
            self.fences[n] = dict(snap)
            self.res.pop(n, None)

    def op(self, eng, fn, reads=(), writes=(), signal=True):
        if self.stopped:
            return
        waits = self._deps(eng, reads, writes)
        if signal:
            self.cnt[eng] += 1
            sv = (eng, self.cnt[eng])
        else:
            sv = (eng, self.cnt[eng] + 1)
        sem = self.sem[eng]
        sems = self.sem

        def run(e, waits=waits, fn=fn, signal=signal, sem=sem):
            for s, v in waits:
                e.wait_ge(sems[s], v)
            ins = fn(e)
            if signal:
                ins.then_inc(sem, 1)

        self.prog[eng].append(run)
        self._record(reads, writes, sv)

    def dma(self, eng, stream, out, in_, reads=(), writes=()):
        if self.stopped:
            return
        if eng not in self.dpool:
            n = 16 if eng == "sp" else 8
            self.dpool[eng] = [f"d_{eng}{i}" for i in range(n)]
            self.dctr[eng] = 0
            for nm in self.dpool[eng]:
                self._mksem(nm)
        pool = self.dpool[eng]
        stream = pool[self.dctr[eng] % len(pool)]
        self.dctr[eng] += 1
        waits = self._deps(eng, reads, writes)
        prev = self.cnt[stream]
        if prev > 0 and self.waited[eng].get(stream, 0) < prev:
            self.waited[eng][stream] = prev
            waits.append((stream, prev))
        self.cnt[stream] += 16
        sv = (stream, self.cnt[stream])
        sem = self.sem[stream]
        sems = self.sem

        def run(e, waits=waits, out=out, in_=in_, sem=sem):
            for s, v in waits:
                e.wait_ge(sems[s], v)
            e.dma_start(out=out, in_=in_).then_inc(sem, 16)

        self.prog[eng].append(run)
        self._record(reads, writes, sv)

    def finish(self, final_eng="sp"):
        waits = []
        for s, c in self.cnt.items():
            if c > 0 and s != final_eng:
                waits.append((s, c))
        sems = self.sem

        def run(e, waits=waits):
            for s, v in waits:
                e.wait_ge(sems[s], v)

        self.prog[final_eng].append(run)

    def emit(self):
        nc = self.nc
        prog = self.prog
        with nc.Block() as block:
            @block.tensor
            def _(e):
                for f in prog["pe"]:
                    f(e)

            @block.scalar
            def _(e):
                for f in prog["act"]:
                    f(e)

            @block.vector
            def _(e):
                for f in prog["dve"]:
                    f(e)

            @block.gpsimd
            def _(e):
                for f in prog["pool"]:
                    f(e)

            @block.sync
            def _(e):
                for f in prog["sp"]:
                    f(e)


class _Stop(Exception):
    pass


def build_nc(stop=None, dumps=()):
    nc = bass.Bass("TRN2", target_bir_lowering=False)

    def din(name, shape):
        return nc.dram_tensor(name, list(shape), F32, kind="ExternalInput").ap()

    def dout(name, shape):
        return nc.dram_tensor(name, list(shape), F32, kind="ExternalOutput").ap()

    xin = din("xin", [10, 128, D])
    cvec = din("cvec", [17, D])
    cachek = din("cachek", [16, 128, 256])
    cachev = din("cachev", [16, 128, 256])
    stconv = din("stconv", [16, 30, 1024])
    w1 = din("w1", [14, 128, 4096])
    permd = din("permd", [5, 128, 128])
    w2 = din("w2", [16, 128, 6144])
    w3 = din("w3", [8, 128, 4096])
    w4 = din("w4", [44, 128, 4096])
    w5 = din("w5", [16, 128, 5632])
    wa = din("wa", [48, 128, 4096])
    bada = din("bada", [96, 128])
    sinks8 = din("sinks8", [8, 128])
    wdw = din("wdw", [31, 1024])
    cvs = din("cvs", [3, 1024])
    lngb = din("lngb", [4, D])
    ropec = din("ropec", [3, 128, 512])
    ropes = din("ropes", [3, 128, 512])
    flags = din("flags", [128, 3])
    identd = din("identd", [128, 128])
    masks = din("masks", [3, 128, 128])
    msc = din("msc", [16, 128, 128])
    msn = din("msn", [128, 128])

    y_main = dout("y_main", [9, 128, D])
    kwin_p = dout("kwin_p", [128, 256])
    vwin_p = dout("vwin_p", [128, 256])
    conv_p = dout("conv_p", [30, 1024])
    kwin_s = dout("kwin_s", [16, 128, 256])
    vwin_s = dout("vwin_s", [16, 128, 256])
    conv_s = dout("conv_s", [16, 30, 1024])

    with ExitStack() as st:
        S = Sync(nc, st)
        T = lambda name, shape, dt=F32: st.enter_context(nc.sbuf_tensor(name, list(shape), dt))
        ps = [st.enter_context(nc.psum_tensor(f"ps{i}", [128, 512], F32)) for i in range(8)]
        bank_ctr = [0]

        def nb():
            b = bank_ctr[0] % 6
            bank_ctr[0] += 1
            return b

        def PS(b):
            return ("ps", b)

        def stage(name, **tiles):
            for k, (ap, shape, dt, rname) in tiles.items():
                if k in dumps:
                    d = nc.dram_tensor("dbg_" + k, list(shape), dt, kind="ExternalOutput").ap()
                    S.dma("sp", "st", d, ap, reads=[rname])
            if stop == name:
                S.stopped = True

        identf = T("identf", [128, 128])
        identb = T("identb", [128, 128], BF16)
        onesf = T("onesf", [128, 128])
        onesb = T("onesb", [128, 64], BF16)
        mk = T("mk", [128, 3, 128], BF16)
        mksc = T("mksc", [128, 16, 128], BF16)
        mksn = T("mksn", [128, 128], BF16)
        flg = T("flg", [128, 3])
        eps_t = T("eps_t", [128, 1])
        S.dma("sp", "ld", identf[:], identd, writes=["identf"])
        S.dma("pool", "wld", identb[:], identd, writes=["identb"])
        S.dma("sp", "ld", flg[:], flags, writes=["flg"])
        perm = T("perm", [128, 5, 128])
        S.dma("sp", "ld", perm[:], permd.rearrange("i p c -> p i c"), writes=["perm"])
        S.op("dve", lambda e: e.memset(onesf[:], 1.0), writes=["onesf"])
        S.op("dve", lambda e: e.memset(onesb[:], 1.0), writes=["onesb"])
        S.op("dve", lambda e: e.memset(eps_t[:], EPS), writes=["eps_t"])

        NRING = 4
        ring = [T(f"ring{i}", [128, RING], BF16) for i in range(NRING)]
        ring_ctr = [0]

        def wload(src, nel):
            i = ring_ctr[0] % NRING
            ring_ctr[0] += 1
            S.dma("pool", "wld", ring[i][:, 0:nel], src, writes=[("ring", i)])
            return i

        xs = T("xs", [128, D])
        zt = T("zt", [128, D], BF16)
        uT = T("uT", [128, 16, 512], BF16)
        mergedT = T("mergedT", [128, 16, 384], BF16)
        tA = [T(f"tA{i}", [128, 512]) for i in range(2)]
        tB = [T(f"tB{i}", [128, 512]) for i in range(2)]
        pTb = [T(f"pTb{i}", [128, 512], BF16) for i in range(2)]
        rc = T("rc", [128, 256])
        stats = T("stats", [128, 3, 4, 6])
        mv = T("mv", [128, 3, 2])
        rstd = T("rstd", [128, 3, 1])
        nmr = T("nmr", [128, 3, 1])
        csT = T("csT", [128, 16, 17], BF16)
        modT = T("modT", [128, 6, 16, 17])
        esT = T("esT", [128, 8])
        bT1 = T("bT1", [128, 96])
        tmp_ctr = [0]

        def tmpi():
            i = tmp_ctr[0] % 2
            tmp_ctr[0] += 1
            return i

        def small_T(src_ap, nrows, name, ncol_chunks, tmp, tres):
            dst = T(name, [128, ncol_chunks, nrows])
            S.dma("sp", "ld", tmp, src_ap, writes=[tres])
            for c in range(ncol_chunks):
                b = nb()
                S.op("pe", lambda e, b=b, c=c: e.transpose(out=ps[b][:, 0:nrows], in_=tmp[:, c * 128:(c + 1) * 128],
                                                           identity=identf[0:nrows, 0:nrows]),
                     reads=[tres, "identf"], writes=[PS(b)])
                S.op("dve", lambda e, b=b, c=c: e.tensor_copy(out=dst[:, c, :], in_=ps[b][:, 0:nrows]),
                     reads=[], writes=[PS(b), name])
            return dst

        bT = small_T(bada, 96, "bT", 1, xs[0:96, 0:128], "xs")
        wdwT = small_T(wdw, 31, "wdwT", 8, xs[0:31, 0:1024], "xs")
        cvT = small_T(cvs, 3, "cvT", 8, xs[0:3, 0:1024], "xs")
        skT = small_T(sinks8, 8, "skT", 1, xs[0:8, 0:128], "xs")
        S.op("act", lambda e: e.activation(out=esT[:], in_=skT[:, 0, :], func=AF.Exp), reads=["skT"], writes=["esT"])
        S.op("dve", lambda e: e.tensor_scalar(out=bT1[:], in0=bT[:, 0, :], scalar1=1.0, scalar2=None, op0=ALU.add),
             reads=["bT"], writes=["bT1"])

        cld = xs[0:17, :]
        S.dma("sp", "ld", cld, cvec, writes=["xs"])
        S.op("act", lambda e: e.activation(out=cld, in_=cld, func=AF.Silu), writes=["xs"])
        for kc in range(16):
            b = nb()
            S.op("pe", lambda e, b=b, kc=kc: e.transpose(out=ps[b][:, 0:17], in_=cld[:, kc * 128:(kc + 1) * 128],
                                                         identity=identf[0:17, 0:17]),
                 reads=["xs", "identf"], writes=[PS(b)])
            S.op("dve", lambda e, b=b, kc=kc: e.tensor_copy(out=csT[:, kc, :], in_=ps[b][:, 0:17]),
                 writes=[PS(b), "csT"])

        ada_pending = []

        def ada_blk(blk):
            ri = wload(wa[blk], 4096)
            rv = ring[ri][:, 0:4096].rearrange("p (k c) -> p k c", c=256)
            grp = blk // 8
            for jj in range(2):
                ch = (blk % 8) * 2 + jj
                b = nb()
                for kc in range(16):
                    S.op("pe", lambda e, b=b, kc=kc, jj=jj: e.matmul(ps[b][:, 0:17], lhsT=rv[:, kc, jj * 128:(jj + 1) * 128],
                                                                     rhs=csT[:, kc, :], start=(kc == 0), stop=(kc == 15)),
                         reads=[("ring", ri), "csT"], writes=[PS(b)], signal=(kc == 15))
                bsrc = bT1 if grp in (1, 4) else bT[:, 0, :]
                col = grp * 16 + ch
                S.op("act", lambda e, b=b, ch=ch, bsrc=bsrc, col=col: e.activation(
                    out=modT[:, grp, ch, :], in_=ps[b][:, 0:17], func=AF.Identity, bias=bsrc[:, col:col + 1], scale=1.0),
                    reads=["bT", "bT1"], writes=[PS(b), "modT"])

        try:
            stage("const", bT=(bT[:, 0, :], [128, 96], F32, "bT"), wdwT=(wdwT[:].rearrange("p a b -> p (a b)"), [128, 248], F32, "wdwT"),
                  esT=(esT[:], [128, 8], F32, "esT"), csT=(csT[:].rearrange("p a b -> p (a b)"), [128, 272], BF16, "csT"))
            for blk in range(16):
                ada_blk(blk)
            ada_pending.extend(range(16, 48))
            S.dma("pool", "wld", mk[:], masks.rearrange("i p c -> p i c"), writes=["mk"])
            for q in range(4):
                S.dma("pool", "wld", mksc[:, q * 4:(q + 1) * 4, :], msc[q * 4:(q + 1) * 4].rearrange("i p c -> p i c"), writes=["mksc"])
            S.dma("pool", "wld", mksn[:], msn, writes=["mksn"])
            stage("ada", modT=(modT[:].rearrange("p a b c -> p (a b c)"), [128, 6 * 16 * 17], F32, "modT"))
        except _Stop:
            S.finish("sp")
            S.emit()
            return nc

        def ln_stats(src, rname, ti=0):
            for c in range(4):
                S.op("dve", lambda e, c=c: e.bn_stats(out=stats[:, ti, c, :], in_=src[:, c * 512:(c + 1) * 512]),
                     reads=[rname], writes=[("stats", ti)])
            S.op("dve", lambda e: e.bn_aggr(out=mv[:, ti, :], in_=stats[:, ti, :, :]), reads=[("stats", ti)], writes=[("mv", ti)])
            S.op("act", lambda e: e.activation(out=rstd[:, ti, :], in_=mv[:, ti, 1:2], func=AF.Sqrt, bias=eps_t[:, 0:1], scale=1.0),
                 reads=[("mv", ti), "eps_t"], writes=[("rstd", ti)])
            S.op("dve", lambda e: e.reciprocal(out=rstd[:, ti, :], in_=rstd[:, ti, :]), reads=[("rstd", ti)], writes=[("rstd", ti)])
            S.op("dve", lambda e: e.tensor_scalar(out=nmr[:, ti, :], in0=mv[:, ti, 0:1], scalar1=rstd[:, ti, 0:1], scalar2=-1.0,
                                                  op0=ALU.mult, op1=ALU.mult), reads=[("mv", ti), ("rstd", ti)], writes=[("nmr", ti)])

        def ln_z(src, rname, ti, ztile, zname):
            S.op("act", lambda e: e.activation(out=ztile[:], in_=src, func=AF.Identity, bias=nmr[:, ti, 0:1], scale=rstd[:, ti, 0:1]),
                 reads=[rname, ("nmr", ti), ("rstd", ti)], writes=[zname])

        def ln_mod_T(src, rname, col0, gsh, gsc, sample):
            ln_stats(src, rname, 0)
            ln_z(src, rname, 0, zt, "zt")
            ln_T(zt, "zt", col0, gsh, gsc, sample)

        def ln_T(zt, zname, col0, gsh, gsc, sample):
            for half in range(2):
                b = nb()
                pv = ps[b][:].bitcast(BF16).rearrange("p (k c) -> p k c", c=128)[:, 0:8, :]
                for k8 in range(8):
                    kc = half * 8 + k8
                    S.op("pe", lambda e, pv=pv, k8=k8, kc=kc: e.transpose(out=pv[:, k8, :], in_=zt[:, kc * 128:(kc + 1) * 128],
                                                                         identity=identb[:]),
                         reads=[zname, "identb"], writes=[PS(b)], signal=(k8 == 7))
                for k8 in range(8):
                    kc = half * 8 + k8
                    if not sample:
                        S.op("act", lambda e, pv=pv, k8=k8, kc=kc: e.activation(
                            out=uT[:, kc, col0:col0 + 128], in_=pv[:, k8, :], func=AF.Identity,
                            bias=modT[:, gsh, kc, 0:1], scale=modT[:, gsc, kc, 0:1]),
                            reads=["modT"], writes=[PS(b), "uT"])
                    else:
                        i = tmpi()
                        t3 = tA[i][:, 0:128].rearrange("p (s t) -> p s t", t=8)
                        S.op("dve", lambda e, pv=pv, k8=k8, kc=kc, t3=t3: e.tensor_tensor(
                            out=t3, in0=pv[:, k8, :].rearrange("p (s t) -> p s t", t=8),
                            in1=modT[:, gsc, kc, 1:17].unsqueeze(2).to_broadcast([128, 16, 8]), op=ALU.mult),
                            reads=["modT"], writes=[PS(b), ("tA", i)])
                        S.op("dve", lambda e, kc=kc, t3=t3: e.tensor_tensor(
                            out=uT[:, kc, col0:col0 + 128].rearrange("p (s t) -> p s t", t=8), in0=t3,
                            in1=modT[:, gsh, kc, 1:17].unsqueeze(2).to_broadcast([128, 16, 8]), op=ALU.add),
                            reads=["modT", ("tA", i)], writes=["uT"])

        def gate_evac(b, dst, dname, grp, j, ncol, has_sample):
            npr = ncol - 128 if has_sample else ncol
            S.op("act", lambda e: e.activation(out=dst[:, 0:npr], in_=ps[b][:, 0:npr], func=AF.Identity, bias=0.0,
                                               scale=modT[:, grp, j, 0:1]),
                 reads=["modT"], writes=[PS(b), dname])
            if has_sample:
                S.op("dve", lambda e: e.tensor_tensor(
                    out=dst[:, npr:ncol].rearrange("p (s t) -> p s t", t=8),
                    in0=ps[b][:, npr:ncol].rearrange("p (s t) -> p s t", t=8),
                    in1=modT[:, grp, j, 1:17].unsqueeze(2).to_broadcast([128, 16, 8]), op=ALU.mult),
                    reads=["modT"], writes=[PS(b), dname])

        def resid_add(XR, src, sname, j, ntile):
            for t in range(ntile):
                b = nb()
                S.op("pe", lambda e, b=b, t=t: e.transpose(out=ps[b][:, 0:128], in_=src[:, t * 128:(t + 1) * 128], identity=identf[:]),
                     reads=[sname, "identf"], writes=[PS(b)])
                S.op("dve", lambda e, b=b, t=t: e.scalar_tensor_tensor(
                    out=XR[:, t, j * 128:(j + 1) * 128], in0=XR[:, t, j * 128:(j + 1) * 128], scalar=ALPHA,
                    in1=ps[b][:, 0:128], op0=ALU.mult, op1=ALU.add),
                    reads=[], writes=[PS(b), ("XR", t)])

        def s2_tiles(hidx, mains, has_s):
            out = []
            srcs = [hidx] + list(mains)
            for ct, xi in enumerate(srcs):
                def fa(ct=ct, xi=xi):
                    S.dma("sp", "ld", xs[:], xin[xi], writes=["xs"])
                    ln_stats(xs[:], "xs", 0)
                    ln_z(xs[:], "xs", 0, zt, "zt")

                def fb(ct=ct):
                    ln_T(zt, "zt", 128 * ct, 0, 1, has_s and ct == 3)
                out.append(fa)
                out.append(fb)
            return out

        def drain_ada(n):
            for _ in range(n):
                if ada_pending:
                    ada_blk(ada_pending.pop(0))

        R0 = nc.sbuf_bytes_remaining
        spans = {}
        late_spans = {}
        pre_s10_snap = {}

        def alloc(stack, key, name, shape, dt):
            rem0 = nc.sbuf_bytes_remaining
            t = stack.enter_context(nc.sbuf_tensor(name, list(shape), dt))
            spans[key] = (R0 - rem0, R0 - nc.sbuf_bytes_remaining)
            return t

        def early_snap(pi, keys):
            if pi == 0 or (pi - 1) not in pre_s10_snap:
                return None
            for k in keys:
                a0, a1 = spans[k]
                for (b0, b1) in late_spans[pi - 1]:
                    if a0 < b1 and b0 < a1:
                        return None
            return pre_s10_snap[pi - 1]

        def run_pass(pi, hidx, mains, has_s, s2_next):
            NT = 3
            NC_ = 128 * NT
            npr_t = NT - 1 if has_s else NT
            npc = npr_t * 128
            stage(f"s2_{pi}", **{f"uT{pi}": (uT[:].rearrange("p a b -> p (a b)"), [128, 8192], BF16, "uT")})
            with ExitStack() as s1:
                T1 = lambda name, shape, dt=F32: s1.enter_context(nc.sbuf_tensor(f"{name}{pi}", list(shape), dt))
                gluT = T1("gluT", [128, 8, 512], BF16)
                gluTf = T1("gluTf", [128, 8, 256])
                attnT = T1("attnT", [128, 8, 384], BF16)
                convT = T1("convT", [128, 8, 384], BF16)
                sgT = T1("sgT", [128, 2, 16, 384], BF16)
                kTf = alloc(s1, "kTf", f"kTf{pi}", [128, 4, 256], F32)
                vf = alloc(s1, "vf", f"vf{pi}", [128, 4, 256], F32)
                S.fence(["gluT", "gluTf", "attnT", "convT", "sgT"])
                S.fence(["kTf", "vf"], early_snap(pi, ["kTf", "vf"]))
                with ExitStack() as sA:
                    TA_ = lambda name, shape, dt=F32: sA.enter_context(nc.sbuf_tensor(f"{name}{pi}", list(shape), dt))
                    qT = alloc(sA, "qT", f"qT{pi}", [128, 8, 384], BF16)
                    kT = alloc(sA, "kT", f"kT{pi}", [128, 4, 512], BF16)
                    vb = alloc(sA, "vb", f"vb{pi}", [128, 4, 256], BF16)
                    sR = ExitStack()
                    rcos = alloc(sR, "rcos", f"rcos{pi}", [128, 512], F32)
                    rsin = alloc(sR, "rsin", f"rsin{pi}", [128, 512], F32)
                    kraw = alloc(sR, "kraw", f"kraw{pi}", [128, 2, 512], F32)
                    S.fence(["qT", "kT", "vb", "rcos", "rsin", ("kraw", 0), ("kraw", 1)],
                            early_snap(pi, ["qT", "kT", "vb", "rcos", "rsin", "kraw"]))
                    S.dma("sp", "ld", rcos[:], ropec[pi], writes=["rcos"])
                    S.dma("sp", "ld", rsin[:], ropes[pi], writes=["rsin"])

                    def proj(ri, col_lo, ncol, c0, b):
                        rv = ring[ri][:, 0:4096].rearrange("p (k c) -> p k c", c=256)
                        for kc in range(16):
                            S.op("pe", lambda e, kc=kc: e.matmul(ps[b][:, 0:ncol], lhsT=rv[:, kc, c0:c0 + 128],
                                                                 rhs=uT[:, kc, col_lo:col_lo + ncol], start=(kc == 0), stop=(kc == 15)),
                                 reads=[("ring", ri), "uT"], writes=[PS(b)], signal=(kc == 15))

                    def rope_evac(ba, bb, col_lo, ncol, dst, dname, dstf=None):
                        i = tmpi()
                        S.op("dve", lambda e: e.tensor_tensor(out=tA[i][:, 0:ncol], in0=ps[ba][:, 0:ncol], in1=rcos[:, col_lo:col_lo + ncol], op=ALU.mult),
                             reads=["rcos"], writes=[PS(ba), ("tA", i)])
                        S.op("dve", lambda e: e.tensor_tensor(out=tB[i][:, 0:ncol], in0=ps[bb][:, 0:ncol], in1=rsin[:, col_lo:col_lo + ncol], op=ALU.mult),
                             reads=["rsin"], writes=[PS(bb), ("tB", i)])
                        S.op("dve", lambda e: e.tensor_tensor(out=dst, in0=tA[i][:, 0:ncol], in1=tB[i][:, 0:ncol], op=ALU.add),
                             reads=[("tA", i), ("tB", i)], writes=[dname])
                        if dstf is not None:
                            S.op("dve", lambda e: e.tensor_tensor(out=dstf, in0=tA[i][:, 256:512], in1=tB[i][:, 256:512], op=ALU.add),
                                 reads=[("tA", i), ("tB", i)], writes=["kTf"])

                    def k_unit():
                        ri = wload(w1[0], 4096)
                        bA, bB = nb(), nb()
                        proj(ri, 0, 512, 0, bA)
                        proj(ri, 0, 512, 128, bB)
                        S.op("act", lambda e: e.activation(out=kraw[:, 0, :], in_=ps[bA][:], func=AF.Identity, bias=0.0, scale=1.0),
                             writes=[PS(bA), ("kraw", 0)])
                        S.op("act", lambda e: e.activation(out=kraw[:, 1, :], in_=ps[bB][:], func=AF.Identity, bias=0.0, scale=1.0),
                             writes=[PS(bB), ("kraw", 1)])
                        for g in range(4):
                            ba, bb = nb(), nb()
                            S.op("pe", lambda e, g=g, ba=ba: e.matmul(ps[ba][:, 0:512], lhsT=perm[:, 1 + g % 2, :], rhs=kraw[:, g // 2, :], start=True, stop=True),
                                 reads=["perm", ("kraw", g // 2)], writes=[PS(ba)])
                            S.op("pe", lambda e, g=g, bb=bb: e.matmul(ps[bb][:, 0:512], lhsT=perm[:, 3 + g % 2, :], rhs=kraw[:, g // 2, :], start=True, stop=True),
                                 reads=["perm", ("kraw", g // 2)], writes=[PS(bb)])
                            rope_evac(ba, bb, 0, 512, kT[:, g, :], "kT", kTf[:, g, :])

                    k_unit()

                    def v_unit():
                        ri = wload(w1[1], 4096)
                        rvv = ring[ri][:, 0:4096].rearrange("p (k c) -> p k c", c=256)
                        for ct in range(4):
                            b = nb()
                            for kc in range(16):
                                S.op("pe", lambda e, kc=kc, ct=ct, b=b: e.matmul(ps[b][:, 0:256], lhsT=uT[:, kc, ct * 128:(ct + 1) * 128],
                                                                             rhs=rvv[:, kc, :], start=(kc == 0), stop=(kc == 15)),
                                     reads=[("ring", ri), "uT"], writes=[PS(b)], signal=(kc == 15))
                            S.op("act", lambda e, ct=ct, b=b: e.activation(out=vb[:, ct, :], in_=ps[b][:, 0:256], func=AF.Identity, bias=0.0, scale=1.0),
                                 writes=[PS(b), "vb"])
                            S.op("dve", lambda e, ct=ct, b=b: e.tensor_copy(out=vf[:, ct, :], in_=ps[b][:, 0:256]), writes=[PS(b), "vf"])

                    v_unit()
                    def q_unit(iq):
                        ri = wload(w1[2 + iq], 4096)
                        for jj in range(2):
                            c = 2 * iq + jj
                            ba = nb()
                            proj(ri, 128, 384, jj * 128, ba)
                            S.op("act", lambda e, jj=jj, ba=ba: e.activation(out=kraw[:, jj, 0:384], in_=ps[ba][:, 0:384], func=AF.Identity, bias=0.0, scale=1.0),
                                 writes=[PS(ba), ("kraw", jj)])
                            bb = nb()
                            S.op("pe", lambda e, jj=jj, bb=bb: e.matmul(ps[bb][:, 0:384], lhsT=perm[:, 0, :], rhs=kraw[:, jj, 0:384], start=True, stop=True),
                                 reads=["perm", ("kraw", jj)], writes=[PS(bb)])
                            rope_evac(ba, bb, 128, 384, qT[:, c, :], "qT")

                    for iq in range(4):
                        q_unit(iq)

                    def glu_unit(i8):
                        ri = wload(w1[6 + i8], 4096)
                        ba, bb = nb(), nb()
                        proj(ri, 0, 512, 0, ba)
                        proj(ri, 0, 512, 128, bb)
                        i = tmpi()
                        S.op("act", lambda e: e.activation(out=tA[i][:], in_=ps[bb][:], func=AF.Sigmoid),
                             writes=[PS(bb), ("tA", i)])
                        S.op("dve", lambda e: e.tensor_tensor(out=gluT[:, i8, :], in0=ps[ba][:], in1=tA[i][:], op=ALU.mult),
                             reads=[("tA", i)], writes=[PS(ba), "gluT"])
                        S.op("dve", lambda e: e.tensor_tensor(out=gluTf[:, i8, :], in0=ps[ba][:, 256:512], in1=tA[i][:, 256:512], op=ALU.mult),
                             reads=[("tA", i)], writes=[PS(ba), "gluTf"])
                        S.op("dve", lambda e: e.tensor_scalar(out=gluT[:, i8, 0:128], in0=gluT[:, i8, 0:128], scalar1=flg[:, pi:pi + 1],
                                                              scalar2=None, op0=ALU.mult), reads=["flg"], writes=["gluT"])

                    for i8 in range(8):
                        glu_unit(i8)

                    stage(f"s3_{pi}", **{f"kT{pi}": (kT[:].rearrange("p a b -> p (a b)"), [128, 2048], BF16, "kT"), f"qT{pi}": (qT[:].rearrange("p a b -> p (a b)"), [128, 3072], BF16, "qT"), f"vb{pi}": (vb[:].rearrange("p a b -> p (a b)"), [128, 1024], BF16, "vb"), f"gluT{pi}": (gluT[:].rearrange("p a b -> p (a b)"), [128, 4096], BF16, "gluT"), f"ring0_{pi}": (ring[0][:], [128, 6144], BF16, ("ring", 0)), f"ring1_{pi}": (ring[1][:], [128, 6144], BF16, ("ring", 1)), f"rcos{pi}": (rcos[:], [128, 512], F32, "rcos"), f"tA{pi}": (tA[0][:], [128, 512], F32, ("tA", 0))})
                    gate_pending = list(range(16))

                    def gate_unit(j):
                        ri = wload(w2[j][:, 0:4096], 4096)
                        rv = ring[ri][:, 0:4096].rearrange("p (k c) -> p k c", c=128)
                        bA, bB = nb(), nb()
                        for kc in range(16):
                            S.op("pe", lambda e, kc=kc: e.matmul(ps[bA][:, 0:NC_], lhsT=rv[:, kc, :], rhs=uT[:, kc, 128:512], start=(kc == 0), stop=(kc == 15)),
                                 reads=[("ring", ri), "uT"], writes=[PS(bA)], signal=(kc == 15))
                        for kc in range(16):
                            S.op("pe", lambda e, kc=kc: e.matmul(ps[bB][:, 0:NC_], lhsT=rv[:, 16 + kc, :], rhs=uT[:, kc, 128:512], start=(kc == 0), stop=(kc == 15)),
                                 reads=[("ring", ri), "uT"], writes=[PS(bB)], signal=(kc == 15))
                        S.op("act", lambda e: e.activation(out=sgT[:, 0, j, :], in_=ps[bA][:, 0:NC_], func=AF.Sigmoid), writes=[PS(bA), "sgT"])
                        S.op("act", lambda e: e.activation(out=sgT[:, 1, j, :], in_=ps[bB][:, 0:NC_], func=AF.Sigmoid), writes=[PS(bB), "sgT"])

                    def drain_gates(n):
                        for _ in range(n):
                            if gate_pending:
                                gate_unit(gate_pending.pop(0))

                    sR.close()
                    if has_s:
                        kcb = TA_("kcb", [128, 16, 256], BF16)
                        kcT = TA_("kcT", [128, 4, 16, 128], BF16)
                        vcb = TA_("vcb", [128, 16, 256], BF16)
                        S.fence(["kcb", "kcT", "vcb"])
                    def attn_tile(qc0, keytiles):
                        nk = len(keytiles)
                        for cp in range(4):
                            bo, bd = 6, 7
                            def qk_exp(ki):
                                kfn, vfn, mask_ap, kres = keytiles[ki]
                                bE, bO = nb(), nb()
                                for bank, hhs in ((bE, (0, 2)), (bO, (1, 3))):
                                    for n_, hh in enumerate(hhs):
                                        S.op("pe", lambda e, n_=n_, bank=bank, mask_ap=mask_ap: e.matmul(
                                            ps[bank][:, n_ * 128:(n_ + 1) * 128], lhsT=identb[:], rhs=mask_ap, start=(n_ == 0), stop=False,
                                            skip_group_check=True),
                                            reads=["identb", "mk", "mksc", "mksn"], writes=[PS(bank)], signal=False)
                                    for n_, hh in enumerate(hhs):
                                        h = 4 * cp + hh
                                        c, hf = h // 2, h % 2
                                        kap = kfn(cp)
                                        S.op("pe", lambda e, n_=n_, bank=bank, kap=kap, c=c, hf=hf: e.matmul(
                                            ps[bank][:, n_ * 128:(n_ + 1) * 128], lhsT=kap[hf * 64:(hf + 1) * 64, :],
                                            rhs=qT[hf * 64:(hf + 1) * 64, c, qc0:qc0 + 128], start=False, stop=True, skip_group_check=True),
                                            reads=[kres, "qT"], writes=[PS(bank)], signal=(n_ == 1))
                                i = tmpi()
                                S.op("act", lambda e, i=i, bE=bE: e.activation(out=pTb[i][:, 0:256], in_=ps[bE][:, 0:256], func=AF.Exp, scale=0.125),
                                     writes=[PS(bE), ("pTb", i)])
                                S.op("act", lambda e, i=i, bO=bO: e.activation(out=pTb[i][:, 256:512], in_=ps[bO][:, 0:256], func=AF.Exp, scale=0.125),
                                     writes=[PS(bO), ("pTb", i)])
                                return i

                            def pv(ki, i):
                                kfn, vfn, mask_ap, kres = keytiles[ki]
                                for hh in range(4):
                                    h = 4 * cp + hh
                                    c, hf = h // 2, h % 2
                                    cl = (c % 2) * 128
                                    pc = {0: 0, 2: 128, 1: 256, 3: 384}[hh]
                                    vap = vfn(cp)
                                    S.op("pe", lambda e, pc=pc, i=i, vap=vap, hf=hf, cl=cl, ki=ki, hh=hh: e.matmul(
                                        ps[bo][hf * 64:(hf + 1) * 64, cl:cl + 128], lhsT=vap, rhs=pTb[i][:, pc:pc + 128],
                                        start=(ki == 0 and hh < 2), stop=(ki == nk - 1), skip_group_check=True),
                                        reads=[("pTb", i), "vb", "vcb"], writes=[PS(bo)], signal=False)
                                    S.op("pe", lambda e, pc=pc, i=i, hf=hf, cl=cl, ki=ki, hh=hh: e.matmul(
                                        ps[bd][hf * 64:(hf + 1) * 64, cl:cl + 128], lhsT=onesb[:], rhs=pTb[i][:, pc:pc + 128],
                                        start=(ki == 0 and hh < 2), stop=(ki == nk - 1), skip_group_check=True),
                                        reads=[("pTb", i), "onesb"], writes=[PS(bd)], signal=(hh == 3))

                            icur = qk_exp(0)
                            for ki in range(nk):
                                inext = qk_exp(ki + 1) if ki + 1 < nk else None
                                pv(ki, icur)
                                icur = inext
                            for cc in range(2):
                                c = 2 * cp + cc
                                S.op("dve", lambda e, cc=cc, c=c: e.tensor_scalar(out=rc[:, cc * 128:(cc + 1) * 128], in0=ps[bd][:, cc * 128:(cc + 1) * 128],
                                                                               scalar1=esT[:, c:c + 1], scalar2=None, op0=ALU.add),
                                     reads=["esT"], writes=[PS(bd), "rc"])
                            S.op("dve", lambda e: e.reciprocal(out=rc[:, 0:256], in_=rc[:, 0:256]), reads=["rc"], writes=["rc"])
                            S.op("dve", lambda e, cp=cp: e.tensor_tensor(
                                out=attnT[:, 2 * cp:2 * cp + 2, qc0:qc0 + 128], in0=ps[bo][:, 0:256].rearrange("p (c q) -> p c q", c=2),
                                in1=rc[:, 0:256].rearrange("p (c q) -> p c q", c=2), op=ALU.mult),
                                reads=["rc"], writes=[PS(bo), "attnT"])
                            drain_gates(1)
                            drain_ada(1)

                    for t in range(npr_t):
                        ct = t + 1
                        mprev = mk[:, 2, :] if (pi == 0 and t == 0) else mk[:, 0, :]
                        kts = [(lambda g, ct=ct: kT[:, g, (ct - 1) * 128:ct * 128], lambda g, ct=ct: vb[:, ct - 1, g * 64:(g + 1) * 64], mprev, "kT"),
                               (lambda g, ct=ct: kT[:, g, ct * 128:(ct + 1) * 128], lambda g, ct=ct: vb[:, ct, g * 64:(g + 1) * 64], mk[:, 1, :], "kT")]
                        attn_tile(t * 128, kts)
                    if has_s:
                        for q in range(4):
                            S.dma("pool", "wld", kcb[:, q * 4:(q + 1) * 4, :], cachek[q * 4:(q + 1) * 4].rearrange("s p c -> p s c"), writes=["kcb"])
                        for q in range(4):
                            S.dma("pool", "wld", vcb[:, q * 4:(q + 1) * 4, :], cachev[q * 4:(q + 1) * 4].rearrange("s p c -> p s c"), writes=["vcb"])
                        for s in range(16):
                            b = nb()
                            pv = ps[b][:].bitcast(BF16)
                            for g in range(4):
                                for hf in range(2):
                                    S.op("pe", lambda e, g=g, hf=hf, pv=pv, s=s: e.transpose(
                                        out=pv[hf * 64:(hf + 1) * 64, g * 128:(g + 1) * 128],
                                        in_=kcb[:, s, g * 64:(g + 1) * 64], identity=identb[:]),
                                        reads=["kcb", "identb"], writes=[PS(b)], signal=(g == 3 and hf == 1))
                            S.op("act", lambda e, s=s, pv=pv: e.activation(out=kcT[:, :, s, :], in_=pv[:, 0:512].rearrange("p (g k) -> p g k", g=4),
                                                                          func=AF.Identity, bias=0.0, scale=1.0), writes=[PS(b), "kcT"])
                        kts = []
                        for s in range(16):
                            kts.append((lambda g, s=s: kcT[:, g, s, :], lambda g, s=s: vcb[:, s, g * 64:(g + 1) * 64], mksc[:, s, :], "kcT"))
                        kts.append((lambda g: kT[:, g, 384:512], lambda g: vb[:, 3, g * 64:(g + 1) * 64], mksn[:], "kT"))
                        attn_tile(256, kts)
                stage(f"s4_{pi}", **{f"attnT{pi}": (attnT[:].rearrange("p a b -> p (a b)"), [128, 3072], BF16, "attnT")})
                with ExitStack() as sB:
                    TB_ = lambda name, shape, dt=F32: sB.enter_context(nc.sbuf_tensor(f"{name}{pi}", list(shape), dt))
                    ycv = TB_("ycv", [128, 8, 384])
                    diag2 = [TB_("diagA", [128, 31, 128], BF16), TB_("diagB", [128, 31, 128], BF16)]
                    cm = TB_("cm", [128, 384])
                    cr = TB_("cr", [128, 384])
                    S.fence([("ycv", 0), ("ycv", 1), ("ycv", 2), ("ycv", 3), ("ycv", 4), ("ycv", 5), ("ycv", 6), ("ycv", 7), ("diag", 0), ("diag", 1), "cm", "cr", "gs", ("stl", 0), ("stl", 1)])
                    if has_s:
                        gs = TB_("gs", [128, 8, 16, 38], BF16)
                        stl = TB_("stl", [120, 2, 1024], BF16)
                        for q4 in range(4):
                            S.dma("pool", "wld", stl[:, q4 % 2, :], stconv[q4 * 4:(q4 + 1) * 4].rearrange("s t c -> (s t) c"), writes=[("stl", q4 % 2)])
                            b = nb()
                            pv = ps[b][:].bitcast(BF16)
                            for i8 in range(8):
                                S.op("pe", lambda e, i8=i8, pv=pv, q4=q4: e.transpose(out=pv[:, i8 * 120:(i8 + 1) * 120], in_=stl[:, q4 % 2, i8 * 128:(i8 + 1) * 128],
                                                                                identity=identb[0:120, 0:120]),
                                     reads=[("stl", q4 % 2), "identb"], writes=[PS(b)], signal=(i8 == 7))
                            for i8 in range(8):
                                S.op("act", lambda e, i8=i8, pv=pv, q4=q4: e.activation(
                                    out=gs[:, i8, q4 * 4:(q4 + 1) * 4, 0:30], in_=pv[:, i8 * 120:(i8 + 1) * 120].rearrange("p (s t) -> p s t", s=4),
                                    func=AF.Identity, bias=0.0, scale=1.0), writes=[PS(b), "gs"])
                        for i8 in range(8):
                            S.op("dve", lambda e, i8=i8: e.tensor_copy(out=gs[:, i8, :, 30:38], in_=gluT[:, i8, 384:512].rearrange("p (s t) -> p s t", t=8)),
                                 reads=["gluT"], writes=["gs"])
                    bS, bQ = 6, 7
                    pend5 = []
                    for i8 in range(8):
                        diag = diag2[i8 % 2]
                        dres = ("diag", i8 % 2)
                        S.op("dve", lambda e, i8=i8, diag=diag: e.tensor_tensor(out=diag[:], in0=identf[:].unsqueeze(1).to_broadcast([128, 31, 128]),
                                                                     in1=wdwT[:, i8, :].unsqueeze(2).to_broadcast([128, 31, 128]), op=ALU.mult),
                             reads=["identf", "wdwT"], writes=[dres])
                        drain_ada(1)
                        b = nb()
                        for j in range(31):
                            S.op("pe", lambda e, j=j, i8=i8, b=b, diag=diag: e.matmul(ps[b][:, 0:npc], lhsT=diag[:, j, :], rhs=gluT[:, i8, 98 + j:98 + j + npc],
                                                                      start=(j == 0), stop=(j == 30)),
                                 reads=[dres, "gluT"], writes=[PS(b)], signal=(j == 30 and not has_s))
                        if has_s:
                            for j in range(31):
                                S.op("pe", lambda e, j=j, i8=i8, b=b, diag=diag: e.matmul(ps[b][:, npc:npc + 128], lhsT=diag[:, j, :], rhs=gs[:, i8, :, j:j + 8],
                                                                          start=False, stop=(j == 30), skip_group_check=True),
                                     reads=[dres, "gs"], writes=[PS(b)], signal=(j == 30))
                        S.op("act", lambda e, i8=i8, b=b: e.activation(out=ycv[:, i8, :], in_=ps[b][:, 0:NC_], func=AF.Identity, bias=cvT[:, i8, 0:1], scale=1.0),
                             reads=["cvT"], writes=[PS(b), ("ycv", i8)])
                        i = tmpi()
                        S.op("act", lambda e, i8=i8, i=i: e.activation(out=tA[i][:, 0:NC_], in_=ycv[:, i8, :], func=AF.Square),
                             reads=[("ycv", i8)], writes=[("tA", i)])

                        def stats_mm(i8=i8, i=i):
                            S.op("pe", lambda e: e.matmul(ps[bS][:, 0:NC_], lhsT=onesf[:], rhs=ycv[:, i8, :], start=(i8 == 0), stop=(i8 == 7)),
                                 reads=["onesf", ("ycv", i8)], writes=[PS(bS)], signal=(i8 == 7))
                            S.op("pe", lambda e: e.matmul(ps[bQ][:, 0:NC_], lhsT=onesf[:], rhs=tA[i][:, 0:NC_], start=(i8 == 0), stop=(i8 == 7)),
                                 reads=["onesf", ("tA", i)], writes=[PS(bQ)], signal=True)
                        if pend5:
                            pend5.pop(0)()
                        pend5.append(stats_mm)
                    while pend5:
                        pend5.pop(0)()
                    S.op("act", lambda e: e.activation(out=cm[:], in_=ps[bS][:, 0:NC_], func=AF.Identity, bias=0.0, scale=1.0 / 1024),
                         writes=[PS(bS), "cm"])
                    S.op("dve", lambda e: e.tensor_tensor(out=cr[:], in0=cm[:], in1=cm[:], op=ALU.mult), reads=["cm"], writes=["cr"])
                    S.op("dve", lambda e: e.scalar_tensor_tensor(out=cr[:], in0=ps[bQ][:, 0:NC_], scalar=1.0 / 1024, in1=cr[:],
                                                                 op0=ALU.mult, op1=ALU.subtract), reads=[], writes=[PS(bQ), "cr"])
                    S.op("act", lambda e: e.activation(out=cr[:], in_=cr[:], func=AF.Sqrt, bias=eps_t[:, 0:1], scale=1.0),
                         reads=["eps_t"], writes=["cr"])
                    S.op("dve", lambda e: e.reciprocal(out=cr[:], in_=cr[:]), reads=[], writes=["cr"])
                    for i8 in range(8):
                        S.op("dve", lambda e, i8=i8: e.tensor_tensor(out=ycv[:, i8, :], in0=ycv[:, i8, :], in1=cm[:], op=ALU.subtract),
                             reads=["cm"], writes=[("ycv", i8)])
                        S.op("dve", lambda e, i8=i8: e.tensor_tensor(out=ycv[:, i8, :], in0=ycv[:, i8, :], in1=cr[:], op=ALU.mult),
                             reads=["cr"], writes=[("ycv", i8)])
                        S.op("act", lambda e, i8=i8: e.activation(out=convT[:, i8, :], in_=ycv[:, i8, :], func=AF.Silu,
                                                                  bias=cvT[:, i8, 2:3], scale=cvT[:, i8, 1:2]),
                             reads=[("ycv", i8), "cvT"], writes=["convT"])
                stage(f"s5_{pi}", **{f"convT{pi}": (convT[:].rearrange("p a b -> p (a b)"), [128, 3072], BF16, "convT")})
                if has_s:
                    sO = ExitStack()
                    osb = sO.enter_context(nc.sbuf_tensor("osb", [128, 512], F32))
                    osc = sO.enter_context(nc.sbuf_tensor("osc", [128, 1024], F32))
                    osp = sO.enter_context(nc.sbuf_tensor("osp", [32, 1024], F32))
                    S.fence(["osb", "osc", "osp"])
                    S.dma("sp", "st", kwin_s.rearrange("s p c -> s (p c)")[:, 0:120 * 256], cachek.rearrange("s p c -> s (p c)")[:, 8 * 256:128 * 256])
                    S.dma("sp", "st", vwin_s.rearrange("s p c -> s (p c)")[:, 0:120 * 256], cachev.rearrange("s p c -> s (p c)")[:, 8 * 256:128 * 256])
                    S.dma("sp", "st", conv_s.rearrange("s t c -> s (t c)")[:, 0:22 * 1024], stconv.rearrange("s t c -> s (t c)")[:, 8 * 1024:30 * 1024])
                    for which in range(2):
                        b = nb()
                        for g in range(4):
                            S.op("pe", lambda e, g=g, b=b, which=which: e.transpose(out=ps[b][:, g * 64:(g + 1) * 64],
                                                                                    in_=kTf[0:64, g, which * 128:(which + 1) * 128],
                                                                                    identity=identf[0:64, 0:64]),
                                 reads=["kTf", "identf"], writes=[PS(b)], signal=(g == 3))
                        S.op("dve", lambda e, b=b, which=which: e.tensor_copy(out=osb[:, which * 256:(which + 1) * 256], in_=ps[b][:, 0:256]),
                             writes=[PS(b), "osb"])
                    S.dma("sp", "st", kwin_p, osb[:, 0:256], reads=["osb"])
                    S.dma("sp", "st", vwin_p, vf[:, 2, :], reads=["vf"])
                    for s in range(16):
                        S.dma("sp", "st", kwin_s[s, 120:128, :], osb[s * 8:(s + 1) * 8, 256:512], reads=["osb"])
                        S.dma("sp", "st", vwin_s[s, 120:128, :], vf[s * 8:(s + 1) * 8, 3, :], reads=["vf"])
                    for half in range(2):
                        b = nb()
                        for i4 in range(4):
                            i8 = half * 4 + i4
                            S.op("pe", lambda e, i8=i8, i4=i4, b=b: e.transpose(out=ps[b][:, i4 * 128:(i4 + 1) * 128], in_=gluTf[:, i8, 128:256],
                                                                               identity=identf[:]),
                                 reads=["gluTf", "identf"], writes=[PS(b)], signal=(i4 == 3))
                        S.op("dve", lambda e, half=half, b=b: e.tensor_copy(out=osc[:, half * 512:(half + 1) * 512], in_=ps[b][:]),
                             writes=[PS(b), "osc"])
                        b2 = nb()
                        for i4 in range(4):
                            i8 = half * 4 + i4
                            S.op("pe", lambda e, i8=i8, i4=i4, b2=b2: e.transpose(out=ps[b2][0:32, i4 * 128:(i4 + 1) * 128], in_=gluTf[:, i8, 96:128],
                                                                                 identity=identf[:]),
                                 reads=["gluTf", "identf"], writes=[PS(b2)], signal=(i4 == 3))
                        S.op("dve", lambda e, half=half, b2=b2: e.tensor_copy(out=osp[:, half * 512:(half + 1) * 512], in_=ps[b2][0:32, :]),
                             writes=[PS(b2), "osp"])
                    S.dma("sp", "st", conv_p, osp[2:32, :], reads=["osp"])
                    for s in range(16):
                        S.dma("sp", "st", conv_s[s, 22:30, :], osc[s * 8:(s + 1) * 8, :], reads=["osc"])
                    sO.close()

                def s6_unit(j):
                    ri2 = wload(w2[j][:, 4096:6144], 2048)
                    rv2 = ring[ri2][:, 0:2048].rearrange("p (k c) -> p k c", c=128)
                    bC, bD = nb(), nb()
                    for kc in range(8):
                        S.op("pe", lambda e, kc=kc: e.matmul(ps[bC][:, 0:NC_], lhsT=rv2[:, kc, :], rhs=attnT[:, kc, :], start=(kc == 0), stop=(kc == 7)),
                             reads=[("ring", ri2), "attnT"], writes=[PS(bC)], signal=(kc == 7))
                    for kc in range(8):
                        S.op("pe", lambda e, kc=kc: e.matmul(ps[bD][:, 0:NC_], lhsT=rv2[:, 8 + kc, :], rhs=convT[:, kc, :], start=(kc == 0), stop=(kc == 7)),
                             reads=[("ring", ri2), "convT"], writes=[PS(bD)], signal=(kc == 7))
                    i = tmpi()
                    S.op("dve", lambda e: e.tensor_tensor(out=tA[i][:, 0:NC_], in0=ps[bC][:, 0:NC_], in1=sgT[:, 0, j, :], op=ALU.mult),
                         reads=["sgT"], writes=[PS(bC), ("tA", i)])
                    S.op("dve", lambda e: e.tensor_tensor(out=tB[i][:, 0:NC_], in0=ps[bD][:, 0:NC_], in1=sgT[:, 1, j, :], op=ALU.mult),
                         reads=["sgT"], writes=[PS(bD), ("tB", i)])
                    S.op("dve", lambda e: e.tensor_tensor(out=mergedT[:, j, :], in0=tA[i][:, 0:NC_], in1=tB[i][:, 0:NC_], op=ALU.add),
                         reads=[("tA", i), ("tB", i)], writes=["mergedT"])

                drain_gates(16)
                for j in range(16):
                    s6_unit(j)
                    drain_ada(1)

            stage(f"s6_{pi}", **{f"mergedT{pi}": (mergedT[:].rearrange("p a b -> p (a b)"), [128, 6144], BF16, "mergedT")})
            with ExitStack() as s2b:
                XR = alloc(s2b, "XR", f"XR{pi}", [128, 3, D], F32)
                lnG = alloc(s2b, "lnG", f"lnG{pi}", [128, 2, D], F32)
                hT = s2b.enter_context(nc.sbuf_tensor(f"hT{pi}", [128, 44, 384], BF16))
                late_spans[pi] = [spans["XR"], spans["lnG"]]
                S.fence(["lnG", "hT", ("XR", 0), ("XR", 1), ("XR", 2)])
                for t in range(NT):
                    S.dma("sp", "ld", XR[:, t, :], xin[mains[t]], writes=[("XR", t)])

                def load_lnG(r0):
                    for k in range(2):
                        S.dma("sp", "ld", lnG[:, k, :], lngb[r0 + k:r0 + k + 1, :].broadcast_to([128, D]), writes=["lnG"])

                zt1 = s2b.enter_context(nc.sbuf_tensor(f"zt1_{pi}", [128, D], BF16))
                zt2 = s2b.enter_context(nc.sbuf_tensor(f"zt2_{pi}", [128, D], BF16))
                S.fence(["zt1", "zt2"])
                zts = [(zt, "zt"), (zt1, "zt1"), (zt2, "zt2")]

                def ln_affine_all():
                    for t in range(NT):
                        ln_stats(XR[:, t, :], ("XR", t), t)
                    for t in range(NT):
                        S.op("act", lambda e, t=t: e.activation(out=XR[:, t, :], in_=XR[:, t, :], func=AF.Identity,
                                                                bias=nmr[:, t, 0:1], scale=rstd[:, t, 0:1]),
                             reads=[("nmr", t), ("rstd", t)], writes=[("XR", t)])
                    for t in range(NT):
                        S.op("dve", lambda e, t=t: e.tensor_tensor(out=XR[:, t, :], in0=XR[:, t, :], in1=lnG[:, 0, :], op=ALU.mult),
                             reads=["lnG"], writes=[("XR", t)])
                        S.op("dve", lambda e, t=t: e.tensor_tensor(out=XR[:, t, :], in0=XR[:, t, :], in1=lnG[:, 1, :], op=ALU.add),
                             reads=["lnG"], writes=[("XR", t)])

                def s7_blk(blk):
                    ri = wload(w3[blk], 4096)
                    rv = ring[ri][:, 0:4096].rearrange("p (k c) -> p k c", c=256)
                    for jj in range(2):
                        j = blk * 2 + jj
                        b = nb()
                        for kc in range(16):
                            S.op("pe", lambda e, kc=kc, jj=jj, b=b: e.matmul(ps[b][:, 0:NC_], lhsT=rv[:, kc, jj * 128:(jj + 1) * 128], rhs=mergedT[:, kc, :],
                                                                        start=(kc == 0), stop=(kc == 15)),
                                 reads=[("ring", ri), "mergedT"], writes=[PS(b)], signal=(kc == 15))
                        i = tmpi()
                        gate_evac(b, tB[i], ("tB", i), 2, j, NC_, has_s)
                        if pend7:
                            pend7.pop(0)()
                        pend7.append(lambda i=i, j=j: resid_add(XR, tB[i], ("tB", i), j, NT))

                pend7 = []
                for blk in range(8):
                    s7_blk(blk)
                while pend7:
                    pend7.pop(0)()
                stage(f"s7a_{pi}", **{f"r1_{pi}": (XR[:, 0, :], [128, 2048], F32, ("XR", 0))})
                load_lnG(0)
                for t in range(NT):
                    ln_stats(XR[:, t, :], ("XR", t), t)
                for t in range(NT):
                    S.op("act", lambda e, t=t: e.activation(out=XR[:, t, :], in_=XR[:, t, :], func=AF.Identity,
                                                            bias=nmr[:, t, 0:1], scale=rstd[:, t, 0:1]),
                         reads=[("nmr", t), ("rstd", t)], writes=[("XR", t)])
                for t in range(NT):
                    S.op("dve", lambda e, t=t: e.tensor_tensor(out=XR[:, t, :], in0=XR[:, t, :], in1=lnG[:, 0, :], op=ALU.mult),
                         reads=["lnG"], writes=[("XR", t)])
                    S.op("dve", lambda e, t=t: e.tensor_tensor(out=XR[:, t, :], in0=XR[:, t, :], in1=lnG[:, 1, :], op=ALU.add),
                         reads=["lnG"], writes=[("XR", t)])
                    ln_stats(XR[:, t, :], ("XR", t), t)
                    ln_z(XR[:, t, :], ("XR", t), t, zts[t][0], zts[t][1])
                    ln_T(zts[t][0], zts[t][1], 128 * (t + 1), 3, 4, has_s and t == NT - 1)

                stage(f"s7_{pi}", **{f"x1_{pi}": (XR[:, 0, :], [128, 2048], F32, ("XR", 0)), f"u2T{pi}": (uT[:].rearrange("p a b -> p (a b)"), [128, 8192], BF16, "uT"), f"lnG{pi}": (lnG[:].rearrange("p a b -> p (a b)"), [128, 4096], F32, "lnG")})
                def s8_unit(j):
                    ri = wload(w4[j], 4096)
                    rv = ring[ri][:, 0:4096].rearrange("p (k c) -> p k c", c=128)
                    bG, bU = nb(), nb()
                    for kc in range(16):
                        S.op("pe", lambda e, kc=kc: e.matmul(ps[bG][:, 0:NC_], lhsT=rv[:, kc, :], rhs=uT[:, kc, 128:512], start=(kc == 0), stop=(kc == 15)),
                             reads=[("ring", ri), "uT"], writes=[PS(bG)], signal=(kc == 15))
                    for kc in range(16):
                        S.op("pe", lambda e, kc=kc: e.matmul(ps[bU][:, 0:NC_], lhsT=rv[:, 16 + kc, :], rhs=uT[:, kc, 128:512], start=(kc == 0), stop=(kc == 15)),
                             reads=[("ring", ri), "uT"], writes=[PS(bU)], signal=(kc == 15))
                    i = tmpi()
                    S.op("act", lambda e: e.activation(out=tA[i][:, 0:NC_], in_=ps[bG][:, 0:NC_], func=AF.Silu), writes=[PS(bG), ("tA", i)])
                    S.op("dve", lambda e: e.tensor_tensor(out=hT[:, j, :], in0=ps[bU][:, 0:NC_], in1=tA[i][:, 0:NC_], op=ALU.mult),
                         reads=[("tA", i)], writes=[PS(bU), "hT"])

                for j in range(44):
                    s8_unit(j)

                stage(f"s8_{pi}", **{f"hT{pi}": (hT[:].rearrange("p a b -> p (a b)"), [128, 44 * 384], BF16, "hT")})
                def s9_unit(j):
                    ria = wload(w5[j][:, 0:2816], 2816)
                    rib = wload(w5[j][:, 2816:5632], 2816)
                    rva = ring[ria][:, 0:2816].rearrange("p (k c) -> p k c", c=128)
                    rvb = ring[rib][:, 0:2816].rearrange("p (k c) -> p k c", c=128)
                    b = nb()
                    for kc in range(44):
                        rvx, rix, kk = (rva, ria, kc) if kc < 22 else (rvb, rib, kc - 22)
                        S.op("pe", lambda e, kc=kc, rvx=rvx, kk=kk: e.matmul(ps[b][:, 0:NC_], lhsT=rvx[:, kk, :], rhs=hT[:, kc, :], start=(kc == 0), stop=(kc == 43)),
                             reads=[("ring", rix), "hT"], writes=[PS(b)], signal=(kc == 43))
                    i = tmpi()
                    gate_evac(b, tB[i], ("tB", i), 5, j, NC_, has_s)
                    if pend9:
                        pend9.pop(0)()
                    pend9.append(lambda i=i, j=j: resid_add(XR, tB[i], ("tB", i), j, NT))

                pend9 = []
                for j in range(16):
                    s9_unit(j)
                    if s2_next and j % 2 == 1:
                        s2_next.pop(0)()
                while pend9:
                    pend9.pop(0)()
                while s2_next:
                    s2_next.pop(0)()
                stage(f"s9_{pi}")
                pre_s10_snap[pi] = S.snapshot()
                load_lnG(2)
                ln_affine_all()
                for t in range(NT):
                    S.dma("sp", "st", y_main[mains[t] - 1], XR[:, t, :], reads=[("XR", t)])

        passes = [(0, [1, 2, 3], False), (3, [4, 5, 6], False), (6, [7, 8, 9], True)]
        for f in s2_tiles(*passes[0]):
            f()
        for pi, (hidx, mains, has_s) in enumerate(passes):
            nxt = s2_tiles(*passes[pi + 1]) if pi + 1 < len(passes) else []
            run_pass(pi, hidx, mains, has_s, nxt)
            drain_ada(48)

        S.finish("sp")
        S.emit()
    return nc


def _tile_w(W, ncols):
    K, N = W.shape
    return np.ascontiguousarray(W.reshape(K // 128, 128, N // ncols, ncols).transpose(2, 1, 0, 3).reshape(N // ncols, 128, -1))


def _tile_units(units):
    parts = []
    for W in units:
        K = W.shape[0]
        parts.append(W.reshape(K // 128, 128, 128).transpose(1, 0, 2).reshape(128, -1))
    return np.concatenate(parts, axis=1)


_CACHE = {}


def kernel(x_prompt, x_sample, c_prompt, c_sample, cache_k_win, cache_v_win, state_conv,
           w_ada, b_ada, w_in, attn_sinks, w_dw, b_dw, cn_gain, cn_bias, w_br_attn, w_br_conv, w_out,
           ln1_gain, ln1_bias, w_ffn_gate, w_ffn_up, w_ffn_down, ln2_gain, ln2_bias):
    f = lambda a: np.asarray(a, dtype=np.float32)
    x_prompt, x_sample, c_prompt, c_sample = f(x_prompt), f(x_sample), f(c_prompt), f(c_sample)
    cache_k_win, cache_v_win, state_conv = f(cache_k_win), f(cache_v_win), f(state_conv)
    w_in0 = f(w_in)[0]
    partner = np.concatenate([np.arange(32, 64), np.arange(0, 32)])
    qcols = np.arange(1024).reshape(16, 64)
    qrh = qcols[:, partner].reshape(-1)
    kcols = 1024 + np.arange(256).reshape(4, 64)
    krh = kcols[:, partner]
    blocks1 = [_tile_w(w_in0[:, 1024:1280], 256)[0], _tile_w(w_in0[:, 1280:1536], 256)[0]]
    for i in range(4):
        blocks1.append(_tile_w(w_in0[:, i * 256:(i + 1) * 256], 256)[0])
    for i in range(8):
        cols = np.concatenate([1536 + np.arange(i * 128, (i + 1) * 128), 2560 + np.arange(i * 128, (i + 1) * 128)])
        blocks1.append(_tile_w(w_in0[:, cols], 256)[0])
    w1 = np.stack(blocks1)
    wba, wbc = f(w_br_attn)[0], f(w_br_conv)[0]
    w2 = np.stack([_tile_units([w_in0[:, 3584 + j * 128:3584 + (j + 1) * 128], w_in0[:, 5632 + j * 128:5632 + (j + 1) * 128],
                                wba[:, j * 128:(j + 1) * 128], wbc[:, j * 128:(j + 1) * 128]]) for j in range(16)])
    w3 = _tile_w(f(w_out)[0], 256)
    wg, wu = f(w_ffn_gate)[0], f(w_ffn_up)[0]
    w4 = np.stack([_tile_units([wg[:, j * 128:(j + 1) * 128], wu[:, j * 128:(j + 1) * 128]]) for j in range(44)])
    w5 = _tile_w(f(w_ffn_down)[0], 128)
    wa = _tile_w(f(w_ada)[0], 256)
    bada = np.ascontiguousarray(f(b_ada)[0].reshape(96, 128))
    sk = f(attn_sinks)[0]
    sinks8 = np.ascontiguousarray(np.repeat(sk.reshape(8, 2), 64, axis=1))
    wdw = f(w_dw)[0]
    cvs = np.stack([f(b_dw)[0], f(cn_gain)[0], f(cn_bias)[0]])
    lngb = np.stack([f(ln1_gain)[0], f(ln1_bias)[0], f(ln2_gain)[0], f(ln2_bias)[0]])
    identd = np.eye(128, dtype=np.float32)
    dst = np.arange(128)
    permd = np.zeros((5, 128, 128), np.float32)
    permd[0, (dst // 64) * 64 + partner[dst % 64], dst] = 1.0
    for h in range(2):
        permd[1 + h, h * 64 + dst % 64, dst] = 1.0
        permd[3 + h, h * 64 + partner[dst % 64], dst] = 1.0
    kk = np.arange(128)[:, None]
    qq = np.arange(128)[None, :]
    m_prev = np.where(kk > qq, 0.0, NEG).astype(np.float32)
    m_own = np.where(kk <= qq, 0.0, NEG).astype(np.float32)
    qs, qt = np.arange(128)[None, :] // 8, np.arange(128)[None, :] % 8
    msc = np.stack([np.where((qs == s) & (kk > qt), 0.0, NEG) for s in range(16)]).astype(np.float32)
    ks_, kt_ = np.arange(128)[:, None] // 8, np.arange(128)[:, None] % 8
    msn = np.where((ks_ == qs) & (kt_ <= qt), 0.0, NEG).astype(np.float32)
    inv = (10000.0 ** (-np.arange(32, dtype=np.float64) / 32.0)).astype(np.float32)
    prow = np.arange(128)
    sgn = np.where((prow % 64) < 32, -1.0, 1.0).astype(np.float32)[:, None]

    def rope_tabs(pos):
        ang = (pos.astype(np.float32)[None, :] * inv[prow % 32][:, None]).astype(np.float32).astype(np.float64)
        return np.cos(ang).astype(np.float32), (np.sin(ang) * sgn).astype(np.float32)

    in_maps = []
    for c in range(NCORE):
        halo = np.zeros((128, D), np.float32) if c == 0 else x_prompt[0, c * 1024 - 128:c * 1024]
        xin = np.concatenate([halo[None], x_prompt[0, c * 1024:(c + 1) * 1024].reshape(8, 128, D),
                              x_sample[c * 16:(c + 1) * 16].reshape(1, 128, D)], axis=0)
        cvec = np.concatenate([c_prompt, c_sample[c * 16:(c + 1) * 16]], axis=0)
        rc_, rs_ = [], []
        for p in range(3):
            base = c * 1024 - 128 + p * 384
            pos = base + np.arange(512)
            if p == 2:
                pos = pos.copy()
                pos[384:] = 8192 + (np.arange(128) % 8)
            a, b = rope_tabs(pos)
            rc_.append(a)
            rs_.append(b)
        flags = np.ones((128, 3), np.float32)
        if c == 0:
            flags[:, 0] = 0.0
        m0 = np.full((128, 128), NEG, np.float32) if c == 0 else m_prev
        in_maps.append(dict(
            xin=np.ascontiguousarray(xin), cvec=np.ascontiguousarray(cvec),
            cachek=np.ascontiguousarray(cache_k_win[0, c * 16:(c + 1) * 16].reshape(16, 128, 256)),
            cachev=np.ascontiguousarray(cache_v_win[0, c * 16:(c + 1) * 16].reshape(16, 128, 256)),
            stconv=np.ascontiguousarray(state_conv[0, c * 16:(c + 1) * 16]),
            w1=w1, w2=w2, w3=w3, w4=w4, w5=w5, wa=wa, bada=bada, sinks8=sinks8, wdw=wdw, cvs=cvs, lngb=lngb,
            ropec=np.stack(rc_), ropes=np.stack(rs_), flags=flags, identd=identd,
            masks=np.stack([m_prev, m_own, m0]), msc=msc, msn=msn, permd=permd))
    if "nc" not in _CACHE:
        _CACHE["nc"] = build_nc()
    res = run_bass_kernel_spmd(_CACHE["nc"], in_maps, core_ids=list(range(NCORE)))
    R = res.results
    y_prompt = np.concatenate([R[c]["y_main"][0:8].reshape(1024, D) for c in range(NCORE)], axis=0)[None]
    y_sample = np.concatenate([R[c]["y_main"][8].reshape(16, 8, D) for c in range(NCORE)], axis=0)
    k_win_prompt = R[7]["kwin_p"].reshape(1, 1, 128, 4, 64)
    v_win_prompt = R[7]["vwin_p"].reshape(1, 1, 128, 4, 64)
    conv_prompt = R[7]["conv_p"].reshape(1, 1, 30, 1024)
    k_win_sample = np.concatenate([R[c]["kwin_s"] for c in range(NCORE)], axis=0).reshape(1, 128, 128, 4, 64)
    v_win_sample = np.concatenate([R[c]["vwin_s"] for c in range(NCORE)], axis=0).reshape(1, 128, 128, 4, 64)
    conv_sample = np.concatenate([R[c]["conv_s"] for c in range(NCORE)], axis=0).reshape(1, 128, 30, 1024)
    return (y_prompt.astype(np.float32), y_sample.astype(np.float32), k_win_prompt.astype(np.float32),
            v_win_prompt.astype(np.float32), conv_prompt.astype(np.float32), k_win_sample.astype(np.float32),
            v_win_sample.astype(np.float32), conv_sample.astype(np.float32))
```

```python
import numpy as np
from contextlib import ExitStack
import concourse.bass as bass
import concourse.mybir as mybir
from concourse.bass_utils import run_bass_kernel_spmd

F32 = mybir.dt.float32
BF16 = mybir.dt.bfloat16
AF = mybir.ActivationFunctionType
ALU = mybir.AluOpType

D = 2048
NCORE = 8
ALPHA = 2.0 ** 0.25
EPS = 1e-5
NEG = -30000.0
DFF = 5632
RING = 4096


class Sync:
    ENG = ("pe", "act", "dve", "pool", "sp")

    def __init__(self, nc, stack):
        self.nc = nc
        self.stack = stack
        self.sem = {}
        self.cnt = {}
        self.prog = {e: [] for e in self.ENG}
        self.waited = {e: {} for e in self.ENG}
        self.res = {}
        self.stopped = False
        self.dpool = {}
        self.dctr = {}
        self.fences = {}
        for e in self.ENG:
            self._mksem(e)

    def _mksem(self, name):
        self.sem[name] = self.stack.enter_context(self.nc.semaphore("s_" + name))
        self.cnt[name] = 0

    def _deps(self, eng, reads, writes):
        need = {}

        def add(sv):
            if sv is None:
                return
            s, v = sv
            if need.get(s, 0) < v:
                need[s] = v

        for r in reads:
            st = self.res.get(r)
            if st:
                add(st["w"])
        for w in writes:
            st = self.res.get(w)
            if st:
                add(st["w"])
                for sv in st["r"]:
                    add(sv)
            fc = self.fences.pop(w, None)
            if fc:
                for sv in fc.items():
                    add(sv)
        out = []
        for s, v in need.items():
            if s == "pe" and eng == "pe":
                continue
            if self.waited[eng].get(s, 0) >= v:
                continue
            self.waited[eng][s] = v
            out.append((s, v))
        return out

    def _record(self, reads, writes, sv):
        for r in reads:
            st = self.res.setdefault(r, {"w": None, "r": []})
            st["r"].append(sv)
            if len(st["r"]) > 64:
                mx = {}
                for s, v in st["r"]:
                    mx[s] = max(mx.get(s, 0), v)
                st["r"] = list(mx.items())
        for w in writes:
            self.res[w] = {"w": sv, "r": []}

    def snapshot(self):
        return {k: v for k, v in self.cnt.items() if v > 0}

    def fence(self, names, snap=None):
        if snap is None:
            snap = {k: v for k, v in self.cnt.items() if v > 0}
        for n in names:
            self.fences[n] = dict(snap)
            self.res.pop(n, None)

    def op(self, eng, fn, reads=(), writes=(), signal=True):
        if self.stopped:
            return
        waits = self._deps(eng, reads, writes)
        if signal:
            self.cnt[eng] += 1
            sv = (eng, self.cnt[eng])
        else:
            sv = (eng, self.cnt[eng] + 1)
        sem = self.sem[eng]
        sems = self.sem

        def run(e, waits=waits, fn=fn, signal=signal, sem=sem):
            for s, v in waits:
                e.wait_ge(sems[s], v)
            ins = fn(e)
            if signal:
                ins.then_inc(sem, 1)

        self.prog[eng].append(run)
        self._record(reads, writes, sv)

    def dma(self, eng, stream, out, in_, reads=(), writes=()):
        if self.stopped:
            return
        if eng not in self.dpool:
            n = 16 if eng == "sp" else 8
            self.dpool[eng] = [f"d_{eng}{i}" for i in range(n)]
            self.dctr[eng] = 0
            for nm in self.dpool[eng]:
                self._mksem(nm)
        pool = self.dpool[eng]
        stream = pool[self.dctr[eng] % len(pool)]
        self.dctr[eng] += 1
        waits = self._deps(eng, reads, writes)
        prev = self.cnt[stream]
        if prev > 0 and self.waited[eng].get(stream, 0) < prev:
            self.waited[eng][stream] = prev
            waits.append((stream, prev))
        self.cnt[stream] += 16
        sv = (stream, self.cnt[stream])
        sem = self.sem[stream]
        sems = self.sem

        def run(e, waits=waits, out=out, in_=in_, sem=sem):
            for s, v in waits:
                e.wait_ge(sems[s], v)
            e.dma_start(out=out, in_=in_).then_inc(sem, 16)

        self.prog[eng].append(run)
        self._record(reads, writes, sv)

    def finish(self, final_eng="sp"):
        waits = []
        for s, c in self.cnt.items():
            if c > 0 and s != final_eng:
                waits.append((s, c))
        sems = self.sem

        def run(e, waits=waits):
            for s, v in waits:
                e.wait_ge(sems[s], v)

        self.prog[final_eng].append(run)

    def emit(self):
        nc = self.nc
        prog = self.prog
        with nc.Block() as block:
            @block.tensor
            def _(e):
                for f in prog["pe"]:
                    f(e)

            @block.scalar
            def _(e):
                for f in prog["act"]:
                    f(e)

            @block.vector
            def _(e):
                for f in prog["dve"]:
                    f(e)

            @block.gpsimd
            def _(e):
                for f in prog["pool"]:
                    f(e)

            @block.sync
            def _(e):
                for f in prog["sp"]:
                    f(e)


class _Stop(Exception):
    pass


def build_nc(stop=None, dumps=()):
    nc = bass.Bass("TRN2", target_bir_lowering=False)

    def din(name, shape):
        return nc.dram_tensor(name, list(shape), F32, kind="ExternalInput").ap()

    def dout(name, shape):
        return nc.dram_tensor(name, list(shape), F32, kind="ExternalOutput").ap()

    xin = din("xin", [10, 128, D])
    cvec = din("cvec", [17, D])
    cachek = din("cachek", [16, 128, 256])
    cachev = din("cachev", [16, 128, 256])
    stconv = din("stconv", [16, 30, 1024])
    w1 = din("w1", [14, 128, 4096])
    permd = din("permd", [5, 128, 128])
    w2 = din("w2", [16, 128, 6144])
    w3 = din("w3", [8, 128, 4096])
    w4 = din("w4", [44, 128, 4096])
    w5 = din("w5", [16, 128, 5632])
    wa = din("wa", [48, 128, 4096])
    bada = din("bada", [96, 128])
    sinks8 = din("sinks8", [8, 128])
    wdw = din("wdw", [31, 1024])
    cvs = din("cvs", [3, 1024])
    lngb = din("lngb", [4, D])
    ropec = din("ropec", [3, 128, 512])
    ropes = din("ropes", [3, 128, 512])
    flags = din("flags", [128, 3])
    identd = din("identd", [128, 128])
    masks = din("masks", [3, 128, 128])
    msc = din("msc", [16, 128, 128])
    msn = din("msn", [128, 128])

    y_main = dout("y_main", [9, 128, D])
    kwin_p = dout("kwin_p", [128, 256])
    vwin_p = dout("vwin_p", [128, 256])
    conv_p = dout("conv_p", [30, 1024])
    kwin_s = dout("kwin_s", [16, 128, 256])
    vwin_s = dout("vwin_s", [16, 128, 256])
    conv_s = dout("conv_s", [16, 30, 1024])

    with ExitStack() as st:
        S = Sync(nc, st)
        T = lambda name, shape, dt=F32: st.enter_context(nc.sbuf_tensor(name, list(shape), dt))
        ps = [st.enter_context(nc.psum_tensor(f"ps{i}", [128, 512], F32)) for i in range(8)]
        bank_ctr = [0]

        def nb():
            b = bank_ctr[0] % 6
            bank_ctr[0] += 1
            return b

        def PS(b):
            return ("ps", b)

        def stage(name, **tiles):
            for k, (ap, shape, dt, rname) in tiles.items():
                if k in dumps:
                    d = nc.dram_tensor("dbg_" + k, list(shape), dt, kind="ExternalOutput").ap()
                    S.dma("sp", "st", d, ap, reads=[rname])
            if stop == name:
                S.stopped = True

        identf = T("identf", [128, 128])
        identb = T("identb", [128, 128], BF16)
        onesf = T("onesf", [128, 128])
        onesb = T("onesb", [128, 64], BF16)
        mk = T("mk", [128, 3, 128], BF16)
        mksc = T("mksc", [128, 16, 128], BF16)
        mksn = T("mksn", [128, 128], BF16)
        flg = T("flg", [128, 3])
        eps_t = T("eps_t", [128, 1])
        S.dma("sp", "ld", identf[:], identd, writes=["identf"])
        S.dma("pool", "wld", identb[:], identd, writes=["identb"])
        S.dma("sp", "ld", flg[:], flags, writes=["flg"])
        perm = T("perm", [128, 5, 128])
        S.dma("sp", "ld", perm[:], permd.rearrange("i p c -> p i c"), writes=["perm"])
        S.op("dve", lambda e: e.memset(onesf[:], 1.0), writes=["onesf"])
        S.op("dve", lambda e: e.memset(onesb[:], 1.0), writes=["onesb"])
        S.op("dve", lambda e: e.memset(eps_t[:], EPS), writes=["eps_t"])

        NRING = 4
        ring = [T(f"ring{i}", [128, RING], BF16) for i in range(NRING)]
        ring_ctr = [0]

        def wload(src, nel):
            i = ring_ctr[0] % NRING
            ring_ctr[0] += 1
            S.dma("pool", "wld", ring[i][:, 0:nel], src, writes=[("ring", i)])
            return i

        xs = T("xs", [128, D])
        zt = T("zt", [128, D], BF16)
        uT = T("uT", [128, 16, 512], BF16)
        mergedT = T("mergedT", [128, 16, 384], BF16)
        tA = [T(f"tA{i}", [128, 512]) for i in range(2)]
        tB = [T(f"tB{i}", [128, 512]) for i in range(2)]
        pTb = [T(f"pTb{i}", [128, 512], BF16) for i in range(2)]
        rc = T("rc", [128, 256])
        stats = T("stats", [128, 3, 4, 6])
        mv = T("mv", [128, 3, 2])
        rstd = T("rstd", [128, 3, 1])
        nmr = T("nmr", [128, 3, 1])
        csT = T("csT", [128, 16, 17], BF16)
        modT = T("modT", [128, 6, 16, 17])
        esT = T("esT", [128, 8])
        bT1 = T("bT1", [128, 96])
        tmp_ctr = [0]

        def tmpi():
            i = tmp_ctr[0] % 2
            tmp_ctr[0] += 1
            return i

        def small_T(src_ap, nrows, name, ncol_chunks, tmp, tres):
            dst = T(name, [128, ncol_chunks, nrows])
            S.dma("sp", "ld", tmp, src_ap, writes=[tres])
            for c in range(ncol_chunks):
                b = nb()
                S.op("pe", lambda e, b=b, c=c: e.transpose(out=ps[b][:, 0:nrows], in_=tmp[:, c * 128:(c + 1) * 128],
                                                           identity=identf[0:nrows, 0:nrows]),
                     reads=[tres, "identf"], writes=[PS(b)])
                S.op("dve", lambda e, b=b, c=c: e.tensor_copy(out=dst[:, c, :], in_=ps[b][:, 0:nrows]),
                     reads=[], writes=[PS(b), name])
            return dst

        bT = small_T(bada, 96, "bT", 1, xs[0:96, 0:128], "xs")
        wdwT = small_T(wdw, 31, "wdwT", 8, xs[0:31, 0:1024], "xs")
        cvT = small_T(cvs, 3, "cvT", 8, xs[0:3, 0:1024], "xs")
        skT = small_T(sinks8, 8, "skT", 1, xs[0:8, 0:128], "xs")
        S.op("act", lambda e: e.activation(out=esT[:], in_=skT[:, 0, :], func=AF.Exp), reads=["skT"], writes=["esT"])
        S.op("dve", lambda e: e.tensor_scalar(out=bT1[:], in0=bT[:, 0, :], scalar1=1.0, scalar2=None, op0=ALU.add),
             reads=["bT"], writes=["bT1"])

        cld = xs[0:17, :]
        S.dma("sp", "ld", cld, cvec, writes=["xs"])
        S.op("act", lambda e: e.activation(out=cld, in_=cld, func=AF.Silu), writes=["xs"])
        for kc in range(16):
            b = nb()
            S.op("pe", lambda e, b=b, kc=kc: e.transpose(out=ps[b][:, 0:17], in_=cld[:, kc * 128:(kc + 1) * 128],
                                                         identity=identf[0:17, 0:17]),
                 reads=["xs", "identf"], writes=[PS(b)])
            S.op("dve", lambda e, b=b, kc=kc: e.tensor_copy(out=csT[:, kc, :], in_=ps[b][:, 0:17]),
                 writes=[PS(b), "csT"])

        ada_pending = []

        def ada_blk(blk):
            ri = wload(wa[blk], 4096)
            rv = ring[ri][:, 0:4096].rearrange("p (k c) -> p k c", c=256)
            grp = blk // 8
            for jj in range(2):
                ch = (blk % 8) * 2 + jj
                b = nb()
                for kc in range(16):
                    S.op("pe", lambda e, b=b, kc=kc, jj=jj: e.matmul(ps[b][:, 0:17], lhsT=rv[:, kc, jj * 128:(jj + 1) * 128],
                                                                     rhs=csT[:, kc, :], start=(kc == 0), stop=(kc == 15)),
                         reads=[("ring", ri), "csT"], writes=[PS(b)], signal=(kc == 15))
                bsrc = bT1 if grp in (1, 4) else bT[:, 0, :]
                col = grp * 16 + ch
                S.op("act", lambda e, b=b, ch=ch, bsrc=bsrc, col=col: e.activation(
                    out=modT[:, grp, ch, :], in_=ps[b][:, 0:17], func=AF.Identity, bias=bsrc[:, col:col + 1], scale=1.0),
                    reads=["bT", "bT1"], writes=[PS(b), "modT"])

        try:
            stage("const", bT=(bT[:, 0, :], [128, 96], F32, "bT"), wdwT=(wdwT[:].rearrange("p a b -> p (a b)"), [128, 248], F32, "wdwT"),
                  esT=(esT[:], [128, 8], F32, "esT"), csT=(csT[:].rearrange("p a b -> p (a b)"), [128, 272], BF16, "csT"))
            for blk in range(16):
                ada_blk(blk)
            ada_pending.extend(range(16, 48))
            stage("ada", modT=(modT[:].rearrange("p a b c -> p (a b c)"), [128, 6 * 16 * 17], F32, "modT"))
        except _Stop:
            S.finish("sp")
            S.emit()
            return nc

        def ln_stats(src, rname, ti=0):
            for c in range(4):
                S.op("dve", lambda e, c=c: e.bn_stats(out=stats[:, ti, c, :], in_=src[:, c * 512:(c + 1) * 512]),
                     reads=[rname], writes=[("stats", ti)])
            S.op("dve", lambda e: e.bn_aggr(out=mv[:, ti, :], in_=stats[:, ti, :, :]), reads=[("stats", ti)], writes=[("mv", ti)])
            S.op("act", lambda e: e.activation(out=rstd[:, ti, :], in_=mv[:, ti, 1:2], func=AF.Sqrt, bias=eps_t[:, 0:1], scale=1.0),
                 reads=[("mv", ti), "eps_t"], writes=[("rstd", ti)])
            S.op("dve", lambda e: e.reciprocal(out=rstd[:, ti, :], in_=rstd[:, ti, :]), reads=[("rstd", ti)], writes=[("rstd", ti)])
            S.op("dve", lambda e: e.tensor_scalar(out=nmr[:, ti, :], in0=mv[:, ti, 0:1], scalar1=rstd[:, ti, 0:1], scalar2=-1.0,
                                                  op0=ALU.mult, op1=ALU.mult), reads=[("mv", ti), ("rstd", ti)], writes=[("nmr", ti)])

        def ln_z(src, rname, ti, ztile, zname):
            S.op("act", lambda e: e.activation(out=ztile[:], in_=src, func=AF.Identity, bias=nmr[:, ti, 0:1], scale=rstd[:, ti, 0:1]),
                 reads=[rname, ("nmr", ti), ("rstd", ti)], writes=[zname])

        def ln_mod_T(src, rname, col0, gsh, gsc, sample):
            ln_stats(src, rname, 0)
            ln_z(src, rname, 0, zt, "zt")
            ln_T(zt, "zt", col0, gsh, gsc, sample)

        def ln_T(zt, zname, col0, gsh, gsc, sample, split=False):
            for half in range(2):
                b = nb()
                pv = ps[b][:].bitcast(BF16).rearrange("p (k c) -> p k c", c=128)[:, 0:8, :]
                for k8 in range(8):
                    kc = half * 8 + k8
                    S.op("pe", lambda e, pv=pv, k8=k8, kc=kc: e.transpose(out=pv[:, k8, :], in_=zt[:, kc * 128:(kc + 1) * 128],
                                                                         identity=identb[:]),
                         reads=[zname, "identb"], writes=[PS(b)], signal=(k8 == 7))
                for k8 in range(8):
                    kc = half * 8 + k8
                    if not sample and split and half == 1:
                        S.op("dve", lambda e, pv=pv, k8=k8, kc=kc: e.tensor_scalar(
                            out=uT[:, kc, col0:col0 + 128], in0=pv[:, k8, :], scalar1=modT[:, gsc, kc, 0:1],
                            scalar2=modT[:, gsh, kc, 0:1], op0=ALU.mult, op1=ALU.add),
                            reads=["modT"], writes=[PS(b), "uT"])
                    elif not sample:
                        S.op("act", lambda e, pv=pv, k8=k8, kc=kc: e.activation(
                            out=uT[:, kc, col0:col0 + 128], in_=pv[:, k8, :], func=AF.Identity,
                            bias=modT[:, gsh, kc, 0:1], scale=modT[:, gsc, kc, 0:1]),
                            reads=["modT"], writes=[PS(b), "uT"])
                    else:
                        i = tmpi()
                        t3 = tA[i][:, 0:128].rearrange("p (s t) -> p s t", t=8)
                        S.op("dve", lambda e, pv=pv, k8=k8, kc=kc, t3=t3: e.tensor_tensor(
                            out=t3, in0=pv[:, k8, :].rearrange("p (s t) -> p s t", t=8),
                            in1=modT[:, gsc, kc, 1:17].unsqueeze(2).to_broadcast([128, 16, 8]), op=ALU.mult),
                            reads=["modT"], writes=[PS(b), ("tA", i)])
                        S.op("dve", lambda e, kc=kc, t3=t3: e.tensor_tensor(
                            out=uT[:, kc, col0:col0 + 128].rearrange("p (s t) -> p s t", t=8), in0=t3,
                            in1=modT[:, gsh, kc, 1:17].unsqueeze(2).to_broadcast([128, 16, 8]), op=ALU.add),
                            reads=["modT", ("tA", i)], writes=["uT"])

        def gate_evac(b, dst, dname, grp, j, ncol, has_sample):
            npr = ncol - 128 if has_sample else ncol
            S.op("act", lambda e: e.activation(out=dst[:, 0:npr], in_=ps[b][:, 0:npr], func=AF.Identity, bias=0.0,
                                               scale=modT[:, grp, j, 0:1]),
                 reads=["modT"], writes=[PS(b), dname])
            if has_sample:
                S.op("dve", lambda e: e.tensor_tensor(
                    out=dst[:, npr:ncol].rearrange("p (s t) -> p s t", t=8),
                    in0=ps[b][:, npr:ncol].rearrange("p (s t) -> p s t", t=8),
                    in1=modT[:, grp, j, 1:17].unsqueeze(2).to_broadcast([128, 16, 8]), op=ALU.mult),
                    reads=["modT"], writes=[PS(b), dname])

        def resid_add(XR, src, sname, j, ntile):
            for t in range(ntile):
                b = nb()
                S.op("pe", lambda e, b=b, t=t: e.transpose(out=ps[b][:, 0:128], in_=src[:, t * 128:(t + 1) * 128], identity=identf[:]),
                     reads=[sname, "identf"], writes=[PS(b)])
                S.op("dve", lambda e, b=b, t=t: e.scalar_tensor_tensor(
                    out=XR[:, t, j * 128:(j + 1) * 128], in0=XR[:, t, j * 128:(j + 1) * 128], scalar=ALPHA,
                    in1=ps[b][:, 0:128], op0=ALU.mult, op1=ALU.add),
                    reads=[], writes=[PS(b), ("XR", t)])

        def s2_tiles(hidx, mains, has_s, split=False):
            out = []
            srcs = [hidx] + list(mains)
            for ct, xi in enumerate(srcs):
                def fa(ct=ct, xi=xi):
                    S.dma("sp", "ld", xs[:], xin[xi], writes=["xs"])
                    ln_stats(xs[:], "xs", 0)
                    ln_z(xs[:], "xs", 0, zt, "zt")

                def fb(ct=ct):
                    ln_T(zt, "zt", 128 * ct, 0, 1, has_s and ct == 3, split)
                out.append(fa)
                out.append(fb)
            return out

        def drain_ada(n):
            for _ in range(n):
                if ada_pending:
                    ada_blk(ada_pending.pop(0))

        R0 = nc.sbuf_bytes_remaining
        spans = {}
        late_spans = {}
        pre_s10_snap = {}

        def alloc(stack, key, name, shape, dt):
            rem0 = nc.sbuf_bytes_remaining
            t = stack.enter_context(nc.sbuf_tensor(name, list(shape), dt))
            spans[key] = (R0 - rem0, R0 - nc.sbuf_bytes_remaining)
            return t

        def early_snap(pi, keys):
            if pi == 0 or (pi - 1) not in pre_s10_snap:
                return None
            for k in keys:
                a0, a1 = spans[k]
                for (b0, b1) in late_spans[pi - 1]:
                    if a0 < b1 and b0 < a1:
                        return None
            return pre_s10_snap[pi - 1]

        def run_pass(pi, hidx, mains, has_s, s2_next):
            NT = 3
            NC_ = 128 * NT
            npr_t = NT - 1 if has_s else NT
            npc = npr_t * 128
            stage(f"s2_{pi}", **{f"uT{pi}": (uT[:].rearrange("p a b -> p (a b)"), [128, 8192], BF16, "uT")})
            with ExitStack() as s1:
                T1 = lambda name, shape, dt=F32: s1.enter_context(nc.sbuf_tensor(f"{name}{pi}", list(shape), dt))
                gluT = T1("gluT", [128, 8, 512], BF16)
                gluTf = T1("gluTf", [128, 8, 256])
                attnT = T1("attnT", [128, 8, 384], BF16)
                convT = T1("convT", [128, 8, 384], BF16)
                sgT = T1("sgT", [128, 2, 16, 384], BF16)
                kTf = alloc(s1, "kTf", f"kTf{pi}", [128, 4, 256], F32)
                vf = alloc(s1, "vf", f"vf{pi}", [128, 4, 256], F32)
                S.fence(["gluT", "gluTf", "attnT", "convT", "sgT"])
                S.fence(["kTf", "vf"], early_snap(pi, ["kTf", "vf"]))
                with ExitStack() as sA:
                    TA_ = lambda name, shape, dt=F32: sA.enter_context(nc.sbuf_tensor(f"{name}{pi}", list(shape), dt))
                    qT = alloc(sA, "qT", f"qT{pi}", [128, 8, 384], BF16)
                    kT = alloc(sA, "kT", f"kT{pi}", [128, 4, 512], BF16)
                    vb = alloc(sA, "vb", f"vb{pi}", [128, 4, 256], BF16)
                    sR = ExitStack()
                    rcos = alloc(sR, "rcos", f"rcos{pi}", [128, 512], F32)
                    rsin = alloc(sR, "rsin", f"rsin{pi}", [128, 512], F32)
                    kraw = alloc(sR, "kraw", f"kraw{pi}", [128, 2, 512], F32)
                    S.fence(["qT", "kT", "vb", "rcos", "rsin", ("kraw", 0), ("kraw", 1)],
                            early_snap(pi, ["qT", "kT", "vb", "rcos", "rsin", "kraw"]))
                    S.dma("sp", "ld", rcos[:], ropec[pi], writes=["rcos"])
                    S.dma("sp", "ld", rsin[:], ropes[pi], writes=["rsin"])

                    def proj(ri, col_lo, ncol, c0, b):
                        rv = ring[ri][:, 0:4096].rearrange("p (k c) -> p k c", c=256)
                        for kc in range(16):
                            S.op("pe", lambda e, kc=kc: e.matmul(ps[b][:, 0:ncol], lhsT=rv[:, kc, c0:c0 + 128],
                                                                 rhs=uT[:, kc, col_lo:col_lo + ncol], start=(kc == 0), stop=(kc == 15)),
                                 reads=[("ring", ri), "uT"], writes=[PS(b)], signal=(kc == 15))

                    def rope_evac(ba, bb, col_lo, ncol, dst, dname, dstf=None):
                        i = tmpi()
                        S.op("dve", lambda e: e.tensor_tensor(out=tA[i][:, 0:ncol], in0=ps[ba][:, 0:ncol], in1=rcos[:, col_lo:col_lo + ncol], op=ALU.mult),
                             reads=["rcos"], writes=[PS(ba), ("tA", i)])
                        S.op("dve", lambda e: e.tensor_tensor(out=tB[i][:, 0:ncol], in0=ps[bb][:, 0:ncol], in1=rsin[:, col_lo:col_lo + ncol], op=ALU.mult),
                             reads=["rsin"], writes=[PS(bb), ("tB", i)])
                        S.op("dve", lambda e: e.tensor_tensor(out=dst, in0=tA[i][:, 0:ncol], in1=tB[i][:, 0:ncol], op=ALU.add),
                             reads=[("tA", i), ("tB", i)], writes=[dname])
                        if dstf is not None:
                            S.op("dve", lambda e: e.tensor_tensor(out=dstf, in0=tA[i][:, 256:512], in1=tB[i][:, 256:512], op=ALU.add),
                                 reads=[("tA", i), ("tB", i)], writes=["kTf"])

                    def k_unit():
                        ri = wload(w1[0], 4096)
                        bA, bB = nb(), nb()
                        proj(ri, 0, 512, 0, bA)
                        proj(ri, 0, 512, 128, bB)
                        S.op("act", lambda e: e.activation(out=kraw[:, 0, :], in_=ps[bA][:], func=AF.Identity, bias=0.0, scale=1.0),
                             writes=[PS(bA), ("kraw", 0)])
                        S.op("act", lambda e: e.activation(out=kraw[:, 1, :], in_=ps[bB][:], func=AF.Identity, bias=0.0, scale=1.0),
                             writes=[PS(bB), ("kraw", 1)])
                        for g in range(4):
                            ba, bb = nb(), nb()
                            S.op("pe", lambda e, g=g, ba=ba: e.matmul(ps[ba][:, 0:512], lhsT=perm[:, 1 + g % 2, :], rhs=kraw[:, g // 2, :], start=True, stop=True),
                                 reads=["perm", ("kraw", g // 2)], writes=[PS(ba)])
                            S.op("pe", lambda e, g=g, bb=bb: e.matmul(ps[bb][:, 0:512], lhsT=perm[:, 3 + g % 2, :], rhs=kraw[:, g // 2, :], start=True, stop=True),
                                 reads=["perm", ("kraw", g // 2)], writes=[PS(bb)])
                            rope_evac(ba, bb, 0, 512, kT[:, g, :], "kT", kTf[:, g, :])

                    k_unit()

                    def v_unit():
                        ri = wload(w1[1], 4096)
                        rvv = ring[ri][:, 0:4096].rearrange("p (k c) -> p k c", c=256)
                        for ct in range(4):
                            b = nb()
                            for kc in range(16):
                                S.op("pe", lambda e, kc=kc, ct=ct, b=b: e.matmul(ps[b][:, 0:256], lhsT=uT[:, kc, ct * 128:(ct + 1) * 128],
                                                                             rhs=rvv[:, kc, :], start=(kc == 0), stop=(kc == 15)),
                                     reads=[("ring", ri), "uT"], writes=[PS(b)], signal=(kc == 15))
                            S.op("act", lambda e, ct=ct, b=b: e.activation(out=vb[:, ct, :], in_=ps[b][:, 0:256], func=AF.Identity, bias=0.0, scale=1.0),
                                 writes=[PS(b), "vb"])
                            S.op("dve", lambda e, ct=ct, b=b: e.tensor_copy(out=vf[:, ct, :], in_=ps[b][:, 0:256]), writes=[PS(b), "vf"])

                    v_unit()
                    def q_unit(iq):
                        ri = wload(w1[2 + iq], 4096)
                        for jj in range(2):
                            c = 2 * iq + jj
                            ba = nb()
                            proj(ri, 128, 384, jj * 128, ba)
                            S.op("act", lambda e, jj=jj, ba=ba: e.activation(out=kraw[:, jj, 0:384], in_=ps[ba][:, 0:384], func=AF.Identity, bias=0.0, scale=1.0),
                                 writes=[PS(ba), ("kraw", jj)])
                            bb = nb()
                            S.op("pe", lambda e, jj=jj, bb=bb: e.matmul(ps[bb][:, 0:384], lhsT=perm[:, 0, :], rhs=kraw[:, jj, 0:384], start=True, stop=True),
                                 reads=["perm", ("kraw", jj)], writes=[PS(bb)])
                            rope_evac(ba, bb, 128, 384, qT[:, c, :], "qT")

                    for iq in range(4):
                        q_unit(iq)

                    def glu_unit(i8):
                        ri = wload(w1[6 + i8], 4096)
                        ba, bb = nb(), nb()
                        proj(ri, 0, 512, 0, ba)
                        proj(ri, 0, 512, 128, bb)
                        i = tmpi()
                        S.op("act", lambda e: e.activation(out=tA[i][:], in_=ps[bb][:], func=AF.Sigmoid),
                             writes=[PS(bb), ("tA", i)])
                        S.op("dve", lambda e: e.tensor_tensor(out=gluT[:, i8, :], in0=ps[ba][:], in1=tA[i][:], op=ALU.mult),
                             reads=[("tA", i)], writes=[PS(ba), "gluT"])
                        S.op("dve", lambda e: e.tensor_tensor(out=gluTf[:, i8, :], in0=ps[ba][:, 256:512], in1=tA[i][:, 256:512], op=ALU.mult),
                             reads=[("tA", i)], writes=[PS(ba), "gluTf"])
                        S.op("dve", lambda e: e.tensor_scalar(out=gluT[:, i8, 0:128], in0=gluT[:, i8, 0:128], scalar1=flg[:, pi:pi + 1],
                                                              scalar2=None, op0=ALU.mult), reads=["flg"], writes=["gluT"])

                    for i8 in range(8):
                        glu_unit(i8)

                    stage(f"s3_{pi}", **{f"kT{pi}": (kT[:].rearrange("p a b -> p (a b)"), [128, 2048], BF16, "kT"), f"qT{pi}": (qT[:].rearrange("p a b -> p (a b)"), [128, 3072], BF16, "qT"), f"vb{pi}": (vb[:].rearrange("p a b -> p (a b)"), [128, 1024], BF16, "vb"), f"gluT{pi}": (gluT[:].rearrange("p a b -> p (a b)"), [128, 4096], BF16, "gluT"), f"ring0_{pi}": (ring[0][:], [128, 6144], BF16, ("ring", 0)), f"ring1_{pi}": (ring[1][:], [128, 6144], BF16, ("ring", 1)), f"rcos{pi}": (rcos[:], [128, 512], F32, "rcos"), f"tA{pi}": (tA[0][:], [128, 512], F32, ("tA", 0))})
                    gate_pending = list(range(16))

                    def gate_unit(j):
                        ri = wload(w2[j][:, 0:4096], 4096)
                        rv = ring[ri][:, 0:4096].rearrange("p (k c) -> p k c", c=128)
                        bA, bB = nb(), nb()
                        for kc in range(16):
                            S.op("pe", lambda e, kc=kc: e.matmul(ps[bA][:, 0:NC_], lhsT=rv[:, kc, :], rhs=uT[:, kc, 128:512], start=(kc == 0), stop=(kc == 15)),
                                 reads=[("ring", ri), "uT"], writes=[PS(bA)], signal=(kc == 15))
                        for kc in range(16):
                            S.op("pe", lambda e, kc=kc: e.matmul(ps[bB][:, 0:NC_], lhsT=rv[:, 16 + kc, :], rhs=uT[:, kc, 128:512], start=(kc == 0), stop=(kc == 15)),
                                 reads=[("ring", ri), "uT"], writes=[PS(bB)], signal=(kc == 15))
                        S.op("act", lambda e: e.activation(out=sgT[:, 0, j, :], in_=ps[bA][:, 0:NC_], func=AF.Sigmoid), writes=[PS(bA), "sgT"])
                        S.op("act", lambda e: e.activation(out=sgT[:, 1, j, :], in_=ps[bB][:, 0:NC_], func=AF.Sigmoid), writes=[PS(bB), "sgT"])

                    def drain_gates(n):
                        for _ in range(n):
                            if gate_pending:
                                gate_unit(gate_pending.pop(0))

                    sR.close()
                    if has_s:
                        kcb = TA_("kcb", [128, 16, 256], BF16)
                        kcT = TA_("kcT", [128, 4, 16, 128], BF16)
                        vcb = TA_("vcb", [128, 16, 256], BF16)
                        S.fence(["kcb", "kcT", "vcb"])
                    if pi == 0:
                        S.dma("pool", "wld", mk[:], masks.rearrange("i p c -> p i c"), writes=["mk"])
                        for q in range(4):
                            S.dma("pool", "wld", mksc[:, q * 4:(q + 1) * 4, :], msc[q * 4:(q + 1) * 4].rearrange("i p c -> p i c"), writes=["mksc"])
                        S.dma("pool", "wld", mksn[:], msn, writes=["mksn"])
                    def attn_tile(qc0, keytiles):
                        nk = len(keytiles)
                        for cp in range(4):
                            bo, bd = 6, 7
                            def qk_exp(ki):
                                kfn, vfn, mask_ap, kres = keytiles[ki]
                                bE, bO = nb(), nb()
                                for bank, hhs in ((bE, (0, 2)), (bO, (1, 3))):
                                    for n_, hh in enumerate(hhs):
                                        S.op("pe", lambda e, n_=n_, bank=bank, mask_ap=mask_ap: e.matmul(
                                            ps[bank][:, n_ * 128:(n_ + 1) * 128], lhsT=identb[:], rhs=mask_ap, start=(n_ == 0), stop=False,
                                            skip_group_check=True),
                                            reads=["identb", "mk", "mksc", "mksn"], writes=[PS(bank)], signal=False)
                                    for n_, hh in enumerate(hhs):
                                        h = 4 * cp + hh
                                        c, hf = h // 2, h % 2
                                        kap = kfn(cp)
                                        S.op("pe", lambda e, n_=n_, bank=bank, kap=kap, c=c, hf=hf: e.matmul(
                                            ps[bank][:, n_ * 128:(n_ + 1) * 128], lhsT=kap[hf * 64:(hf + 1) * 64, :],
                                            rhs=qT[hf * 64:(hf + 1) * 64, c, qc0:qc0 + 128], start=False, stop=True, skip_group_check=True),
                                            reads=[kres, "qT"], writes=[PS(bank)], signal=(n_ == 1))
                                i = tmpi()
                                S.op("act", lambda e, i=i, bE=bE: e.activation(out=pTb[i][:, 0:256], in_=ps[bE][:, 0:256], func=AF.Exp, scale=0.125),
                                     writes=[PS(bE), ("pTb", i)])
                                S.op("act", lambda e, i=i, bO=bO: e.activation(out=pTb[i][:, 256:512], in_=ps[bO][:, 0:256], func=AF.Exp, scale=0.125),
                                     writes=[PS(bO), ("pTb", i)])
                                return i

                            def pv(ki, i):
                                kfn, vfn, mask_ap, kres = keytiles[ki]
                                for hh in range(4):
                                    h = 4 * cp + hh
                                    c, hf = h // 2, h % 2
                                    cl = (c % 2) * 128
                                    pc = {0: 0, 2: 128, 1: 256, 3: 384}[hh]
                                    vap = vfn(cp)
                                    S.op("pe", lambda e, pc=pc, i=i, vap=vap, hf=hf, cl=cl, ki=ki, hh=hh: e.matmul(
                                        ps[bo][hf * 64:(hf + 1) * 64, cl:cl + 128], lhsT=vap, rhs=pTb[i][:, pc:pc + 128],
                                        start=(ki == 0 and hh < 2), stop=(ki == nk - 1), skip_group_check=True),
                                        reads=[("pTb", i), "vb", "vcb"], writes=[PS(bo)], signal=False)
                                    S.op("pe", lambda e, pc=pc, i=i, hf=hf, cl=cl, ki=ki, hh=hh: e.matmul(
                                        ps[bd][hf * 64:(hf + 1) * 64, cl:cl + 128], lhsT=onesb[:], rhs=pTb[i][:, pc:pc + 128],
                                        start=(ki == 0 and hh < 2), stop=(ki == nk - 1), skip_group_check=True),
                                        reads=[("pTb", i), "onesb"], writes=[PS(bd)], signal=(hh == 3))

                            icur = qk_exp(0)
                            for ki in range(nk):
                                inext = qk_exp(ki + 1) if ki + 1 < nk else None
                                pv(ki, icur)
                                icur = inext
                            for cc in range(2):
                                c = 2 * cp + cc
                                S.op("dve", lambda e, cc=cc, c=c: e.tensor_scalar(out=rc[:, cc * 128:(cc + 1) * 128], in0=ps[bd][:, cc * 128:(cc + 1) * 128],
                                                                               scalar1=esT[:, c:c + 1], scalar2=None, op0=ALU.add),
                                     reads=["esT"], writes=[PS(bd), "rc"])
                            S.op("dve", lambda e: e.reciprocal(out=rc[:, 0:256], in_=rc[:, 0:256]), reads=["rc"], writes=["rc"])
                            S.op("dve", lambda e, cp=cp: e.tensor_tensor(
                                out=attnT[:, 2 * cp:2 * cp + 2, qc0:qc0 + 128], in0=ps[bo][:, 0:256].rearrange("p (c q) -> p c q", c=2),
                                in1=rc[:, 0:256].rearrange("p (c q) -> p c q", c=2), op=ALU.mult),
                                reads=["rc"], writes=[PS(bo), "attnT"])
                            drain_gates(1)
                            drain_ada(1)

                    for t in range(npr_t):
                        ct = t + 1
                        mprev = mk[:, 2, :] if (pi == 0 and t == 0) else mk[:, 0, :]
                        kts = [(lambda g, ct=ct: kT[:, g, (ct - 1) * 128:ct * 128], lambda g, ct=ct: vb[:, ct - 1, g * 64:(g + 1) * 64], mprev, "kT"),
                               (lambda g, ct=ct: kT[:, g, ct * 128:(ct + 1) * 128], lambda g, ct=ct: vb[:, ct, g * 64:(g + 1) * 64], mk[:, 1, :], "kT")]
                        attn_tile(t * 128, kts)
                    if has_s:
                        for q in range(4):
                            S.dma("pool", "wld", kcb[:, q * 4:(q + 1) * 4, :], cachek[q * 4:(q + 1) * 4].rearrange("s p c -> p s c"), writes=["kcb"])
                        for q in range(4):
                            S.dma("pool", "wld", vcb[:, q * 4:(q + 1) * 4, :], cachev[q * 4:(q + 1) * 4].rearrange("s p c -> p s c"), writes=["vcb"])
                        for s in range(16):
                            b = nb()
                            pv = ps[b][:].bitcast(BF16)
                            for g in range(4):
                                for hf in range(2):
                                    S.op("pe", lambda e, g=g, hf=hf, pv=pv, s=s: e.transpose(
                                        out=pv[hf * 64:(hf + 1) * 64, g * 128:(g + 1) * 128],
                                        in_=kcb[:, s, g * 64:(g + 1) * 64], identity=identb[:]),
                                        reads=["kcb", "identb"], writes=[PS(b)], signal=(g == 3 and hf == 1))
                            S.op("act", lambda e, s=s, pv=pv: e.activation(out=kcT[:, :, s, :], in_=pv[:, 0:512].rearrange("p (g k) -> p g k", g=4),
                                                                          func=AF.Identity, bias=0.0, scale=1.0), writes=[PS(b), "kcT"])
                        kts = []
                        for s in range(16):
                            kts.append((lambda g, s=s: kcT[:, g, s, :], lambda g, s=s: vcb[:, s, g * 64:(g + 1) * 64], mksc[:, s, :], "kcT"))
                        kts.append((lambda g: kT[:, g, 384:512], lambda g: vb[:, 3, g * 64:(g + 1) * 64], mksn[:], "kT"))
                        attn_tile(256, kts)
                stage(f"s4_{pi}", **{f"attnT{pi}": (attnT[:].rearrange("p a b -> p (a b)"), [128, 3072], BF16, "attnT")})
                with ExitStack() as sB:
                    TB_ = lambda name, shape, dt=F32: sB.enter_context(nc.sbuf_tensor(f"{name}{pi}", list(shape), dt))
                    ycv = TB_("ycv", [128, 8, 384])
                    diag2 = [TB_("diagA", [128, 31, 128], BF16), TB_("diagB", [128, 31, 128], BF16)]
                    cm = TB_("cm", [128, 384])
                    cr = TB_("cr", [128, 384])
                    S.fence([("ycv", 0), ("ycv", 1), ("ycv", 2), ("ycv", 3), ("ycv", 4), ("ycv", 5), ("ycv", 6), ("ycv", 7), ("diag", 0), ("diag", 1), "cm", "cr", "gs", ("stl", 0), ("stl", 1)])
                    if has_s:
                        gs = TB_("gs", [128, 8, 16, 38], BF16)
                        stl = TB_("stl", [120, 2, 1024], BF16)
                        for q4 in range(4):
                            S.dma("pool", "wld", stl[:, q4 % 2, :], stconv[q4 * 4:(q4 + 1) * 4].rearrange("s t c -> (s t) c"), writes=[("stl", q4 % 2)])
                            b = nb()
                            pv = ps[b][:].bitcast(BF16)
                            for i8 in range(8):
                                S.op("pe", lambda e, i8=i8, pv=pv, q4=q4: e.transpose(out=pv[:, i8 * 120:(i8 + 1) * 120], in_=stl[:, q4 % 2, i8 * 128:(i8 + 1) * 128],
                                                                                identity=identb[0:120, 0:120]),
                                     reads=[("stl", q4 % 2), "identb"], writes=[PS(b)], signal=(i8 == 7))
                            for i8 in range(8):
                                S.op("act", lambda e, i8=i8, pv=pv, q4=q4: e.activation(
                                    out=gs[:, i8, q4 * 4:(q4 + 1) * 4, 0:30], in_=pv[:, i8 * 120:(i8 + 1) * 120].rearrange("p (s t) -> p s t", s=4),
                                    func=AF.Identity, bias=0.0, scale=1.0), writes=[PS(b), "gs"])
                        for i8 in range(8):
                            S.op("dve", lambda e, i8=i8: e.tensor_copy(out=gs[:, i8, :, 30:38], in_=gluT[:, i8, 384:512].rearrange("p (s t) -> p s t", t=8)),
                                 reads=["gluT"], writes=["gs"])
                    bS, bQ = 6, 7
                    pend5 = []
                    for i8 in range(8):
                        diag = diag2[i8 % 2]
                        dres = ("diag", i8 % 2)
                        S.op("dve", lambda e, i8=i8, diag=diag: e.tensor_tensor(out=diag[:], in0=identf[:].unsqueeze(1).to_broadcast([128, 31, 128]),
                                                                     in1=wdwT[:, i8, :].unsqueeze(2).to_broadcast([128, 31, 128]), op=ALU.mult),
                             reads=["identf", "wdwT"], writes=[dres])
                        drain_ada(1)
                        b = nb()
                        for j in range(31):
                            S.op("pe", lambda e, j=j, i8=i8, b=b, diag=diag: e.matmul(ps[b][:, 0:npc], lhsT=diag[:, j, :], rhs=gluT[:, i8, 98 + j:98 + j + npc],
                                                                      start=(j == 0), stop=(j == 30)),
                                 reads=[dres, "gluT"], writes=[PS(b)], signal=(j == 30 and not has_s))
                        if has_s:
                            for j in range(31):
                                S.op("pe", lambda e, j=j, i8=i8, b=b, diag=diag: e.matmul(ps[b][:, npc:npc + 128], lhsT=diag[:, j, :], rhs=gs[:, i8, :, j:j + 8],
                                                                          start=False, stop=(j == 30), skip_group_check=True),
                                     reads=[dres, "gs"], writes=[PS(b)], signal=(j == 30))
                        S.op("act", lambda e, i8=i8, b=b: e.activation(out=ycv[:, i8, :], in_=ps[b][:, 0:NC_], func=AF.Identity, bias=cvT[:, i8, 0:1], scale=1.0),
                             reads=["cvT"], writes=[PS(b), ("ycv", i8)])
                        i = tmpi()
                        S.op("act", lambda e, i8=i8, i=i: e.activation(out=tA[i][:, 0:NC_], in_=ycv[:, i8, :], func=AF.Square),
                             reads=[("ycv", i8)], writes=[("tA", i)])

                        def stats_mm(i8=i8, i=i):
                            S.op("pe", lambda e: e.matmul(ps[bS][:, 0:NC_], lhsT=onesf[:], rhs=ycv[:, i8, :], start=(i8 == 0), stop=(i8 == 7)),
                                 reads=["onesf", ("ycv", i8)], writes=[PS(bS)], signal=(i8 == 7))
                            S.op("pe", lambda e: e.matmul(ps[bQ][:, 0:NC_], lhsT=onesf[:], rhs=tA[i][:, 0:NC_], start=(i8 == 0), stop=(i8 == 7)),
                                 reads=["onesf", ("tA", i)], writes=[PS(bQ)], signal=True)
                        if pend5:
                            pend5.pop(0)()
                        pend5.append(stats_mm)
                    while pend5:
                        pend5.pop(0)()
                    S.op("act", lambda e: e.activation(out=cm[:], in_=ps[bS][:, 0:NC_], func=AF.Identity, bias=0.0, scale=1.0 / 1024),
                         writes=[PS(bS), "cm"])
                    S.op("dve", lambda e: e.tensor_tensor(out=cr[:], in0=cm[:], in1=cm[:], op=ALU.mult), reads=["cm"], writes=["cr"])
                    S.op("dve", lambda e: e.scalar_tensor_tensor(out=cr[:], in0=ps[bQ][:, 0:NC_], scalar=1.0 / 1024, in1=cr[:],
                                                                 op0=ALU.mult, op1=ALU.subtract), reads=[], writes=[PS(bQ), "cr"])
                    S.op("act", lambda e: e.activation(out=cr[:], in_=cr[:], func=AF.Sqrt, bias=eps_t[:, 0:1], scale=1.0),
                         reads=["eps_t"], writes=["cr"])
                    S.op("dve", lambda e: e.reciprocal(out=cr[:], in_=cr[:]), reads=[], writes=["cr"])
                    for i8 in range(8):
                        S.op("dve", lambda e, i8=i8: e.tensor_tensor(out=ycv[:, i8, :], in0=ycv[:, i8, :], in1=cm[:], op=ALU.subtract),
                             reads=["cm"], writes=[("ycv", i8)])
                        S.op("dve", lambda e, i8=i8: e.tensor_tensor(out=ycv[:, i8, :], in0=ycv[:, i8, :], in1=cr[:], op=ALU.mult),
                             reads=["cr"], writes=[("ycv", i8)])
                        S.op("act", lambda e, i8=i8: e.activation(out=convT[:, i8, :], in_=ycv[:, i8, :], func=AF.Silu,
                                                                  bias=cvT[:, i8, 2:3], scale=cvT[:, i8, 1:2]),
                             reads=[("ycv", i8), "cvT"], writes=["convT"])
                stage(f"s5_{pi}", **{f"convT{pi}": (convT[:].rearrange("p a b -> p (a b)"), [128, 3072], BF16, "convT")})
                if has_s:
                    sO = ExitStack()
                    osb = sO.enter_context(nc.sbuf_tensor("osb", [128, 512], F32))
                    osc = sO.enter_context(nc.sbuf_tensor("osc", [128, 1024], F32))
                    osp = sO.enter_context(nc.sbuf_tensor("osp", [32, 1024], F32))
                    S.fence(["osb", "osc", "osp"])
                    S.dma("sp", "st", kwin_s.rearrange("s p c -> s (p c)")[:, 0:120 * 256], cachek.rearrange("s p c -> s (p c)")[:, 8 * 256:128 * 256])
                    S.dma("sp", "st", vwin_s.rearrange("s p c -> s (p c)")[:, 0:120 * 256], cachev.rearrange("s p c -> s (p c)")[:, 8 * 256:128 * 256])
                    S.dma("sp", "st", conv_s.rearrange("s t c -> s (t c)")[:, 0:22 * 1024], stconv.rearrange("s t c -> s (t c)")[:, 8 * 1024:30 * 1024])
                    for which in range(2):
                        b = nb()
                        for g in range(4):
                            S.op("pe", lambda e, g=g, b=b, which=which: e.transpose(out=ps[b][:, g * 64:(g + 1) * 64],
                                                                                    in_=kTf[0:64, g, which * 128:(which + 1) * 128],
                                                                                    identity=identf[0:64, 0:64]),
                                 reads=["kTf", "identf"], writes=[PS(b)], signal=(g == 3))
                        S.op("dve", lambda e, b=b, which=which: e.tensor_copy(out=osb[:, which * 256:(which + 1) * 256], in_=ps[b][:, 0:256]),
                             writes=[PS(b), "osb"])
                    S.dma("sp", "st", kwin_p, osb[:, 0:256], reads=["osb"])
                    S.dma("sp", "st", vwin_p, vf[:, 2, :], reads=["vf"])
                    for s in range(16):
                        S.dma("sp", "st", kwin_s[s, 120:128, :], osb[s * 8:(s + 1) * 8, 256:512], reads=["osb"])
                        S.dma("sp", "st", vwin_s[s, 120:128, :], vf[s * 8:(s + 1) * 8, 3, :], reads=["vf"])
                    for half in range(2):
                        b = nb()
                        for i4 in range(4):
                            i8 = half * 4 + i4
                            S.op("pe", lambda e, i8=i8, i4=i4, b=b: e.transpose(out=ps[b][:, i4 * 128:(i4 + 1) * 128], in_=gluTf[:, i8, 128:256],
                                                                               identity=identf[:]),
                                 reads=["gluTf", "identf"], writes=[PS(b)], signal=(i4 == 3))
                        S.op("dve", lambda e, half=half, b=b: e.tensor_copy(out=osc[:, half * 512:(half + 1) * 512], in_=ps[b][:]),
                             writes=[PS(b), "osc"])
                        b2 = nb()
                        for i4 in range(4):
                            i8 = half * 4 + i4
                            S.op("pe", lambda e, i8=i8, i4=i4, b2=b2: e.transpose(out=ps[b2][0:32, i4 * 128:(i4 + 1) * 128], in_=gluTf[:, i8, 96:128],
                                                                                 identity=identf[:]),
                                 reads=["gluTf", "identf"], writes=[PS(b2)], signal=(i4 == 3))
                        S.op("dve", lambda e, half=half, b2=b2: e.tensor_copy(out=osp[:, half * 512:(half + 1) * 512], in_=ps[b2][0:32, :]),
                             writes=[PS(b2), "osp"])
                    S.dma("sp", "st", conv_p, osp[2:32, :], reads=["osp"])
                    for s in range(16):
                        S.dma("sp", "st", conv_s[s, 22:30, :], osc[s * 8:(s + 1) * 8, :], reads=["osc"])
                    sO.close()

                def s6_unit(j):
                    ri2 = wload(w2[j][:, 4096:6144], 2048)
                    rv2 = ring[ri2][:, 0:2048].rearrange("p (k c) -> p k c", c=128)
                    bC, bD = nb(), nb()
                    for kc in range(8):
                        S.op("pe", lambda e, kc=kc: e.matmul(ps[bC][:, 0:NC_], lhsT=rv2[:, kc, :], rhs=attnT[:, kc, :], start=(kc == 0), stop=(kc == 7)),
                             reads=[("ring", ri2), "attnT"], writes=[PS(bC)], signal=(kc == 7))
                    for kc in range(8):
                        S.op("pe", lambda e, kc=kc: e.matmul(ps[bD][:, 0:NC_], lhsT=rv2[:, 8 + kc, :], rhs=convT[:, kc, :], start=(kc == 0), stop=(kc == 7)),
                             reads=[("ring", ri2), "convT"], writes=[PS(bD)], signal=(kc == 7))
                    i = tmpi()
                    S.op("dve", lambda e: e.tensor_tensor(out=tA[i][:, 0:NC_], in0=ps[bC][:, 0:NC_], in1=sgT[:, 0, j, :], op=ALU.mult),
                         reads=["sgT"], writes=[PS(bC), ("tA", i)])
                    S.op("dve", lambda e: e.tensor_tensor(out=tB[i][:, 0:NC_], in0=ps[bD][:, 0:NC_], in1=sgT[:, 1, j, :], op=ALU.mult),
                         reads=["sgT"], writes=[PS(bD), ("tB", i)])
                    S.op("dve", lambda e: e.tensor_tensor(out=mergedT[:, j, :], in0=tA[i][:, 0:NC_], in1=tB[i][:, 0:NC_], op=ALU.add),
                         reads=[("tA", i), ("tB", i)], writes=["mergedT"])

                drain_gates(16)
                for j in range(16):
                    s6_unit(j)
                    drain_ada(1)

            stage(f"s6_{pi}", **{f"mergedT{pi}": (mergedT[:].rearrange("p a b -> p (a b)"), [128, 6144], BF16, "mergedT")})
            with ExitStack() as s2b:
                XR = alloc(s2b, "XR", f"XR{pi}", [128, 3, D], F32)
                lnG = alloc(s2b, "lnG", f"lnG{pi}", [128, 2, D], F32)
                hT = s2b.enter_context(nc.sbuf_tensor(f"hT{pi}", [128, 44, 384], BF16))
                late_spans[pi] = [spans["XR"], spans["lnG"]]
                S.fence(["lnG", "hT", ("XR", 0), ("XR", 1), ("XR", 2)])
                for t in range(NT):
                    S.dma("sp", "ld", XR[:, t, :], xin[mains[t]], writes=[("XR", t)])

                def load_lnG(r0):
                    for k in range(2):
                        S.dma("sp", "ld", lnG[:, k, :], lngb[r0 + k:r0 + k + 1, :].broadcast_to([128, D]), writes=["lnG"])

                zt1 = s2b.enter_context(nc.sbuf_tensor(f"zt1_{pi}", [128, D], BF16))
                zt2 = s2b.enter_context(nc.sbuf_tensor(f"zt2_{pi}", [128, D], BF16))
                S.fence(["zt1", "zt2"])
                zts = [(zt, "zt"), (zt1, "zt1"), (zt2, "zt2")]

                def ln_affine_all():
                    for t in range(NT):
                        ln_stats(XR[:, t, :], ("XR", t), t)
                    for t in range(NT):
                        S.op("act", lambda e, t=t: e.activation(out=XR[:, t, :], in_=XR[:, t, :], func=AF.Identity,
                                                                bias=nmr[:, t, 0:1], scale=rstd[:, t, 0:1]),
                             reads=[("nmr", t), ("rstd", t)], writes=[("XR", t)])
                    for t in range(NT):
                        S.op("dve", lambda e, t=t: e.tensor_tensor(out=XR[:, t, :], in0=XR[:, t, :], in1=lnG[:, 0, :], op=ALU.mult),
                             reads=["lnG"], writes=[("XR", t)])
                        S.op("dve", lambda e, t=t: e.tensor_tensor(out=XR[:, t, :], in0=XR[:, t, :], in1=lnG[:, 1, :], op=ALU.add),
                             reads=["lnG"], writes=[("XR", t)])

                def s7_blk(blk):
                    ri = wload(w3[blk], 4096)
                    rv = ring[ri][:, 0:4096].rearrange("p (k c) -> p k c", c=256)
                    for jj in range(2):
                        j = blk * 2 + jj
                        b = nb()
                        for kc in range(16):
                            S.op("pe", lambda e, kc=kc, jj=jj, b=b: e.matmul(ps[b][:, 0:NC_], lhsT=rv[:, kc, jj * 128:(jj + 1) * 128], rhs=mergedT[:, kc, :],
                                                                        start=(kc == 0), stop=(kc == 15)),
                                 reads=[("ring", ri), "mergedT"], writes=[PS(b)], signal=(kc == 15))
                        i = tmpi()
                        gate_evac(b, tB[i], ("tB", i), 2, j, NC_, has_s)
                        if pend7:
                            pend7.pop(0)()
                        pend7.append(lambda i=i, j=j: resid_add(XR, tB[i], ("tB", i), j, NT))

                pend7 = []
                for blk in range(8):
                    s7_blk(blk)
                while pend7:
                    pend7.pop(0)()
                stage(f"s7a_{pi}", **{f"r1_{pi}": (XR[:, 0, :], [128, 2048], F32, ("XR", 0))})
                load_lnG(0)
                for t in range(NT):
                    ln_stats(XR[:, t, :], ("XR", t), t)
                for t in range(NT):
                    S.op("act", lambda e, t=t: e.activation(out=XR[:, t, :], in_=XR[:, t, :], func=AF.Identity,
                                                            bias=nmr[:, t, 0:1], scale=rstd[:, t, 0:1]),
                         reads=[("nmr", t), ("rstd", t)], writes=[("XR", t)])
                for t in range(NT):
                    S.op("dve", lambda e, t=t: e.tensor_tensor(out=XR[:, t, :], in0=XR[:, t, :], in1=lnG[:, 0, :], op=ALU.mult),
                         reads=["lnG"], writes=[("XR", t)])
                    S.op("dve", lambda e, t=t: e.tensor_tensor(out=XR[:, t, :], in0=XR[:, t, :], in1=lnG[:, 1, :], op=ALU.add),
                         reads=["lnG"], writes=[("XR", t)])
                    ln_stats(XR[:, t, :], ("XR", t), t)
                    ln_z(XR[:, t, :], ("XR", t), t, zts[t][0], zts[t][1])
                    ln_T(zts[t][0], zts[t][1], 128 * (t + 1), 3, 4, has_s and t == NT - 1)

                stage(f"s7_{pi}", **{f"x1_{pi}": (XR[:, 0, :], [128, 2048], F32, ("XR", 0)), f"u2T{pi}": (uT[:].rearrange("p a b -> p (a b)"), [128, 8192], BF16, "uT"), f"lnG{pi}": (lnG[:].rearrange("p a b -> p (a b)"), [128, 4096], F32, "lnG")})
                def s8_unit(j):
                    ri = wload(w4[j], 4096)
                    rv = ring[ri][:, 0:4096].rearrange("p (k c) -> p k c", c=128)
                    bG, bU = nb(), nb()
                    for kc in range(16):
                        S.op("pe", lambda e, kc=kc: e.matmul(ps[bG][:, 0:NC_], lhsT=rv[:, kc, :], rhs=uT[:, kc, 128:512], start=(kc == 0), stop=(kc == 15)),
                             reads=[("ring", ri), "uT"], writes=[PS(bG)], signal=(kc == 15))
                    for kc in range(16):
                        S.op("pe", lambda e, kc=kc: e.matmul(ps[bU][:, 0:NC_], lhsT=rv[:, 16 + kc, :], rhs=uT[:, kc, 128:512], start=(kc == 0), stop=(kc == 15)),
                             reads=[("ring", ri), "uT"], writes=[PS(bU)], signal=(kc == 15))
                    i = tmpi()
                    S.op("act", lambda e: e.activation(out=tA[i][:, 0:NC_], in_=ps[bG][:, 0:NC_], func=AF.Silu), writes=[PS(bG), ("tA", i)])
                    S.op("dve", lambda e: e.tensor_tensor(out=hT[:, j, :], in0=ps[bU][:, 0:NC_], in1=tA[i][:, 0:NC_], op=ALU.mult),
                         reads=[("tA", i)], writes=[PS(bU), "hT"])

                for j in range(44):
                    s8_unit(j)

                stage(f"s8_{pi}", **{f"hT{pi}": (hT[:].rearrange("p a b -> p (a b)"), [128, 44 * 384], BF16, "hT")})
                def s9_unit(j):
                    ria = wload(w5[j][:, 0:2816], 2816)
                    rib = wload(w5[j][:, 2816:5632], 2816)
                    rva = ring[ria][:, 0:2816].rearrange("p (k c) -> p k c", c=128)
                    rvb = ring[rib][:, 0:2816].rearrange("p (k c) -> p k c", c=128)
                    b = nb()
                    for kc in range(44):
                        rvx, rix, kk = (rva, ria, kc) if kc < 22 else (rvb, rib, kc - 22)
                        S.op("pe", lambda e, kc=kc, rvx=rvx, kk=kk: e.matmul(ps[b][:, 0:NC_], lhsT=rvx[:, kk, :], rhs=hT[:, kc, :], start=(kc == 0), stop=(kc == 43)),
                             reads=[("ring", rix), "hT"], writes=[PS(b)], signal=(kc == 43))
                    i = tmpi()
                    gate_evac(b, tB[i], ("tB", i), 5, j, NC_, has_s)
                    if pend9:
                        pend9.pop(0)()
                    pend9.append(lambda i=i, j=j: resid_add(XR, tB[i], ("tB", i), j, NT))

                pend9 = []
                for j in range(16):
                    s9_unit(j)
                    if s2_next and j % 2 == 1:
                        s2_next.pop(0)()
                while pend9:
                    pend9.pop(0)()
                while s2_next:
                    s2_next.pop(0)()
                stage(f"s9_{pi}")
                pre_s10_snap[pi] = S.snapshot()
                load_lnG(2)
                ln_affine_all()
                for t in range(NT):
                    S.dma("sp", "st", y_main[mains[t] - 1], XR[:, t, :], reads=[("XR", t)])

        passes = [(0, [1, 2, 3], False), (3, [4, 5, 6], False), (6, [7, 8, 9], True)]
        for f in s2_tiles(*passes[0], split=True):
            f()
        for pi, (hidx, mains, has_s) in enumerate(passes):
            nxt = s2_tiles(*passes[pi + 1]) if pi + 1 < len(passes) else []
            run_pass(pi, hidx, mains, has_s, nxt)
            drain_ada(48)

        S.finish("sp")
        S.emit()
    return nc


def _tile_w(W, ncols):
    K, N = W.shape
    return np.ascontiguousarray(W.reshape(K // 128, 128, N // ncols, ncols).transpose(2, 1, 0, 3).reshape(N // ncols, 128, -1))


def _tile_units(units):
    parts = []
    for W in units:
        K = W.shape[0]
        parts.append(W.reshape(K // 128, 128, 128).transpose(1, 0, 2).reshape(128, -1))
    return np.concatenate(parts, axis=1)


_CACHE = {}


def kernel(x_prompt, x_sample, c_prompt, c_sample, cache_k_win, cache_v_win, state_conv,
           w_ada, b_ada, w_in, attn_sinks, w_dw, b_dw, cn_gain, cn_bias, w_br_attn, w_br_conv, w_out,
           ln1_gain, ln1_bias, w_ffn_gate, w_ffn_up, w_ffn_down, ln2_gain, ln2_bias):
    f = lambda a: np.asarray(a, dtype=np.float32)
    x_prompt, x_sample, c_prompt, c_sample = f(x_prompt), f(x_sample), f(c_prompt), f(c_sample)
    cache_k_win, cache_v_win, state_conv = f(cache_k_win), f(cache_v_win), f(state_conv)
    w_in0 = f(w_in)[0]
    partner = np.concatenate([np.arange(32, 64), np.arange(0, 32)])
    qcols = np.arange(1024).reshape(16, 64)
    qrh = qcols[:, partner].reshape(-1)
    kcols = 1024 + np.arange(256).reshape(4, 64)
    krh = kcols[:, partner]
    blocks1 = [_tile_w(w_in0[:, 1024:1280], 256)[0], _tile_w(w_in0[:, 1280:1536], 256)[0]]
    for i in range(4):
        blocks1.append(_tile_w(w_in0[:, i * 256:(i + 1) * 256], 256)[0])
    for i in range(8):
        cols = np.concatenate([1536 + np.arange(i * 128, (i + 1) * 128), 2560 + np.arange(i * 128, (i + 1) * 128)])
        blocks1.append(_tile_w(w_in0[:, cols], 256)[0])
    w1 = np.stack(blocks1)
    wba, wbc = f(w_br_attn)[0], f(w_br_conv)[0]
    w2 = np.stack([_tile_units([w_in0[:, 3584 + j * 128:3584 + (j + 1) * 128], w_in0[:, 5632 + j * 128:5632 + (j + 1) * 128],
                                wba[:, j * 128:(j + 1) * 128], wbc[:, j * 128:(j + 1) * 128]]) for j in range(16)])
    w3 = _tile_w(f(w_out)[0], 256)
    wg, wu = f(w_ffn_gate)[0], f(w_ffn_up)[0]
    w4 = np.stack([_tile_units([wg[:, j * 128:(j + 1) * 128], wu[:, j * 128:(j + 1) * 128]]) for j in range(44)])
    w5 = _tile_w(f(w_ffn_down)[0], 128)
    wa = _tile_w(f(w_ada)[0], 256)
    bada = np.ascontiguousarray(f(b_ada)[0].reshape(96, 128))
    sk = f(attn_sinks)[0]
    sinks8 = np.ascontiguousarray(np.repeat(sk.reshape(8, 2), 64, axis=1))
    wdw = f(w_dw)[0]
    cvs = np.stack([f(b_dw)[0], f(cn_gain)[0], f(cn_bias)[0]])
    lngb = np.stack([f(ln1_gain)[0], f(ln1_bias)[0], f(ln2_gain)[0], f(ln2_bias)[0]])
    identd = np.eye(128, dtype=np.float32)
    dst = np.arange(128)
    permd = np.zeros((5, 128, 128), np.float32)
    permd[0, (dst // 64) * 64 + partner[dst % 64], dst] = 1.0
    for h in range(2):
        permd[1 + h, h * 64 + dst % 64, dst] = 1.0
        permd[3 + h, h * 64 + partner[dst % 64], dst] = 1.0
    kk = np.arange(128)[:, None]
    qq = np.arange(128)[None, :]
    m_prev = np.where(kk > qq, 0.0, NEG).astype(np.float32)
    m_own = np.where(kk <= qq, 0.0, NEG).astype(np.float32)
    qs, qt = np.arange(128)[None, :] // 8, np.arange(128)[None, :] % 8
    msc = np.stack([np.where((qs == s) & (kk > qt), 0.0, NEG) for s in range(16)]).astype(np.float32)
    ks_, kt_ = np.arange(128)[:, None] // 8, np.arange(128)[:, None] % 8
    msn = np.where((ks_ == qs) & (kt_ <= qt), 0.0, NEG).astype(np.float32)
    inv = (10000.0 ** (-np.arange(32, dtype=np.float64) / 32.0)).astype(np.float32)
    prow = np.arange(128)
    sgn = np.where((prow % 64) < 32, -1.0, 1.0).astype(np.float32)[:, None]

    def rope_tabs(pos):
        ang = (pos.astype(np.float32)[None, :] * inv[prow % 32][:, None]).astype(np.float32).astype(np.float64)
        return np.cos(ang).astype(np.float32), (np.sin(ang) * sgn).astype(np.float32)

    in_maps = []
    for c in range(NCORE):
        halo = np.zeros((128, D), np.float32) if c == 0 else x_prompt[0, c * 1024 - 128:c * 1024]
        xin = np.concatenate([halo[None], x_prompt[0, c * 1024:(c + 1) * 1024].reshape(8, 128, D),
                              x_sample[c * 16:(c + 1) * 16].reshape(1, 128, D)], axis=0)
        cvec = np.concatenate([c_prompt, c_sample[c * 16:(c + 1) * 16]], axis=0)
        rc_, rs_ = [], []
        for p in range(3):
            base = c * 1024 - 128 + p * 384
            pos = base + np.arange(512)
            if p == 2:
                pos = pos.copy()
                pos[384:] = 8192 + (np.arange(128) % 8)
            a, b = rope_tabs(pos)
            rc_.append(a)
            rs_.append(b)
        flags = np.ones((128, 3), np.float32)
        if c == 0:
            flags[:, 0] = 0.0
        m0 = np.full((128, 128), NEG, np.float32) if c == 0 else m_prev
        in_maps.append(dict(
            xin=np.ascontiguousarray(xin), cvec=np.ascontiguousarray(cvec),
            cachek=np.ascontiguousarray(cache_k_win[0, c * 16:(c + 1) * 16].reshape(16, 128, 256)),
            cachev=np.ascontiguousarray(cache_v_win[0, c * 16:(c + 1) * 16].reshape(16, 128, 256)),
            stconv=np.ascontiguousarray(state_conv[0, c * 16:(c + 1) * 16]),
            w1=w1, w2=w2, w3=w3, w4=w4, w5=w5, wa=wa, bada=bada, sinks8=sinks8, wdw=wdw, cvs=cvs, lngb=lngb,
            ropec=np.stack(rc_), ropes=np.stack(rs_), flags=flags, identd=identd,
            masks=np.stack([m_prev, m_own, m0]), msc=msc, msn=msn, permd=permd))
    if "nc" not in _CACHE:
        _CACHE["nc"] = build_nc()
    res = run_bass_kernel_spmd(_CACHE["nc"], in_maps, core_ids=list(range(NCORE)))
    R = res.results
    y_prompt = np.concatenate([R[c]["y_main"][0:8].reshape(1024, D) for c in range(NCORE)], axis=0)[None]
    y_sample = np.concatenate([R[c]["y_main"][8].reshape(16, 8, D) for c in range(NCORE)], axis=0)
    k_win_prompt = R[7]["kwin_p"].reshape(1, 1, 128, 4, 64)
    v_win_prompt = R[7]["vwin_p"].reshape(1, 1, 128, 4, 64)
    conv_prompt = R[7]["conv_p"].reshape(1, 1, 30, 1024)
    k_win_sample = np.concatenate([R[c]["kwin_s"] for c in range(NCORE)], axis=0).reshape(1, 128, 128, 4, 64)
    v_win_sample = np.concatenate([R[c]["vwin_s"] for c in range(NCORE)], axis=0).reshape(1, 128, 128, 4, 64)
    conv_sample = np.concatenate([R[c]["conv_s"] for c in range(NCORE)], axis=0).reshape(1, 128, 30, 1024)
    return (y_prompt.astype(np.float32), y_sample.astype(np.float32), k_win_prompt.astype(np.float32),
            v_win_prompt.astype(np.float32), conv_prompt.astype(np.float32), k_win_sample.astype(np.float32),
            v_win_sample.astype(np.float32), conv_sample.astype(np.float32))
```
